# Optimizing a Trainium2 kernel written in Bass

```python
import jax, jax.numpy as jnp
from jax import lax
import numpy as np

D_MODEL = 2048
BATCH = 16
SEQ = 256
DEPTH = 2
DEC_BATCH = 2
DEC_SEQ = 2048
PAST_LEN = 256

GRID_W = 64
ROPE_BASE = 10000.0
NORM_EPS = 1e-6
Q_BLOCK = 128
MLA_HEADS = 8
MLA_Q_LORA = 512
MLA_KV_LORA = 512
MLA_NOPE = 128
MLA_ROPE = 64
MLA_V = 128
MLA_QK = MLA_NOPE + MLA_ROPE
MLA_SCALE = MLA_QK ** -0.5
MLA_WIDTH = MLA_HEADS * MLA_V
GLA_HEADS = 4
GLA_DK = 128
GLA_DV = 256
GLA_GATE_RANK = 16
GLA_GATE_NORM = 16.0
GLA_CHUNK = 16
GLA_WIDTH = GLA_HEADS * GLA_DV
SWA_HEADS = 16
SWA_KV_HEADS = 4
SWA_GROUPS = SWA_HEADS // SWA_KV_HEADS
SWA_HEAD_DIM = 64
SWA_WINDOW = 128
SWA_SCALE = SWA_HEAD_DIM ** -0.5
SWA_WIDTH = SWA_HEADS * SWA_HEAD_DIM
D_FF = 5632
N_MOD = 9
IN_SPLITS = (MLA_Q_LORA, MLA_KV_LORA, MLA_ROPE,
             GLA_HEADS * GLA_DK, GLA_HEADS * GLA_DK, GLA_WIDTH, GLA_WIDTH, GLA_GATE_RANK, GLA_GATE_RANK,
             SWA_WIDTH, SWA_KV_HEADS * SWA_HEAD_DIM, SWA_KV_HEADS * SWA_HEAD_DIM,
             D_MODEL, D_MODEL, D_MODEL)
IN_COLS = sum(IN_SPLITS)

kernel_name = 'hybrid_mla_gla_swa_prefix_dit_step'


def rms_norm(x, g):
    xf = x.astype(jnp.float32)
    y = xf * lax.rsqrt(jnp.mean(xf * xf, axis=-1, keepdims=True) + NORM_EPS)
    return (y * g.astype(jnp.float32)).astype(x.dtype)


def adaln(cond, P):
    return jnp.split(jax.nn.silu(cond) @ P['w_ada'] + P['b_ada'], N_MOD, axis=-1)


def modulate(x, g, shift, scale):
    return rms_norm(x, g) * (1 + scale[..., None, :]) + shift[..., None, :]


def swiglu(h, w_gu, w_down):
    a, u = jnp.split(h @ w_gu, 2, axis=-1)
    return (jax.nn.silu(a) * u) @ w_down


def rope_rotate(x, pos):
    half = x.shape[-1] // 2
    inv_freq = ROPE_BASE ** (-jnp.arange(half, dtype=jnp.float32) / half)
    ang = pos.astype(jnp.float32)[:, None] * inv_freq[None, :]
    shape = (1, x.shape[1]) + (1,) * (x.ndim - 3) + (half,)
    cos = jnp.cos(ang).reshape(shape).astype(x.dtype)
    sin = jnp.sin(ang).reshape(shape).astype(x.dtype)
    x1, x2 = x[..., :half], x[..., half:]
    return jnp.concatenate([x1 * cos - x2 * sin, x1 * sin + x2 * cos], axis=-1)


def axial_rope(x):
    rows = x.shape[1] // GRID_W
    pos = jnp.arange(rows * GRID_W)
    r2 = x.shape[-1] // 2
    return jnp.concatenate([rope_rotate(x[..., :r2], pos // GRID_W),
                            rope_rotate(x[..., r2:], pos % GRID_W)], axis=-1)


def mla_rope(x):
    return jnp.concatenate([x[..., :MLA_NOPE], axial_rope(x[..., MLA_NOPE:])], axis=-1)


def block_softmax_attention(q, k, v, sink=None):
    B, Tq, Hk, G, d = q.shape
    nb = Tq // Q_BLOCK
    qb = jnp.moveaxis(q.reshape(B, nb, Q_BLOCK, Hk, G, d), 1, 0)

    def one_block(qblk):
        s = jnp.einsum('bqhgd,bkhd->bhgqk', qblk, k).astype(jnp.float32)
        if sink is not None:
            s_sink = jnp.broadcast_to(sink.astype(jnp.float32)[None, :, :, None, None], s.shape[:-1] + (1,))
            p = jax.nn.softmax(jnp.concatenate([s, s_sink], axis=-1), axis=-1)[..., :-1]
        else:
            p = jax.nn.softmax(s, axis=-1)
        return jnp.einsum('bhgqk,bkhv->bqhgv', p.astype(v.dtype), v)

    o = lax.map(one_block, qb)
    return jnp.moveaxis(o, 0, 1).reshape(B, Tq, Hk, G, v.shape[-1])


def banded_window_attention(q, k, v, k_ctx, v_ctx, sink):
    B, T, Hk, G, d = q.shape
    W = SWA_WINDOW
    nb = T // W
    L = k_ctx.shape[1]
    qb = q.reshape(B, nb, W, Hk, G, d)
    pad = ((0, 0), (W, W), (0, 0), (0, 0))
    kp = jnp.pad(k, pad).reshape(B, nb + 2, W, Hk, d)
    vp = jnp.pad(v, pad).reshape(B, nb + 2, W, Hk, d)
    kb = jnp.concatenate([kp[:, :-2], kp[:, 1:-1], kp[:, 2:]], axis=2)
    vb = jnp.concatenate([vp[:, :-2], vp[:, 1:-1], vp[:, 2:]], axis=2)
    qpos = jnp.arange(nb)[:, None] * W + jnp.arange(W)[None, :]
    kpos = jnp.arange(nb)[:, None] * W - W + jnp.arange(3 * W)[None, :]
    valid = ((kpos[:, None, :] >= 0) & (kpos[:, None, :] < T)
             & (jnp.abs(qpos[:, :, None] - kpos[:, None, :]) <= W))
    s_loc = jnp.einsum('bnqhgd,bnkhd->bnhgqk', qb, kb).astype(jnp.float32)
    s_loc = jnp.where(valid[None, :, None, None], s_loc, -jnp.inf)
    s_ctx = jnp.einsum('bnqhgd,bkhd->bnhgqk', qb, k_ctx).astype(jnp.float32)
    s_sink = jnp.broadcast_to(sink.astype(jnp.float32)[None, None, :, :, None, None], s_loc.shape[:-1] + (1,))
    p = jax.nn.softmax(jnp.concatenate([s_loc, s_ctx, s_sink], axis=-1), axis=-1)
    p_loc = p[..., :3 * W].astype(v.dtype)
    p_ctx = p[..., 3 * W:3 * W + L].astype(v.dtype)
    o = (jnp.einsum('bnhgqk,bnkhd->bnqhgd', p_loc, vb)
         + jnp.einsum('bnhgqk,bkhd->bnqhgd', p_ctx, v_ctx))
    return o.reshape(B, T, Hk, G, d)


def gla_chunked(q, k, v, log_a, s0):
    B, T, H, dk = q.shape
    C = GLA_CHUNK
    n = T // C

    def rs(a):
        return a.reshape(B, n, C, H, a.shape[-1])

    q, k, v, log_a = rs(q), rs(k), rs(v), rs(log_a)
    b = jnp.cumsum(log_a, axis=2)
    b_last = b[:, :, -1]
    causal = jnp.tril(jnp.ones((C, C), dtype=bool))
    diff = b[:, :, :, None] - b[:, :, None, :]
    decay = jnp.exp(jnp.where(causal[None, None, :, :, None, None], diff, -jnp.inf))
    A = jnp.einsum('bnthk,bnshk,bntshk->bnhts', q, k, decay)
    o_intra = jnp.einsum('bnhts,bnshv->bnthv', A, v)
    q_dec = q * jnp.exp(b)
    k_dec = k * jnp.exp(b_last[:, :, None] - b)
    a_chunk = jnp.exp(b_last)

    def step(S, xs):
        qd, kd, vv, ac = xs
        o = jnp.einsum('bthk,bhkv->bthv', qd, S)
        S = ac[..., None] * S + jnp.einsum('bthk,bthv->bhkv', kd, vv)
        return S, o

    xs = (jnp.moveaxis(q_dec, 1, 0), jnp.moveaxis(k_dec, 1, 0), jnp.moveaxis(v, 1, 0), jnp.moveaxis(a_chunk, 1, 0))
    s_fin, o_inter = lax.scan(step, s0, xs)
    o = o_intra + jnp.moveaxis(o_inter, 0, 1)
    return o.reshape(B, T, H, v.shape[-1]), s_fin


def gla_bidirectional(q, k, v, la_f, la_b, s0_f, s0_b):
    o_f, s_f = gla_chunked(q, k, v, la_f, s0_f)

    def fl(a):
        return jnp.flip(a, axis=1)

    o_b, s_b = gla_chunked(fl(q), fl(k), fl(v), fl(la_b), s0_b)
    return o_f + fl(o_b), s_f, s_b


def mla_query(cq, P):
    B, T, _ = cq.shape
    q = (rms_norm(cq, P['g_mla_q']) @ P['w_mla_uq']).reshape(B, T, MLA_HEADS, MLA_QK)
    return rms_norm(q, P['g_mla_qn'])


def mla_keys_values(c_kv, k_pe, P):
    B, L, _ = c_kv.shape
    kv = (c_kv @ P['w_mla_ukv']).reshape(B, L, MLA_HEADS, MLA_NOPE + MLA_V)
    k_nope, v = kv[..., :MLA_NOPE], kv[..., MLA_NOPE:]
    k = jnp.concatenate([k_nope, jnp.broadcast_to(k_pe[:, :, None, :], (B, L, MLA_HEADS, MLA_ROPE))], axis=-1)
    return rms_norm(k, P['g_mla_kn']), v


def swa_qkv(sq, sk, sv, P):
    B, T, _ = sq.shape
    q = rms_norm(sq.reshape(B, T, SWA_KV_HEADS, SWA_GROUPS, SWA_HEAD_DIM), P['g_swa_qn'])
    k = rms_norm(sk.reshape(B, T, SWA_KV_HEADS, SWA_HEAD_DIM), P['g_swa_kn'])
    v = sv.reshape(B, T, SWA_KV_HEADS, SWA_HEAD_DIM)
    return q, k, v


def token_mixer(h, P, ctx=None):
    B, T, _ = h.shape
    points = np.cumsum(IN_SPLITS)[:-1].tolist()
    (cq, ckv, kpe, gq, gk, gv, gout, ggf, ggb, sq, sk, sv,
     gate_mla, gate_gla, gate_swa) = jnp.split(h @ P['w_in'], points, axis=-1)
    sink = P['swa_sink'].reshape(SWA_KV_HEADS, SWA_GROUPS)
    q_m = mla_query(cq, P)
    c_kv = rms_norm(ckv, P['g_mla_kv'])
    k_m, v_m = mla_keys_values(c_kv, kpe, P)
    q_s, k_s, v_s = swa_qkv(sq, sk, sv, P)
    q_g = gq.reshape(B, T, GLA_HEADS, GLA_DK).astype(jnp.float32) * GLA_DK ** -0.5
    k_g = gk.reshape(B, T, GLA_HEADS, GLA_DK).astype(jnp.float32)
    v_g = gv.reshape(B, T, GLA_HEADS, GLA_DV).astype(jnp.float32)
    la_f = (jax.nn.log_sigmoid((ggf @ P['w_gla_gf'] + P['b_gla_gf']).astype(jnp.float32))
            / GLA_GATE_NORM).reshape(B, T, GLA_HEADS, GLA_DK)
    la_b = (jax.nn.log_sigmoid((ggb @ P['w_gla_gb'] + P['b_gla_gb']).astype(jnp.float32))
            / GLA_GATE_NORM).reshape(B, T, GLA_HEADS, GLA_DK)
    if ctx is None:
        o_mla = block_softmax_attention(q_m[:, :, :, None] * MLA_SCALE, k_m, v_m)
        o_swa = block_softmax_attention(q_s * SWA_SCALE, k_s, v_s, sink)
        s0 = jnp.zeros((B, 2, GLA_HEADS, GLA_DK, GLA_DV), jnp.float32)
    else:
        ctx_ckv, ctx_kpe, ctx_k, ctx_v, s0 = ctx
        k_mc, v_mc = mla_keys_values(ctx_ckv, ctx_kpe, P)
        o_mla = block_softmax_attention(mla_rope(q_m)[:, :, :, None] * MLA_SCALE,
                                        jnp.concatenate([mla_rope(k_m), k_mc], axis=1),
                                        jnp.concatenate([v_m, v_mc], axis=1))
        o_swa = banded_window_attention(axial_rope(q_s) * SWA_SCALE, axial_rope(k_s), v_s, ctx_k, ctx_v, sink)
    s0 = s0.astype(jnp.float32)
    o_g, s_f, s_b = gla_bidirectional(q_g, k_g, v_g, la_f, la_b, s0[:, 0], s0[:, 1])
    o_g = rms_norm(o_g.astype(h.dtype), P['g_gla_out']).reshape(B, T, GLA_WIDTH) * jax.nn.silu(gout)
    merged = (jax.nn.sigmoid(gate_mla) * (o_mla.reshape(B, T, MLA_WIDTH) @ P['w_br_mla'])
              + jax.nn.sigmoid(gate_gla) * (o_g @ P['w_br_gla'])
              + jax.nn.sigmoid(gate_swa) * (o_swa.reshape(B, T, SWA_WIDTH) @ P['w_br_swa']))
    out = merged @ P['w_out']
    if ctx is None:
        return out, (c_kv, kpe, k_s, v_s, jnp.stack([s_f, s_b], axis=1))
    return out, None


def trunk_layer(x, mod, P, ctx=None):
    sh1, sc1, gt1, sh2, sc2, gt2, sh3, sc3, gt3 = mod
    x = x + 0.5 * gt1[..., None, :] * swiglu(modulate(x, P['g_norm1'], sh1, sc1), P['w_ff1_gu'], P['w_ff1_down'])
    m, new_ctx = token_mixer(modulate(x, P['g_norm2'], sh2, sc2), P, ctx)
    x = x + gt2[..., None, :] * m
    x = x + 0.5 * gt3[..., None, :] * swiglu(modulate(x, P['g_norm3'], sh3, sc3), P['w_ff2_gu'], P['w_ff2_down'])
    return x, new_ctx


def setup_inputs(seed: int = 0) -> dict:
    key = jax.random.key(seed)
    ks = jax.random.split(key, 64)
    counter = iter(range(64))

    def nrm(shape, scale=1.0):
        return jax.random.normal(ks[next(counter)], shape, jnp.float32) * scale

    def gain(shape):
        return 1.0 + 0.02 * nrm(shape)

    return {
        'x_prompt': nrm((BATCH, SEQ, D_MODEL)),
        'x_sample': nrm((DEC_BATCH, DEC_SEQ, D_MODEL)),
        'cache_mla_ckv': nrm((DEC_BATCH, DEPTH, PAST_LEN, MLA_KV_LORA)),
        'cache_mla_kpe': nrm((DEC_BATCH, DEPTH, PAST_LEN, MLA_ROPE)),
        'cache_swa_k': nrm((DEC_BATCH, DEPTH, PAST_LEN, SWA_KV_HEADS, SWA_HEAD_DIM)),
        'cache_swa_v': nrm((DEC_BATCH, DEPTH, PAST_LEN, SWA_KV_HEADS, SWA_HEAD_DIM)),
        'state_gla': nrm((DEC_BATCH, DEPTH, 2, GLA_HEADS, GLA_DK, GLA_DV)),
        'c': nrm((DEC_BATCH, D_MODEL)),
        'c_ctx': nrm((D_MODEL,)),
        'w_ada': nrm((DEPTH, D_MODEL, N_MOD * D_MODEL), 0.5 * D_MODEL ** -0.5),
        'b_ada': nrm((DEPTH, N_MOD * D_MODEL), 0.02),
        'g_norm1': gain((DEPTH, D_MODEL)),
        'g_norm2': gain((DEPTH, D_MODEL)),
        'g_norm3': gain((DEPTH, D_MODEL)),
        'w_ff1_gu': nrm((DEPTH, D_MODEL, 2 * D_FF), D_MODEL ** -0.5),
        'w_ff1_down': nrm((DEPTH, D_FF, D_MODEL), D_FF ** -0.5),
        'w_ff2_gu': nrm((DEPTH, D_MODEL, 2 * D_FF), D_MODEL ** -0.5),
        'w_ff2_down': nrm((DEPTH, D_FF, D_MODEL), D_FF ** -0.5),
        'w_in': nrm((DEPTH, D_MODEL, IN_COLS), D_MODEL ** -0.5),
        'g_mla_q': gain((DEPTH, MLA_Q_LORA)),
        'w_mla_uq': nrm((DEPTH, MLA_Q_LORA, MLA_HEADS * MLA_QK), MLA_Q_LORA ** -0.5),
        'g_mla_kv': gain((DEPTH, MLA_KV_LORA)),
        'w_mla_ukv': nrm((DEPTH, MLA_KV_LORA, MLA_HEADS * (MLA_NOPE + MLA_V)), MLA_KV_LORA ** -0.5),
        'g_mla_qn': gain((DEPTH, MLA_QK)),
        'g_mla_kn': gain((DEPTH, MLA_QK)),
        'w_gla_gf': nrm((DEPTH, GLA_GATE_RANK, GLA_HEADS * GLA_DK), GLA_GATE_RANK ** -0.5),
        'b_gla_gf': nrm((DEPTH, GLA_HEADS * GLA_DK), 0.1),
        'w_gla_gb': nrm((DEPTH, GLA_GATE_RANK, GLA_HEADS * GLA_DK), GLA_GATE_RANK ** -0.5),
        'b_gla_gb': nrm((DEPTH, GLA_HEADS * GLA_DK), 0.1),
        'g_gla_out': gain((DEPTH, GLA_DV)),
        'g_swa_qn': gain((DEPTH, SWA_HEAD_DIM)),
        'g_swa_kn': gain((DEPTH, SWA_HEAD_DIM)),
        'swa_sink': nrm((DEPTH, SWA_HEADS)),
        'w_br_mla': nrm((DEPTH, MLA_WIDTH, D_MODEL), MLA_WIDTH ** -0.5),
        'w_br_gla': nrm((DEPTH, GLA_WIDTH, D_MODEL), GLA_WIDTH ** -0.5),
        'w_br_swa': nrm((DEPTH, SWA_WIDTH, D_MODEL), SWA_WIDTH ** -0.5),
        'w_out': nrm((DEPTH, D_MODEL, D_MODEL), D_MODEL ** -0.5),
    }


def reference(x_prompt, x_sample, cache_mla_ckv, cache_mla_kpe, cache_swa_k, cache_swa_v, state_gla,
              c, c_ctx, w_ada, b_ada, g_norm1, g_norm2, g_norm3,
              w_ff1_gu, w_ff1_down, w_ff2_gu, w_ff2_down, w_in,
              g_mla_q, w_mla_uq, g_mla_kv, w_mla_ukv, g_mla_qn, g_mla_kn,
              w_gla_gf, b_gla_gf, w_gla_gb, b_gla_gb, g_gla_out,
              g_swa_qn, g_swa_kn, swa_sink, w_br_mla, w_br_gla, w_br_swa, w_out):
    stacked = {
        'w_ada': w_ada, 'b_ada': b_ada, 'g_norm1': g_norm1, 'g_norm2': g_norm2, 'g_norm3': g_norm3,
        'w_ff1_gu': w_ff1_gu, 'w_ff1_down': w_ff1_down, 'w_ff2_gu': w_ff2_gu, 'w_ff2_down': w_ff2_down,
        'w_in': w_in, 'g_mla_q': g_mla_q, 'w_mla_uq': w_mla_uq, 'g_mla_kv': g_mla_kv,
        'w_mla_ukv': w_mla_ukv, 'g_mla_qn': g_mla_qn, 'g_mla_kn': g_mla_kn,
        'w_gla_gf': w_gla_gf, 'b_gla_gf': b_gla_gf, 'w_gla_gb': w_gla_gb, 'b_gla_gb': b_gla_gb,
        'g_gla_out': g_gla_out, 'g_swa_qn': g_swa_qn, 'g_swa_kn': g_swa_kn, 'swa_sink': swa_sink,
        'w_br_mla': w_br_mla, 'w_br_gla': w_br_gla, 'w_br_swa': w_br_swa, 'w_out': w_out,
    }
    new_ckv, new_kpe, new_sk, new_sv, new_sg = [], [], [], [], []
    y_p, y_s = x_prompt, x_sample
    for l in range(DEPTH):
        P = {name: arr[l] for name, arr in stacked.items()}
        y_p, (ckv, kpe, sk, sv, sg) = trunk_layer(y_p, adaln(c_ctx, P), P)
        new_ckv.append(ckv)
        new_kpe.append(kpe)
        new_sk.append(sk)
        new_sv.append(sv)
        new_sg.append(sg)
        ctx = (cache_mla_ckv[:, l], cache_mla_kpe[:, l], cache_swa_k[:, l], cache_swa_v[:, l], state_gla[:, l])
        y_s, _ = trunk_layer(y_s, adaln(c, P), P, ctx)
    return (y_p, y_s, jnp.stack(new_ckv, axis=1), jnp.stack(new_kpe, axis=1),
            jnp.stack(new_sk, axis=1), jnp.stack(new_sv, axis=1), jnp.stack(new_sg, axis=1))
```

```python
import contextlib
import numpy as np
import concourse.bass as bass
import concourse.mybir as mybir
from concourse.bass_utils import run_bass_kernel_spmd

F32 = mybir.dt.float32
BF16 = mybir.dt.bfloat16
AF = mybir.ActivationFunctionType
ALU = mybir.AluOpType

D = 2048
NCH = 16
T = 512
DEPTH = 2
DFF = 5632
NFC = 44
EPS = 1e-6
IN_COLS = 11872
O_CQ, O_CKV, O_KPE, O_GQ, O_GK, O_GV, O_GOUT, O_GGF, O_GGB = 0, 512, 1024, 1088, 1600, 2112, 3136, 4160, 4176
O_SQ, O_SK, O_SV, O_GM, O_GG, O_GS = 4192, 5216, 5472, 5728, 7776, 9824
MLA_SCALE = 192 ** -0.5
SWA_SCALE = 64 ** -0.5
GLA_QS = 128 ** -0.5


class Buf:
    __slots__ = ("name", "w", "r")

    def __init__(self, name="", w=None):
        self.name = name
        self.w = w
        self.r = {}


class Chan:
    __slots__ = ("sem", "count", "name")

    def __init__(self, name):
        self.sem = None
        self.count = 0
        self.name = name


class Op:
    __slots__ = ("eng", "fn", "deps", "idx", "signal", "waits", "chan")

    def __init__(self, eng, fn, deps, idx, chan=None):
        self.eng = eng
        self.fn = fn
        self.deps = deps
        self.idx = idx
        self.signal = False
        self.waits = None
        self.chan = chan


ENGS = ("pe", "act", "dve", "pool", "sp")


class Prog:
    def __init__(self, nc):
        self.nc = nc
        self.ops = {e: [] for e in ENGS}
        self.chans = []

    def _collect(self, reads, writes):
        deps = {}
        for b in reads:
            if b.w is not None:
                k, v = b.w
                if deps.get(k, -1) < v:
                    deps[k] = v
        for b in writes:
            if b.w is not None:
                k, v = b.w
                if deps.get(k, -1) < v:
                    deps[k] = v
            for k, v in b.r.items():
                if deps.get(k, -1) < v:
                    deps[k] = v
        return deps

    def _commit(self, tok, reads, writes):
        k, v = tok
        for b in reads:
            if b.r.get(k, -1) < v:
                b.r[k] = v
        for b in writes:
            b.w = tok
            b.r = {}

    def op(self, eng, fn, reads=(), writes=()):
        deps = self._collect(reads, writes)
        idx = len(self.ops[eng])
        if eng == "pe":
            deps.pop(("e", "pe"), None)
        o = Op(eng, fn, deps, idx)
        self.ops[eng].append(o)
        self._commit((("e", eng), idx), reads, writes)
        return (("e", eng), idx)

    def new_chan(self, name):
        c = Chan(name)
        self.chans.append(c)
        return c

    def dma(self, q, chan, fn, reads=(), writes=()):
        deps = self._collect(reads, writes)
        if chan.count > 0:
            k = ("d", chan)
            if deps.get(k, -1) < chan.count:
                deps[k] = chan.count
        o = Op(q, fn, deps, len(self.ops[q]), chan=chan)
        self.ops[q].append(o)
        chan.count += 16
        self._commit((("d", chan), chan.count), reads, writes)

    def fence(self, fn, engs=("pe", "act", "dve")):
        deps = {("e", f): len(self.ops[f]) - 1 for f in engs if len(self.ops[f]) > 0}
        for c in self.chans:
            if c.count > 0:
                deps[("d", c)] = c.count
        idx = len(self.ops["dve"])
        o = Op("dve", fn, deps, idx)
        self.ops["dve"].append(o)
        return (("e", "dve"), idx)

    def wait_all_dma(self, eng="sp"):
        deps = {("d", c): c.count for c in self.chans if c.count > 0}
        self.ops[eng].append(Op(eng, None, deps, len(self.ops[eng])))

    def emit(self):
        nc = self.nc
        for e in ENGS:
            waited = {}
            for o in self.ops[e]:
                w = []
                for k, v in o.deps.items():
                    if waited.get(k, -1) >= v:
                        continue
                    waited[k] = v
                    w.append((k, v))
                    if k[0] == "e":
                        self.ops[k[1]][v].signal = True
                o.waits = w
        cnt = {}
        for e in ENGS:
            c = 0
            arr = []
            for o in self.ops[e]:
                if o.signal:
                    c += 1
                arr.append(c)
            cnt[e] = arr
        with contextlib.ExitStack() as st:
            esem = {e: st.enter_context(nc.semaphore("s_" + e)) for e in ENGS}
            for i, c in enumerate(self.chans):
                if c.count > 0:
                    c.sem = st.enter_context(nc.semaphore("d%d_%s" % (i, c.name)))
            block = st.enter_context(nc.Block())
            hw = {"pe": block.tensor, "act": block.scalar, "dve": block.vector,
                  "pool": block.gpsimd, "sp": block.sync}

            def run(e):
                def body(eng):
                    for o in self.ops[e]:
                        for k, v in o.waits:
                            if k[0] == "e":
                                eng.wait_ge(esem[k[1]], cnt[k[1]][v])
                            else:
                                eng.wait_ge(k[1].sem, v)
                        if o.fn is None:
                            continue
                        ins = o.fn(eng)
                        if o.chan is not None:
                            ins.then_inc(o.chan.sem, 16)
                        elif o.signal:
                            ins.then_inc(esem[e], 1)
                return body
            for e in ENGS:
                hw[e](run(e))
        return {e: len(self.ops[e]) for e in ENGS}


class Tl:
    __slots__ = ("ap", "b", "bs")

    def __init__(self, ap, b, bs=None):
        self.ap = ap
        self.b = b
        self.bs = bs


def sp_layout():
    off = {}
    c = [0]

    def add(name, w):
        off[name] = (c[0], w)
        c[0] += w
    add("ident", 128)
    add("permT", 64)
    add("maskU", 128)
    add("maskL", 128)
    add("cond", 32)
    add("gsel", 32)
    add("gsel_own", 8)
    add("osel", 4)
    for l in range(DEPTH):
        add("bada%d" % l, 144)
        add("gn%d" % l, 48)
        add("gq%d" % l, 4)
        add("gkv%d" % l, 4)
        add("gqn_n%d" % l, 1)
        add("gqn_r%d" % l, 1)
        add("gkn_n%d" % l, 1)
        add("gkn_r%d" % l, 1)
        add("ggo%d" % l, 2)
        add("gsq%d" % l, 1)
        add("gsk%d" % l, 1)
        add("sink%d" % l, 16)
    return off, c[0]


SP_OFF, SP_N = sp_layout()
ARENA_ELEMS = 56 * 1024
XROWS, XCOLS = 128, 4608
X_CKV, X_SWV = 0, 3584
X_KR = (0, 2048)
X_SWK = [(64, 2048), (0, 2560), (64, 2560), (0, 3072)]
X_SS = (64, 3072)
YCOLS = 8 + 2048


class Builder:
    def __init__(self, do_s=True, n_layers=DEPTH, dbg=None, n_cores=8, do_p=True):
        self.do_p = do_p
        self.rgroups = [[0, 1, 2, 3], [4, 5, 6, 7]] if n_cores == 8 else [[0, 1, 2, 3]]
        self.do_s = do_s
        self.n_layers = n_layers
        self.dbg = dbg
        nc = self.nc = bass.Bass("TRN2", target_bir_lowering=False)
        self.P = Prog(nc)
        P = self.P
        din = lambda name, shape: nc.dram_tensor(name, list(shape), F32, kind="ExternalInput").ap()
        dout = lambda name, shape: nc.dram_tensor(name, list(shape), F32, kind="ExternalOutput").ap()
        self.d = d = {}
        d["xpT"] = din("xpT", [D, T])
        d["xsT"] = din("xsT", [D, 4 * T])
        d["smallp"] = din("smallp", [128, SP_N])
        d["rope"] = din("rope", [5, 64, 2, T])
        d["wg"] = din("wg", [16, DEPTH, 2, 512])
        d["bg"] = din("bg", [1, DEPTH, 2, 512])
        d["swamask"] = din("swamask", [128, 16 * 4 * 128])
        d["w_ada"] = [din("w_ada%d" % l, [D, 9 * D]) for l in range(DEPTH)]
        for nm in ("w_ff1_gu", "w_ff2_gu"):
            d[nm] = din(nm, [DEPTH, D, 2 * DFF])
        for nm in ("w_ff1_down", "w_ff2_down"):
            d[nm] = din(nm, [DEPTH, DFF, D])
        d["w_in"] = din("w_in", [DEPTH, D, IN_COLS])
        d["w_mla_uq"] = din("w_mla_uq", [DEPTH, 512, 1536])
        d["w_mla_ukv"] = din("w_mla_ukv", [DEPTH, 512, 2048])
        for nm in ("w_br_mla", "w_br_gla", "w_br_swa"):
            d[nm] = din(nm, [DEPTH, 1024, D])
        d["w_out"] = din("w_out", [DEPTH, D, D])
        d["ckv_cT"] = din("ckv_cT", [DEPTH, 512, 256])
        d["kpe_cT"] = din("kpe_cT", [DEPTH, 64, 256])
        d["swk_cT"] = din("swk_cT", [DEPTH, 64, 4, 256])
        d["swv_c"] = din("swv_c", [DEPTH, 256, 256])
        d["state"] = din("state", [DEPTH, 2, 4, 128, 256])
        d["ypT"] = dout("ypT", [D, T])
        d["ysT"] = dout("ysT", [D, T])
        d["n_ckvT"] = dout("n_ckvT", [DEPTH, 512, T])
        d["n_kpeT"] = dout("n_kpeT", [DEPTH, 64, T])
        d["n_skT"] = dout("n_skT", [DEPTH, 4, 64, T])
        d["n_sv"] = dout("n_sv", [DEPTH, T, 256])
        d["n_sg"] = dout("n_sg", [DEPTH, 2, 2, 4, 128, 256])
        if dbg is not None:
            d["dbg"] = dout("dbg", list(dbg))
        self.xscr = Tl(nc.dram_tensor("xscr", [D, 4 * T], F32).ap(), None, [Buf("xscr%d" % k) for k in range(4)])
        self.xoutA = [Tl(nc.dram_tensor("xoutA%d" % l, [4 * XROWS, XCOLS], F32).ap(), Buf("xoutA")) for l in range(DEPTH)]
        self.xoutB = [Tl(nc.dram_tensor("xoutB%d" % l, [4 * 128, YCOLS], F32).ap(), Buf("xoutB")) for l in range(DEPTH)]

        def sb(name, shape, dt):
            return Tl(nc.alloc_sbuf_tensor(name, list(shape), dt), Buf(name))
        self.xT = sb("xT", [128, NCH, T], F32)
        self.xT.bs = [Buf("xT%d" % c) for c in range(NCH)]
        self.nslots = 3
        self.slots = [sb("slot%d" % i, [128, 8192], BF16) for i in range(self.nslots)]
        self.slot_ch = [P.new_chan("slot%d" % i) for i in range(self.nslots)]
        self.slot_chs = list(self.slot_ch)
        self.slot_i = 0
        self.xslot_ch = [P.new_chan("xslot%d" % i) for i in range(4)]
        self.sp = sb("smallp_s", [128, SP_N], F32)
        self.rope = sb("rope_s", [64, 2, T], F32)
        self.identb = sb("identb", [128, 128], BF16)
        self.onesb = sb("onesb", [128, 128], BF16)
        self.onesf = sb("onesf", [33, 128], F32)
        self.maskUb = sb("maskUb", [128, 128], BF16)
        self.maskLb = sb("maskLb", [128, 128], BF16)
        self.modT = sb("modT", [128, DEPTH, 144, 2], F32)
        self.modA = sb("modA", [128, 16], F32)
        self.modG = sb("modG", [128, 16], F32)
        self.esink = sb("esink", [128, 16], F32)
        self.fscr = sb("fscr", [128, 2], F32)
        self.arena_elems = (nc.sbuf_bytes_remaining - 512) // 2 // 64 * 64
        self.arena_t = nc.alloc_sbuf_tensor("arena", [128, self.arena_elems], BF16)
        self.a_off = 0
        self.a_tok = None
        self.ps = [Tl(nc.alloc_psum_tensor("ps%d" % i, [128, 512], F32), Buf("ps%d" % i)) for i in range(8)]
        self.ch_in = P.new_chan("in")
        self.ch_out = [P.new_chan("out%d" % i) for i in range(4)]
        self.out_i = 0
        self.ch_x = P.new_chan("xchg")
        self.ch_pl = P.new_chan("pload")
        self.pinned = set()

    def alloc(self, shape, dt, name="", nb=0):
        n = int(np.prod(shape[1:]))
        sz = n * (2 if dt == F32 else 1)
        sz = (sz + 15) // 16 * 16
        assert self.a_off + sz <= self.arena_elems, ("arena overflow", name, self.a_off, sz, self.arena_elems)
        self.a_peak = max(getattr(self, "a_peak", 0), self.a_off + sz)
        v = self.arena_t[0:shape[0], self.a_off:self.a_off + sz]
        self.a_off += sz
        if dt == F32:
            v = v.bitcast(F32)
        v = v[:, 0:n]
        if len(shape) == 3:
            v = v.rearrange("p (a b) -> p a b", a=shape[1])
        elif len(shape) == 4:
            v = v.rearrange("p (a b c) -> p a b c", a=shape[1], b=shape[2])
        return Tl(v, Buf(name, self.a_tok), [Buf(name + str(k), self.a_tok) for k in range(nb)] if nb else None)

    def mark(self):
        return self.a_off

    def release(self, m):
        fs = self.fscr
        self.a_tok = self.P.fence(lambda e: e.memset(fs.ap[:, 0:1], 0.0))
        self.a_off = m

    def sps(self, name):
        o, w = SP_OFF[name]
        return self.sp.ap[:, o:o + w]

    def mm(self, out, lhsT, rhs, start, stop, reads, writes):
        self.P.op("pe", lambda e: e.matmul(out, lhsT, rhs, start=start, stop=stop), reads, writes)

    def tr(self, out, in_, ident, reads, writes):
        self.P.op("pe", lambda e: e.transpose(out, in_, ident), reads, writes)

    def act(self, out, in_, func, reads, writes, bias=None, scale=None):
        kw = {}
        if bias is not None:
            kw["bias"] = bias
        if scale is not None:
            kw["scale"] = scale
        self.P.op("act", lambda e: e.activation(out=out, in_=in_, func=func, **kw), reads, writes)

    def tt(self, out, in0, in1, op, reads, writes):
        self.P.op("dve", lambda e: e.tensor_tensor(out=out, in0=in0, in1=in1, op=op), reads, writes)

    def stt(self, out, in0, scalar, in1, op0, op1, reads, writes):
        self.P.op("dve", lambda e: e.scalar_tensor_tensor(out=out, in0=in0, scalar=scalar, in1=in1, op0=op0, op1=op1), reads, writes)

    def ts(self, out, in0, s1, s2, op0, op1, reads, writes):
        if s2 is None:
            self.P.op("dve", lambda e: e.tensor_scalar(out=out, in0=in0, scalar1=s1, scalar2=None, op0=op0), reads, writes)
        else:
            self.P.op("dve", lambda e: e.tensor_scalar(out=out, in0=in0, scalar1=s1, scalar2=s2, op0=op0, op1=op1), reads, writes)

    def recip(self, out, in_, reads, writes):
        self.P.op("dve", lambda e: e.reciprocal(out=out, in_=in_), reads, writes)

    def vcopy(self, out, in_, reads, writes):
        self.P.op("dve", lambda e: e.tensor_copy(out=out, in_=in_), reads, writes)

    def load(self, out, in_, writes, reads=(), q="sp", ch=None):
        if ch is None:
            ch = self.ch_in if q == "sp" else self.ch_pl
        self.P.dma(q, ch, lambda e: e.dma_start(out=out, in_=in_), reads, writes)

    def store(self, out, in_, reads, writes=()):
        ch = self.ch_out[self.out_i % len(self.ch_out)]
        self.out_i += 1
        self.P.dma("sp", ch, lambda e: e.dma_start(out=out, in_=in_), reads, writes)

    def wslab(self, src, kc, n):
        s = self.slot_i % len(self.slots)
        self.slot_i += 1
        sl = self.slots[s]
        ch_ = self.slot_chs[s]
        if kc is None:
            view = sl.ap[0:src.shape[0], 0:n]
        else:
            view = sl.ap[0:src.shape[0], 0:kc * n].rearrange("p (k n) -> p k n", k=kc)
        self.P.dma("pool", ch_, lambda e: e.dma_start(out=view, in_=src), (), [sl.b])
        return Tl(view, sl.b)

    def push_xslots(self, reserve):
        free = self.arena_elems - self.a_off
        n_extra = max(0, min(len(self.xslot_ch), (free - reserve) // 8192))
        for k_ in range(n_extra):
            self.slots.append(self.alloc([128, 8192], BF16, "xslot%d" % k_))
            self.slot_chs.append(self.xslot_ch[k_])

    def pop_xslots(self):
        del self.slots[self.nslots:]
        del self.slot_chs[self.nslots:]

    def rows(self, w2d, r0, nrow, c0, ncol, p=128):
        return w2d[r0:r0 + nrow, c0:c0 + ncol].rearrange("(c p) n -> p c n", p=p)

    def consts(self):
        d = self.d
        self.load(self.sp.ap[:, :], d["smallp"], [self.sp.b])
        o, _ = SP_OFF["ident"]
        self.identf = self.sp.ap[:, o:o + 128]
        self.vcopy(self.identb.ap[:, :], self.identf, [self.sp.b], [self.identb.b])
        self.vcopy(self.maskUb.ap[:, :], self.sps("maskU"), [self.sp.b], [self.maskUb.b])
        self.vcopy(self.maskLb.ap[:, :], self.sps("maskL"), [self.sp.b], [self.maskLb.b])
        ob, of = self.onesb, self.onesf
        self.P.op("dve", lambda e: e.memset(ob.ap[:, :], 1.0), (), [ob.b])
        self.P.op("dve", lambda e: e.memset(of.ap[:, :], 1.0), (), [of.b])

    def adaln(self):
        d = self.d
        m0 = self.mark()
        self.push_xslots(4 * 1024)
        scT = self.alloc([128, 16, 2], BF16, "scT")
        st = [self.alloc([2, 512], F32, "adast%d" % i) for i in range(2)]
        cond = self.sps("cond").rearrange("p (c j) -> p c j", c=16)
        self.act(scT.ap[:, :, :], cond, AF.Silu, [self.sp.b], [scT.b])
        psM = self.ps[7]
        for l in range(self.n_layers):
            wv = d["w_ada"][l].rearrange("(c p) n -> p c n", p=128)
            for sbk in range(36):
                slab = self.wslab(wv[:, :, sbk * 512:(sbk + 1) * 512], 16, 512)
                pa = self.ps[sbk % 2]
                for c in range(16):
                    self.mm(pa.ap[0:2, :], scT.ap[:, c, :], slab.ap[:, c, :], c == 0, c == 15, [scT.b, slab.b], [pa.b])
                s_ = st[sbk % 2]
                self.act(s_.ap[:, :], pa.ap[0:2, :], AF.Copy, [pa.b], [s_.b])
                for jj in range(4):
                    j = sbk * 4 + jj
                    self.tr(psM.ap[:, 2 * j:2 * j + 2], s_.ap[0:2, jj * 128:(jj + 1) * 128], self.identf[0:2, 0:2], [s_.b, self.sp.b], [psM.b])
            bo, _ = SP_OFF["bada%d" % l]
            bias = self.sp.ap[:, bo:bo + 144].unsqueeze(2).broadcast_to([128, 144, 2])
            self.tt(self.modT.ap[:, l, :, :], psM.ap[:, 0:288].rearrange("p (j c) -> p j c", c=2), bias, ALU.add, [psM.b, self.sp.b], [self.modT.b])
        self.pop_xslots()
        self.release(m0)

    def mod_setup(self, l, i, cond, half_gate):
        go, _ = SP_OFF["gn%d" % l]
        g = self.sp.ap[:, go + 16 * i:go + 16 * i + 16]
        sc = self.modT.ap[:, l, (3 * i + 1) * 16:(3 * i + 2) * 16, cond]
        gt = self.modT.ap[:, l, (3 * i + 2) * 16:(3 * i + 3) * 16, cond]
        self.stt(self.modA.ap[:, :], sc, 1.0, g, ALU.add, ALU.mult, [self.modT.b, self.sp.b], [self.modA.b])
        self.ts(self.modG.ap[:, :], gt, 0.5 if half_gate else 1.0, None, ALU.mult, None, [self.modT.b], [self.modG.b])

    def rstd_bc(self, ss_ps, n, width, tmp, out, reads):
        self.act(tmp.ap, ss_ps, AF.Sqrt, reads, [tmp.b], bias=EPS, scale=1.0 / n)
        self.recip(out.ap, tmp.ap, [tmp.b], [out.b])

    def norm_mod(self, l, i, cond, hT):
        m0 = self.mark()
        sq = [self.alloc([128, T], BF16, "sq%d" % k) for k in range(2)]
        tmp = self.alloc([128, T], F32, "nm_tmp")
        rstd = self.alloc([128, T], F32, "nm_rstd")
        tf = [self.alloc([128, T], F32, "nm_t%d" % k) for k in range(2)]
        xT = self.xT
        ssp = self.ps[6]
        for c in range(16):
            s_ = sq[c % 2]
            self.act(s_.ap[:, :], xT.ap[:, c, :], AF.Square, [xT.bs[c]], [s_.b])
            self.mm(ssp.ap[:, :], self.onesb.ap[:, :], s_.ap[:, :], c == 0, c == 15, [self.onesb.b, s_.b], [ssp.b])
        self.rstd_bc(ssp.ap[:, :], D, T, Tl(tmp.ap[:, :], tmp.b), Tl(rstd.ap[:, :], rstd.b), [ssp.b])
        for c in range(16):
            t_ = tf[c % 2]
            self.stt(t_.ap[:, :], xT.ap[:, c, :], self.modA.ap[:, c:c + 1], rstd.ap[:, :], ALU.mult, ALU.mult, [xT.bs[c], self.modA.b, rstd.b], [t_.b])
            self.act(hT.ap[:, c, :], t_.ap[:, :], AF.Identity, [t_.b, self.modT.b], [hT.bs[c]], bias=self.modT.ap[:, l, 3 * i * 16 + c, cond:cond + 1])
        self.release(m0)

    def ffn(self, l, which, cond, after_dq=None):
        d = self.d
        wgu = d["w_ff%d_gu" % which][l]
        wdn = d["w_ff%d_down" % which][l]
        i = 0 if which == 1 else 2
        m0 = self.mark()
        self.push_xslots(33 * 1024)
        hT = self.alloc([128, 16, T], BF16, "hT", nb=16)
        self.mod_setup(l, i, cond, True)
        self.norm_mod(l, i, cond, hT)
        actT = self.alloc([128, NFC, T], BF16, "actT", nb=NFC)
        sil = [self.alloc([128, T], F32, "sil%d" % k) for k in range(2)]
        k = 0
        for fb in range(11):
            sa = self.wslab(self.rows(wgu, 0, D, fb * 512, 512), 16, 512)
            su = self.wslab(self.rows(wgu, 0, D, DFF + fb * 512, 512), 16, 512)
            for j in range(4):
                fc = fb * 4 + j
                pa = self.ps[(k % 2) * 2]
                pu = self.ps[(k % 2) * 2 + 1]
                for c in range(16):
                    self.mm(pa.ap[:, :], sa.ap[:, c, j * 128:(j + 1) * 128], hT.ap[:, c, :], c == 0, c == 15, [sa.b, hT.bs[c]], [pa.b])
                for c in range(16):
                    self.mm(pu.ap[:, :], su.ap[:, c, j * 128:(j + 1) * 128], hT.ap[:, c, :], c == 0, c == 15, [su.b, hT.bs[c]], [pu.b])
                s_ = sil[k % 2]
                self.act(s_.ap[:, :], pa.ap[:, :], AF.Silu, [pa.b], [s_.b])
                self.tt(actT.ap[:, fc, :], s_.ap[:, :], pu.ap[:, :], ALU.mult, [s_.b, pu.b], [actT.bs[fc]])
                k += 1
        xT = self.xT
        for dq in range(4):
            for sl in range(3):
                nk = 16 if sl < 2 else 12
                sw = self.wslab(self.rows(wdn, sl * 2048, nk * 128, dq * 512, 512), nk, 512)
                for kc in range(nk):
                    fc = sl * 16 + kc
                    for dj in range(4):
                        po = self.ps[4 + dj]
                        self.mm(po.ap[:, :], sw.ap[:, kc, dj * 128:(dj + 1) * 128], actT.ap[:, fc, :], fc == 0, fc == NFC - 1, [sw.b, actT.bs[fc]], [po.b])
            for dj in range(4):
                ch = dq * 4 + dj
                po = self.ps[4 + dj]
                self.stt(xT.ap[:, ch, :], po.ap[:, :], self.modG.ap[:, ch:ch + 1], xT.ap[:, ch, :], ALU.mult, ALU.add, [po.b, self.modG.b, xT.bs[ch]], [xT.bs[ch]])
            if after_dq is not None:
                after_dq(dq)
        self.pop_xslots()
        self.release(m0)

    def nps(self):
        while True:
            self._psi = getattr(self, "_psi", 0) + 1
            if (self._psi % 8) not in self.pinned:
                return self.ps[self._psi % 8]

    def pps(self):
        t = self.nps()
        self.pinned.add(self._psi % 8)
        return t

    def unpin(self, *tls):
        for t in tls:
            self.pinned.discard(self.ps.index(t))

    def rope_apply(self, x, out_ap, out_b, tmp1, tmp2):
        pp = self.nps()
        po, _ = SP_OFF["permT"]
        self.mm(pp.ap[0:64, :], self.sp.ap[0:64, po:po + 64], x.ap, True, True, [self.sp.b, x.b], [pp.b])
        self.tt(tmp1.ap, x.ap, self.rope.ap[:, 0, :], ALU.mult, [x.b, self.rope.b], [tmp1.b])
        self.tt(tmp2.ap, pp.ap[0:64, :], self.rope.ap[:, 1, :], ALU.mult, [pp.b, self.rope.b], [tmp2.b])
        self.tt(out_ap, tmp1.ap, tmp2.ap, ALU.add, [tmp1.b, tmp2.b], [out_b])

    def sumsq_rstd(self, parts_list, n, rstd, tmp, width=T, extra=None, m=128):
        ss = self.nps()
        nmm = len(parts_list) + (len(extra) if extra else 0)
        k = 0
        for (src, sb_, npart) in parts_list:
            sq = self.sqt[self._sqi % 2]
            self._sqi += 1
            self.act(sq.ap[0:npart, 0:width], src, AF.Square, [sb_], [sq.b])
            self.mm(ss.ap[0:m, 0:width], self.onesb.ap[0:npart, 0:m], sq.ap[0:npart, 0:width], k == 0, k == nmm - 1, [self.onesb.b, sq.b], [ss.b])
            k += 1
        if extra:
            for (lh, rh, rd) in extra:
                self.mm(ss.ap[0:m, 0:width], lh, rh, k == 0, k == nmm - 1, rd, [ss.b])
                k += 1
        self.act(tmp.ap[0:m, 0:width], ss.ap[0:m, 0:width], AF.Sqrt, [ss.b], [tmp.b], bias=EPS, scale=1.0 / n)
        self.recip(rstd.ap[0:m, 0:width], tmp.ap[0:m, 0:width], [tmp.b], [rstd.b])

    def mixer(self, l, cond, is_s, phase=0, tg=0, after_norm=None):
        self.tg = tg
        m0 = self.mark()
        hT = self.alloc([128, 16, T], BF16, "hT", nb=16)
        self.hT = hT
        self.mod_setup(l, 1, cond, False)
        self.norm_mod(l, 1, cond, hT)
        if after_norm is not None:
            after_norm()
        oT_gla = None if (is_s and phase == 1) else self.alloc([128, 8, T], BF16, "oT_gla")
        self.sqt = [self.alloc([128, T], BF16, "sqt%d" % k) for k in range(2)]
        self._sqi = 0
        self.rtmp = self.alloc([128, T], F32, "rtmp")
        self.rstd = [self.alloc([128, T], F32, "rstd%d" % k) for k in range(2)]
        self._ri = 0
        if is_s and phase == 1:
            m1 = self.mark()
            self.stage_kv(l, True)
            self.release(m1)
            self.stage_gla(l, True, True, oT_gla)
            self.release(m0)
            return
        if not is_s:
            self.stage_gla(l, False, False, oT_gla)
            self.stage_kv(l, False)
        oT_mla = self.alloc([128, 8, T], BF16, "oT_mla")
        self.stage_mla(l, is_s, oT_mla)
        if is_s:
            self.stage_gla(l, True, False, oT_gla)
        oT_swa = self.alloc([64, 16, T], BF16, "oT_swa")
        self.stage_swa(l, is_s, oT_swa)
        self.stage_merge(l, oT_mla, oT_gla, oT_swa)
        self.release(m0)

    def next_rstd(self):
        self._ri += 1
        return self.rstd[self._ri % 2]

    def stage_kv(self, l, is_s):
        d = self.d
        win = d["w_in"][l]
        hT = self.hT
        if not is_s:
            self.ckvb = self.alloc([128, 4, T], BF16, "ckvb")
            self.kpef = self.alloc([64, T], F32, "kpef")
            self.sqk = self.alloc([64, T], BF16, "sqk")
            self.swkb = self.alloc([64, 4, T], BF16, "swkb")
            self.swvb = self.alloc([128, 4, 256], BF16, "swvb")
            kpef = self.kpef
        else:
            kpef = self.alloc([64, T], F32, "kpef")
            sqk = self.alloc([64, T], BF16, "sqk_s")
        m1 = self.mark()
        ckvf = self.alloc([128, 4, T], F32, "ckvf")
        skn = [self.alloc([64, T], F32, "skn%d" % k) for k in range(2)]
        svf = self.alloc([128, 4, 256], F32, "svf")
        rt1 = self.alloc([64, T], F32, "rt1")
        rt2 = self.alloc([64, T], F32, "rt2")
        rt3 = self.alloc([64, T], F32, "rt3")
        xo_ = self.xoutA[l]
        xin = Tl(xo_.ap[self.tg * 128:(self.tg + 1) * 128, :], xo_.b)
        slab = self.wslab(self.rows(win, 0, D, O_CKV, 512), 16, 512)
        pcs = [self.pps() for k in range(4)]
        for c4 in range(4):
            for c in range(16):
                self.mm(pcs[c4].ap[:, :], slab.ap[:, c, c4 * 128:(c4 + 1) * 128], hT.ap[:, c, :], c == 0, c == 15, [slab.b, hT.bs[c]], [pcs[c4].b])
        rs = self.next_rstd()
        self.sumsq_rstd([(pcs[c4].ap[:, :], pcs[c4].b, 128) for c4 in range(4)], 512, rs, self.rtmp)
        go, _ = SP_OFF["gkv%d" % l]
        for c4 in range(4):
            self.stt(ckvf.ap[:, c4, :], pcs[c4].ap[:, :], self.sp.ap[:, go + c4:go + c4 + 1], rs.ap[:, :], ALU.mult, ALU.mult, [pcs[c4].b, self.sp.b, rs.b], [ckvf.b])
        if not is_s:
            self.store(d["n_ckvT"][l].rearrange("(c p) t -> p c t", p=128), ckvf.ap[:, :, :], [ckvf.b])
            self.act(self.ckvb.ap[:, :, :], ckvf.ap[:, :, :], AF.Copy, [ckvf.b], [self.ckvb.b])
        else:
            self.store(xin.ap[:, X_CKV:X_CKV + 2048].rearrange("p (c t) -> p c t", c=4), ckvf.ap[:, :, :], [ckvf.b], [xin.b])
        self.unpin(*pcs)
        slab = self.wslab(self.rows(win, 0, D, O_KPE, 64), 16, 64)
        pk = self.nps()
        for c in range(16):
            self.mm(pk.ap[0:64, :], slab.ap[:, c, :], hT.ap[:, c, :], c == 0, c == 15, [slab.b, hT.bs[c]], [pk.b])
        self.act(kpef.ap[:, :], pk.ap[0:64, :], AF.Copy, [pk.b], [kpef.b])
        if not is_s:
            self.store(d["n_kpeT"][l], kpef.ap[:, :], [kpef.b])
            self.act(self.sqk.ap[:, :], kpef.ap[:, :], AF.Square, [kpef.b], [self.sqk.b])
        else:
            self.act(sqk.ap[:, :], kpef.ap[:, :], AF.Square, [kpef.b], [sqk.b])
            pr = self.nps()
            self.mm(pr.ap[0:1, :], self.onesb.ap[0:64, 0:1], sqk.ap[:, :], True, True, [self.onesb.b, sqk.b], [pr.b])
            self.act(rt3.ap[0:1, :], pr.ap[0:1, :], AF.Copy, [pr.b], [rt3.b])
            self.store(xin.ap[X_SS[0]:X_SS[0] + 1, X_SS[1]:X_SS[1] + T], rt3.ap[0:1, :], [rt3.b], [xin.b])
            self.P.op("dve", lambda e: e.memset(rt1.ap[:, :], 0.0), (), [rt1.b])
            self.store(xin.ap[X_SS[0] + 1:128, X_SS[1]:X_SS[1] + T], rt1.ap[0:63, :], [rt1.b], [xin.b])
            go, _ = SP_OFF["gkn_r%d" % l]
            kg = skn[0]
            self.ts(kg.ap[:, :], kpef.ap[:, :], self.sp.ap[0:64, go:go + 1], None, ALU.mult, None, [kpef.b, self.sp.b], [kg.b])
            self.rope_apply(Tl(kg.ap[:, :], kg.b), skn[1].ap[:, :], skn[1].b, Tl(rt1.ap[:, :], rt1.b), Tl(rt2.ap[:, :], rt2.b))
            self.store(xin.ap[X_KR[0]:X_KR[0] + 64, X_KR[1]:X_KR[1] + T], skn[1].ap[:, :], [skn[1].b], [xin.b])
        slab = self.wslab(self.rows(win, 0, D, O_SK, 512), 16, 512)
        go, _ = SP_OFF["gsk%d" % l]
        for hk in range(4):
            pk = self.nps()
            for c in range(16):
                self.mm(pk.ap[0:64, :], slab.ap[:, c, hk * 64:(hk + 1) * 64], hT.ap[:, c, :], c == 0, c == 15, [slab.b, hT.bs[c]], [pk.b])
            rs = self.next_rstd()
            self.sumsq_rstd([(pk.ap[0:64, :], pk.b, 64)], 64, rs, self.rtmp, m=64)
            sk_ = skn[hk % 2]
            self.stt(sk_.ap[:, :], pk.ap[0:64, :], self.sp.ap[0:64, go:go + 1], rs.ap[0:64, :], ALU.mult, ALU.mult, [pk.b, self.sp.b, rs.b], [sk_.b])
            if not is_s:
                self.store(d["n_skT"][l, hk], sk_.ap[:, :], [sk_.b])
                self.act(self.swkb.ap[:, hk, :], sk_.ap[:, :], AF.Copy, [sk_.b], [self.swkb.b])
            else:
                self.rope_apply(Tl(sk_.ap[:, :], sk_.b), rt3.ap[:, :], rt3.b, Tl(rt1.ap[:, :], rt1.b), Tl(rt2.ap[:, :], rt2.b))
                self.store(xin.ap[X_SWK[hk][0]:X_SWK[hk][0] + 64, X_SWK[hk][1]:X_SWK[hk][1] + T], rt3.ap[:, :], [rt3.b], [xin.b])
        for st in range(4):
            pv = self.nps()
            for c in range(16):
                self.mm(pv.ap[:, 0:256], hT.ap[:, c, st * 128:(st + 1) * 128], slab.ap[:, c, 256:512], c == 0, c == 15, [slab.b, hT.bs[c]], [pv.b])
            self.act(svf.ap[:, st, :], pv.ap[:, 0:256], AF.Copy, [pv.b], [svf.b])
        if not is_s:
            self.store(d["n_sv"][l].rearrange("(a p) n -> p a n", p=128), svf.ap[:, :, :], [svf.b])
            self.vcopy(self.swvb.ap[:, :, :], svf.ap[:, :, :], [svf.b], [self.swvb.b])
        else:
            self.store(xin.ap[:, X_SWV:X_SWV + 1024].rearrange("p (a n) -> p a n", a=4), svf.ap[:, :, :], [svf.b], [xin.b])
        if not is_s:
            self.release(m1)

    def stage_gla(self, l, is_s, state_only, oT_gla):
        d = self.d
        win = d["w_in"][l]
        hT = self.hT
        full = not state_only
        m1 = self.mark()
        if full:
            oT = self.alloc([128, 8, T], F32, "oTg")
        Sf = self.alloc([128, 2, 4, 256], F32, "Sf")
        Sb = self.alloc([128, 2, 4, 256], BF16, "Sb")
        tS = self.alloc([128, 4, 256], F32, "tS")
        if is_s and full:
            mi = self.mark()
            self.gla_init_states(l, Sf, Sb, tS)
            self.release(mi)
        m2 = self.mark()
        kg = self.alloc([128, 4, T], BF16, "kg")
        vtm = self.alloc([128, 4, 1024], BF16, "vtm")
        la = self.alloc([128, 4, 512], F32, "la")
        ggT = self.alloc([17, 2, T], F32, "ggT")
        wgt = self.alloc([17, 2, 512], F32, "wgt")
        nbuf = 2 if state_only else 1
        ebs = [self.alloc([128, 4, 128], F32, "eb%d" % k_) for k_ in range(nbuf)]
        enbs = [self.alloc([128, 4, 128], F32, "enb%d" % k_) for k_ in range(nbuf)]
        kts = [self.alloc([128, 4, 128], BF16, "kt%d" % k_) for k_ in range(nbuf)]
        ktms = [self.alloc([128, 4, 128], BF16, "ktm%d" % k_) for k_ in range(nbuf)]
        it_ = 0
        cum = self.alloc([128, 2, 128], F32, "cum")
        PA = self.alloc([128, 2, 4], F32, "PA")
        if full:
            qg = self.alloc([128, 4, T], BF16, "qg")
            qt = self.alloc([128, 4, 128], BF16, "qt")
            ATs = self.alloc([128, 4, 128], BF16, "ATs")
        self.ts(cum.ap[:, 0, :], self.sps("maskU"), 1.0 / 16, None, ALU.mult, None, [self.sp.b], [cum.b])
        self.ts(cum.ap[:, 1, :], self.sps("maskL"), 1.0 / 16, None, ALU.mult, None, [self.sp.b], [cum.b])
        self.load(wgt.ap[0:16, :, :], d["wg"][:, l, :, :], [wgt.b])
        self.load(wgt.ap[16:17, :, :], d["bg"][:, l, :, :], [wgt.b])
        self.P.op("dve", lambda e: e.memset(ggT.ap[:, :, :], 1.0), (), [ggT.b])
        if full:
            slab = self.wslab(self.rows(win, 0, D, O_GQ, 512), 16, 512)
            for h in range(4):
                pq = self.nps()
                for c in range(16):
                    self.mm(pq.ap[:, :], slab.ap[:, c, h * 128:(h + 1) * 128], hT.ap[:, c, :], c == 0, c == 15, [slab.b, hT.bs[c]], [pq.b])
                self.ts(qg.ap[:, h, :], pq.ap[:, :], GLA_QS, None, ALU.mult, None, [pq.b], [qg.b])
        slab = self.wslab(self.rows(win, 0, D, O_GK, 512), 16, 512)
        for h in range(4):
            pq = self.nps()
            for c in range(16):
                self.mm(pq.ap[:, :], slab.ap[:, c, h * 128:(h + 1) * 128], hT.ap[:, c, :], c == 0, c == 15, [slab.b, hT.bs[c]], [pq.b])
            self.vcopy(kg.ap[:, h, :], pq.ap[:, :], [pq.b], [kg.b])
        for half in range(2):
            slab = self.wslab(self.rows(win, 0, D, O_GV + half * 512, 512), 16, 512)
            for st in range(4):
                pv = self.nps()
                for c in range(16):
                    self.mm(pv.ap[:, :], hT.ap[:, c, st * 128:(st + 1) * 128], slab.ap[:, c, :], c == 0, c == 15, [slab.b, hT.bs[c]], [pv.b])
                if st % 2 == 0:
                    self.act(vtm.ap[:, st, half * 512:(half + 1) * 512], pv.ap[:, :], AF.Copy, [pv.b], [vtm.b])
                else:
                    self.vcopy(vtm.ap[:, st, half * 512:(half + 1) * 512], pv.ap[:, :], [pv.b], [vtm.b])
        slab = self.wslab(self.rows(win, 0, D, O_GGF, 32), 16, 32)
        for dr in range(2):
            pg = self.nps()
            for c in range(16):
                self.mm(pg.ap[0:16, :], slab.ap[:, c, dr * 16:(dr + 1) * 16], hT.ap[:, c, :], c == 0, c == 15, [slab.b, hT.bs[c]], [pg.b])
            self.vcopy(ggT.ap[0:16, dr, :], pg.ap[0:16, :], [pg.b], [ggT.b])
        seqs = [[0, 1, 2, 3]] if is_s else [[0, 1], [2, 3]]
        first_o = [True] * 4
        st2 = ""
        if st2 == "proj":
            self.release(m1)
            return
        for dr in range(2):
            for st in range(4):
                pl = self.nps()
                self.mm(pl.ap[:, :], ggT.ap[0:17, dr, st * 128:(st + 1) * 128], wgt.ap[0:17, dr, :], True, True, [ggT.b, wgt.b], [pl.b])
                self.act(la.ap[:, st, :], pl.ap[:, :], AF.Sigmoid, [pl.b], [la.b])
            self.act(la.ap[:, :, :], la.ap[:, :, :], AF.Ln, [la.b], [la.b])
            if st2 == "la":
                self.release(m1)
                return
            for si, seq in enumerate(seqs):
                order = seq if dr == 0 else seq[::-1]
                zero_state = not (is_s and full)
                for n in order:
                    eb, enb, kt, ktm = ebs[it_ % nbuf], enbs[it_ % nbuf], kts[it_ % nbuf], ktms[it_ % nbuf]
                    it_ += 1
                    csl = slice(n * 128, (n + 1) * 128)
                    pb = self.nps()
                    for h in range(4):
                        self.mm(pb.ap[:, h * 128:(h + 1) * 128], la.ap[:, n, h * 128:(h + 1) * 128], cum.ap[:, dr, :], True, True, [la.b, cum.b], [pb.b])
                    pbv = pb.ap[:, :].rearrange("p (h t) -> p h t", h=4)
                    self.act(enb.ap[:, :, :], pbv, AF.Exp, [pb.b], [enb.b], scale=-1.0)
                    self.act(eb.ap[:, :, :], pbv, AF.Exp, [pb.b], [eb.b])
                    ebl = eb.ap[:, :, 127] if dr == 0 else eb.ap[:, :, 0]
                    self.tt(kt.ap[:, :, :], kg.ap[:, :, csl], enb.ap[:, :, :], ALU.mult, [kg.b, enb.b], [kt.b])
                    if full:
                        self.tt(qt.ap[:, :, :], qg.ap[:, :, csl], eb.ap[:, :, :], ALU.mult, [qg.b, eb.b], [qt.b])
                        pa = self.nps()
                        for h in range(4):
                            self.mm(pa.ap[:, h * 128:(h + 1) * 128], kt.ap[:, h, :], qt.ap[:, h, :], True, True, [kt.b, qt.b], [pa.b])
                        mk = self.maskUb if dr == 0 else self.maskLb
                        self.tt(ATs.ap[:, :, :], pa.ap[:, :].rearrange("p (h t) -> p h t", h=4), mk.ap[:, :].unsqueeze(1).broadcast_to([128, 4, 128]), ALU.mult, [pa.b, mk.b], [ATs.b])
                        po = [self.pps(), self.pps()]
                        for h in range(4):
                            for vc in range(2):
                                j = h * 2 + vc
                                dst = po[j // 4].ap[:, (j % 4) * 128:(j % 4 + 1) * 128]
                                self.mm(dst, vtm.ap[:, n, h * 256 + vc * 128:h * 256 + (vc + 1) * 128], ATs.ap[:, h, :], True, zero_state, [vtm.b, ATs.b], [po[j // 4].b])
                                if not zero_state:
                                    self.mm(dst, Sb.ap[:, dr, h, vc * 128:(vc + 1) * 128], qt.ap[:, h, :], False, True, [Sb.b, qt.b], [po[j // 4].b])
                        for hf in range(2):
                            ov = oT.ap[:, hf * 4:(hf + 1) * 4, csl]
                            pv_ = po[hf].ap[:, :].rearrange("p (j t) -> p j t", j=4)
                            if first_o[n]:
                                self.act(ov, pv_, AF.Copy, [po[hf].b], [oT.b])
                            else:
                                self.tt(ov, ov, pv_, ALU.add, [oT.b, po[hf].b], [oT.b])
                        first_o[n] = False
                        self.unpin(*po)
                    pt = self.nps()
                    ptb = pt.ap[:, 0:256].bitcast(BF16)
                    for h in range(4):
                        self.tr(ptb[:, h * 128:(h + 1) * 128], kt.ap[:, h, :], self.identb.ap[:, :], [kt.b, self.identb.b], [pt.b])
                    self.act(ktm.ap[:, :, :], ptb.rearrange("p (h k) -> p h k", h=4), AF.Copy, [pt.b], [ktm.b])
                    pd = [self.pps(), self.pps()]
                    for h in range(4):
                        self.mm(pd[h // 2].ap[:, (h % 2) * 256:(h % 2 + 1) * 256], ktm.ap[:, h, :], vtm.ap[:, n, h * 256:(h + 1) * 256], True, True, [ktm.b, vtm.b], [pd[h // 2].b])
                    eblb = ebl.unsqueeze(2).broadcast_to([128, 4, 256])
                    for hf in range(2):
                        sv_ = Sf.ap[:, dr, hf * 2:(hf + 1) * 2, :]
                        dv_ = pd[hf].ap[:, :].rearrange("p (h v) -> p h v", h=2)
                        ev_ = ebl[:, hf * 2:(hf + 1) * 2].unsqueeze(2).broadcast_to([128, 2, 256])
                        tv_ = tS.ap[:, hf * 2:(hf + 1) * 2, :]
                        if zero_state:
                            self.tt(sv_, dv_, ev_, ALU.mult, [pd[hf].b, eb.b], [Sf.b])
                        else:
                            self.tt(tv_, dv_, sv_, ALU.add, [pd[hf].b, Sf.b], [tS.b])
                            self.tt(sv_, tv_, ev_, ALU.mult, [tS.b, eb.b], [Sf.b])
                    self.unpin(*pd)
                    if full:
                        self.act(Sb.ap[:, dr, :, :], Sf.ap[:, dr, :, :], AF.Copy, [Sf.b], [Sb.b])
                    if state_only:
                        if n == order[0]:
                            self.vcopy(PA.ap[:, dr, :], ebl, [eb.b], [PA.b])
                        else:
                            self.tt(PA.ap[:, dr, :], PA.ap[:, dr, :], ebl, ALU.mult, [PA.b, eb.b], [PA.b])
                    zero_state = False
                if not is_s:
                    self.store(d["n_sg"][l, si, dr].rearrange("h k v -> k h v"), Sf.ap[:, dr, :, :], [Sf.b])
        if state_only:
            xo_ = self.xoutB[l]
            xin = Tl(xo_.ap[self.tg * 128:(self.tg + 1) * 128, :], xo_.b)
            self.store(xin.ap[:, 0:8], PA.ap[:, :, :].rearrange("p a b -> p (a b)"), [PA.b], [xin.b])
            self.store(xin.ap[:, 8:8 + 2048], Sf.ap[:, :, :, :].rearrange("p a b c -> p (a b c)"), [Sf.b], [xin.b])
        self.release(m2)
        if full:
            go, _ = SP_OFF["ggo%d" % l]
            on = [self.alloc([128, T], F32, "on%d" % k) for k in range(2)]
            sg = [self.alloc([128, T], F32, "sg%d" % k) for k in range(2)]
            slabs = [self.wslab(self.rows(win, 0, D, O_GOUT + half * 512, 512), 16, 512) for half in range(2)]
            for h in range(4):
                rs = self.next_rstd()
                self.sumsq_rstd([(oT.ap[:, h * 2 + vc, :], oT.b, 128) for vc in range(2)], 256, rs, self.rtmp)
                for vc in range(2):
                    j = h * 2 + vc
                    pg = self.nps()
                    sl = slabs[j // 4]
                    for c in range(16):
                        self.mm(pg.ap[:, :], sl.ap[:, c, (j % 4) * 128:(j % 4 + 1) * 128], hT.ap[:, c, :], c == 0, c == 15, [sl.b, hT.bs[c]], [pg.b])
                    self.act(sg[j % 2].ap[:, :], pg.ap[:, :], AF.Silu, [pg.b], [sg[j % 2].b])
                    self.stt(on[j % 2].ap[:, :], oT.ap[:, j, :], self.sp.ap[:, go + vc:go + vc + 1], rs.ap[:, :], ALU.mult, ALU.mult, [oT.b, self.sp.b, rs.b], [on[j % 2].b])
                    self.tt(oT_gla.ap[:, j, :], on[j % 2].ap[:, :], sg[j % 2].ap[:, :], ALU.mult, [on[j % 2].b, sg[j % 2].b], [oT_gla.b])
        self.release(m1)

    def gla_init_states(self, l, Sf, Sb, tS):
        d = self.d
        xo = self.xoutB[l]
        F = self.alloc([128, 4, 256], F32, "glaF")
        Sl = self.alloc([128, 4, 256], F32, "glaSl")
        Aj = self.alloc([128, 4, 8], F32, "glaA")
        go, _ = SP_OFF["gsel"]
        self.load(Aj.ap[:, :, :], xo.ap[:, 0:8].rearrange("(r p) c -> p r c", p=128), [Aj.b], [xo.b])
        for dr in range(2):
            self.load(F.ap[:, :, :], d["state"][l, dr].rearrange("h k v -> k h v"), [F.b])
            order = [0, 1, 2, 3] if dr == 0 else [3, 2, 1, 0]
            for idx, j in enumerate(order):
                if self.tg == "own":
                    go_ = SP_OFF["gsel_own"][0]
                else:
                    go_ = go + self.tg * 8
                sel = self.sp.ap[:, go_ + dr * 4 + j:go_ + dr * 4 + j + 1]
                sv_ = Sf.ap[:, dr, :, :]
                if idx == 0:
                    self.ts(sv_, F.ap[:, :, :], sel, None, ALU.mult, None, [F.b, self.sp.b], [Sf.b])
                else:
                    self.stt(sv_, F.ap[:, :, :], sel, sv_, ALU.mult, ALU.add, [F.b, self.sp.b, Sf.b], [Sf.b])
                if idx < 3:
                    self.load(Sl.ap[:, :, :], xo.ap[j * 128:(j + 1) * 128, 8 + dr * 1024:8 + (dr + 1) * 1024].rearrange("p (h v) -> p h v", h=4), [Sl.b], [xo.b])
                    av = Aj.ap[:, j, dr * 4:(dr + 1) * 4].unsqueeze(2).broadcast_to([128, 4, 256])
                    self.tt(tS.ap[:, :, :], F.ap[:, :, :], av, ALU.mult, [F.b, Aj.b], [tS.b])
                    self.tt(F.ap[:, :, :], tS.ap[:, :, :], Sl.ap[:, :, :], ALU.add, [tS.b, Sl.b], [F.b])
            self.act(Sb.ap[:, dr, :, :], Sf.ap[:, dr, :, :], AF.Copy, [Sf.b], [Sb.b])

    def stage_mla(self, l, is_s, oT_mla):
        d = self.d
        win = d["w_in"][l]
        hT = self.hT
        m1 = self.mark()
        NK = 2304 if is_s else T
        cqn = self.alloc([128, 4, T], BF16, "cqn")
        qn = self.alloc([128, 8, T], BF16, "qn")
        qr = self.alloc([64, 8, T], BF16, "qr")
        knT = self.alloc([128, NK], BF16, "knT")
        krT = self.alloc([64, NK], BF16, "krT")
        vh = self.alloc([128, NK // 128, 128], BF16, "vh")
        PT = [self.alloc([128, T], BF16, "PT%d" % k) for k in range(3)]
        rd = self.alloc([128, T], F32, "rd")
        if is_s:
            ckva = self.alloc([128, 4, NK], BF16, "ckva")
            krx = self.alloc([64, NK], BF16, "krx")
            ssr = self.alloc([1, NK], BF16, "ssr")
            kpc = self.alloc([64, 256], F32, "kpc")
            sqc = self.alloc([64, 256], BF16, "sqc")
            qrf = self.alloc([64, T], F32, "qrf")
            rt1 = self.alloc([64, T], F32, "mrt1")
            rt2 = self.alloc([64, T], F32, "mrt2")
            xo = self.xoutA[l]
            xr = xo.ap.rearrange("(r p) c -> p r c", p=128)
            for c_ in range(4):
                self.load(ckva.ap[:, c_, 0:2048].rearrange("p (r t) -> p r t", r=4), xr[:, :, X_CKV + c_ * T:X_CKV + (c_ + 1) * T], [ckva.b], [xo.b], q="pool")
            self.load(krx.ap[0:64, 0:2048].rearrange("p (r t) -> p r t", r=4), xr[X_KR[0]:X_KR[0] + 64, :, X_KR[1]:X_KR[1] + T], [krx.b], [xo.b], q="pool")
            self.load(ssr.ap[0:1, 0:2048].rearrange("p (r t) -> p r t", r=4), xr[X_SS[0]:X_SS[0] + 1, :, X_SS[1]:X_SS[1] + T], [ssr.b], [xo.b], q="pool")
            self.load(ckva.ap[:, :, 2048:2304], d["ckv_cT"][l].rearrange("(c p) t -> p c t", p=128), [ckva.b], q="pool")
            self.load(kpc.ap[:, :], d["kpe_cT"][l], [kpc.b])
            self.act(sqc.ap[:, :], kpc.ap[:, :], AF.Square, [kpc.b], [sqc.b])
            go, _ = SP_OFF["gkn_r%d" % l]
            self.ts(kpc.ap[:, :], kpc.ap[:, :], self.sp.ap[0:64, go:go + 1], None, ALU.mult, None, [kpc.b, self.sp.b], [kpc.b])
            ckv_src = ckva
        else:
            ckv_src = self.ckvb
        slab = self.wslab(self.rows(win, 0, D, O_CQ, 512), 16, 512)
        pcs = [self.pps() for k in range(4)]
        for c4 in range(4):
            for c in range(16):
                self.mm(pcs[c4].ap[:, :], slab.ap[:, c, c4 * 128:(c4 + 1) * 128], hT.ap[:, c, :], c == 0, c == 15, [slab.b, hT.bs[c]], [pcs[c4].b])
        rs = self.next_rstd()
        self.sumsq_rstd([(pcs[c4].ap[:, :], pcs[c4].b, 128) for c4 in range(4)], 512, rs, self.rtmp)
        go, _ = SP_OFF["gq%d" % l]
        for c4 in range(4):
            self.stt(cqn.ap[:, c4, :], pcs[c4].ap[:, :], self.sp.ap[:, go + c4:go + c4 + 1], rs.ap[:, :], ALU.mult, ALU.mult, [pcs[c4].b, self.sp.b, rs.b], [cqn.b])
        self.unpin(*pcs)
        wq = self.wslab(d["w_mla_uq"][l].rearrange("(c p) n -> p c n", p=128), 4, 1536)
        gn, _ = SP_OFF["gqn_n%d" % l]
        gr, _ = SP_OFF["gqn_r%d" % l]
        for h in range(8):
            pn = self.pps()
            pr = self.pps()
            for kc in range(4):
                self.mm(pn.ap[:, :], wq.ap[:, kc, h * 192:h * 192 + 128], cqn.ap[:, kc, :], kc == 0, kc == 3, [wq.b, cqn.b], [pn.b])
            for kc in range(4):
                self.mm(pr.ap[0:64, :], wq.ap[:, kc, h * 192 + 128:h * 192 + 192], cqn.ap[:, kc, :], kc == 0, kc == 3, [wq.b, cqn.b], [pr.b])
            rs = self.next_rstd()
            self.sumsq_rstd([(pn.ap[:, :], pn.b, 128), (pr.ap[0:64, :], pr.b, 64)], 192, rs, self.rtmp)
            self.stt(qn.ap[:, h, :], pn.ap[:, :], self.sp.ap[:, gn:gn + 1], rs.ap[:, :], ALU.mult, ALU.mult, [pn.b, self.sp.b, rs.b], [qn.b])
            if is_s:
                self.stt(qrf.ap[:, :], pr.ap[0:64, :], self.sp.ap[0:64, gr:gr + 1], rs.ap[0:64, :], ALU.mult, ALU.mult, [pr.b, self.sp.b, rs.b], [qrf.b])
                self.rope_apply(Tl(qrf.ap[:, :], qrf.b), qr.ap[:, h, :], qr.b, Tl(rt1.ap[:, :], rt1.b), Tl(rt2.ap[:, :], rt2.b))
            else:
                self.stt(qr.ap[:, h, :], pr.ap[0:64, :], self.sp.ap[0:64, gr:gr + 1], rs.ap[0:64, :], ALU.mult, ALU.mult, [pr.b, self.sp.b, rs.b], [qr.b])
            self.unpin(pn, pr)
        wkv = self.wslab(d["w_mla_ukv"][l].rearrange("(c p) n -> p c n", p=128), 4, 2048)
        gn, _ = SP_OFF["gkn_n%d" % l]
        gr, _ = SP_OFF["gkn_r%d" % l]
        blocks = [(n0, min(512, NK - n0)) for n0 in range(0, NK, 512)]
        for h in range(8):
            for (n0, nb) in blocks:
                pk = self.pps()
                for kc in range(4):
                    self.mm(pk.ap[:, 0:nb], wkv.ap[:, kc, h * 256:h * 256 + 128], ckv_src.ap[:, kc, n0:n0 + nb], kc == 0, kc == 3, [wkv.b, ckv_src.b], [pk.b])
                rs = self.next_rstd()
                if not is_s:
                    extra = [(self.onesb.ap[0:64, :], self.sqk.ap[:, n0:n0 + nb], [self.onesb.b, self.sqk.b])]
                elif n0 < 2048:
                    extra = [(self.onesb.ap[0:1, :], ssr.ap[0:1, n0:n0 + nb], [self.onesb.b, ssr.b])]
                else:
                    extra = [(self.onesb.ap[0:64, :], sqc.ap[:, :], [self.onesb.b, sqc.b])]
                self.sumsq_rstd([(pk.ap[:, 0:nb], pk.b, 128)], 192, rs, self.rtmp, width=nb, extra=extra)
                self.stt(knT.ap[:, n0:n0 + nb], pk.ap[:, 0:nb], self.sp.ap[:, gn:gn + 1], rs.ap[:, 0:nb], ALU.mult, ALU.mult, [pk.b, self.sp.b, rs.b], [knT.b])
                self.unpin(pk)
                if not is_s:
                    self.stt(krT.ap[:, n0:n0 + nb], self.kpef.ap[:, n0:n0 + nb], self.sp.ap[0:64, gr:gr + 1], rs.ap[0:64, 0:nb], ALU.mult, ALU.mult, [self.kpef.b, self.sp.b, rs.b], [krT.b])
                elif n0 < 2048:
                    self.tt(krT.ap[:, n0:n0 + nb], krx.ap[0:64, n0:n0 + nb], rs.ap[0:64, 0:nb], ALU.mult, [krx.b, rs.b], [krT.b])
                else:
                    self.tt(krT.ap[:, n0:n0 + nb], kpc.ap[:, :], rs.ap[0:64, 0:nb], ALU.mult, [kpc.b, rs.b], [krT.b])
                pv = self.nps()
                for i in range(nb // 128):
                    st = n0 // 128 + i
                    for kc in range(4):
                        self.mm(pv.ap[:, i * 128:(i + 1) * 128], ckv_src.ap[:, kc, st * 128:(st + 1) * 128], wkv.ap[:, kc, h * 256 + 128:h * 256 + 256], kc == 0, kc == 3, [ckv_src.b, wkv.b], [pv.b])
                self.act(vh.ap[:, n0 // 128:n0 // 128 + nb // 128, :], pv.ap[:, 0:nb].rearrange("p (a v) -> p a v", v=128), AF.Copy, [pv.b], [vh.b])
            qsets = [(0, T, list(range(NK // 128)))] if is_s else [(0, 256, [0, 1]), (256, 256, [2, 3])]
            pO = self.pps()
            pD = self.pps()
            for (q0, qn_, tiles) in qsets:
                for ti, st in enumerate(tiles):
                    psc = self.nps()
                    self.mm(psc.ap[:, 0:qn_], knT.ap[:, st * 128:(st + 1) * 128], qn.ap[:, h, q0:q0 + qn_], True, False, [knT.b, qn.b], [psc.b])
                    self.mm(psc.ap[:, 0:qn_], krT.ap[0:64, st * 128:(st + 1) * 128], qr.ap[0:64, h, q0:q0 + qn_], False, True, [krT.b, qr.b], [psc.b])
                    pt_ = PT[self._sqi % 3]
                    self._sqi += 1
                    self.act(pt_.ap[:, 0:qn_], psc.ap[:, 0:qn_], AF.Exp, [psc.b], [pt_.b], scale=MLA_SCALE)
                    first, last = ti == 0, ti == len(tiles) - 1
                    self.mm(pO.ap[:, q0:q0 + qn_], vh.ap[:, st, :], pt_.ap[:, 0:qn_], first, last, [vh.b, pt_.b], [pO.b])
                    self.mm(pD.ap[:, q0:q0 + qn_], self.onesb.ap[:, :], pt_.ap[:, 0:qn_], first, last, [self.onesb.b, pt_.b], [pD.b])
            self.recip(rd.ap[:, :], pD.ap[:, :], [pD.b], [rd.b])
            self.tt(oT_mla.ap[:, h, :], pO.ap[:, :], rd.ap[:, :], ALU.mult, [pO.b, rd.b], [oT_mla.b])
            self.unpin(pO, pD)
        self.release(m1)

    def stage_swa(self, l, is_s, oT_swa):
        d = self.d
        win = d["w_in"][l]
        hT = self.hT
        m1 = self.mark()
        nqh = 4 if is_s else 16
        qs = self.alloc([64, nqh, T], BF16, "qs")
        rd = self.alloc([64, T], F32, "srd")
        so, _ = SP_OFF["sink%d" % l]
        self.act(self.esink.ap[:, :], self.sp.ap[:, so:so + 16], AF.Exp, [self.sp.b], [self.esink.b])
        if is_s:
            qf = self.alloc([64, T], F32, "sqf")
            rt1 = self.alloc([64, T], F32, "srt1")
            rt2 = self.alloc([64, T], F32, "srt2")
        go, _ = SP_OFF["gsq%d" % l]

        def q_heads(slab, hh_list, dst0):
            for n_, hh in enumerate(hh_list):
                pq = self.nps()
                for c in range(16):
                    self.mm(pq.ap[0:64, :], slab.ap[:, c, hh * 64:(hh + 1) * 64], hT.ap[:, c, :], c == 0, c == 15, [slab.b, hT.bs[c]], [pq.b])
                rs = self.next_rstd()
                self.sumsq_rstd([(pq.ap[0:64, :], pq.b, 64)], 64, rs, self.rtmp, m=64)
                if is_s:
                    self.stt(qf.ap[:, :], pq.ap[0:64, :], self.sp.ap[0:64, go:go + 1], rs.ap[0:64, :], ALU.mult, ALU.mult, [pq.b, self.sp.b, rs.b], [qf.b])
                    self.rope_apply(Tl(qf.ap[:, :], qf.b), qs.ap[:, dst0 + n_, :], qs.b, Tl(rt1.ap[:, :], rt1.b), Tl(rt2.ap[:, :], rt2.b))
                else:
                    self.stt(qs.ap[:, dst0 + n_, :], pq.ap[0:64, :], self.sp.ap[0:64, go:go + 1], rs.ap[0:64, :], ALU.mult, ALU.mult, [pq.b, self.sp.b, rs.b], [qs.b])
        if not is_s:
            for half in range(2):
                slab = self.wslab(self.rows(win, 0, D, O_SQ + half * 512, 512), 16, 512)
                q_heads(slab, list(range(8)), half * 8)
        if not is_s:
            PT = [self.alloc([128, 2, T], BF16, "sPT%d" % k) for k in range(2)]
            k = 0
            for hk in range(4):
                for bb in range(2):
                    for gp in range(2):
                        ps_ = [self.pps(), self.pps()]
                        for st in range(2):
                            kt_ = 2 * bb + st
                            self.mm(ps_[st].ap[:, :].rearrange("p (g t) -> p g t", g=2), self.swkb.ap[0:64, hk, kt_ * 128:(kt_ + 1) * 128],
                                    qs.ap[0:64, hk * 4 + gp * 2:hk * 4 + gp * 2 + 2, bb * 256:(bb + 1) * 256], True, True, [self.swkb.b, qs.b], [ps_[st].b])
                        pt_ = PT[k % 2]
                        k += 1
                        for st in range(2):
                            self.act(pt_.ap[:, st, :], ps_[st].ap[:, :], AF.Exp, [ps_[st].b], [pt_.b], scale=SWA_SCALE)
                        self.unpin(*ps_)
                        pO = self.pps()
                        pD = self.pps()
                        for st in range(2):
                            kt_ = 2 * bb + st
                            self.mm(pO.ap[0:64, :], self.swvb.ap[:, kt_, hk * 64:(hk + 1) * 64], pt_.ap[:, st, :], st == 0, st == 1, [self.swvb.b, pt_.b], [pO.b])
                            self.mm(pD.ap[0:64, :], self.onesb.ap[:, 0:64], pt_.ap[:, st, :], st == 0, st == 1, [self.onesb.b, pt_.b], [pD.b])
                        for g2 in range(2):
                            hd = hk * 4 + gp * 2 + g2
                            self.ts(rd.ap[:, g2 * 256:(g2 + 1) * 256], pD.ap[0:64, g2 * 256:(g2 + 1) * 256], self.esink.ap[0:64, hd:hd + 1], None, ALU.add, None, [pD.b, self.esink.b], [rd.b])
                        self.recip(rd.ap[:, :], rd.ap[:, :], [rd.b], [rd.b])
                        self.tt(oT_swa.ap[:, hk * 4 + gp * 2:hk * 4 + gp * 2 + 2, bb * 256:(bb + 1) * 256], pO.ap[0:64, :].rearrange("p (g t) -> p g t", g=2),
                                rd.ap[:, :].rearrange("p (g t) -> p g t", g=2), ALU.mult, [pO.b, rd.b], [oT_swa.b])
                        self.unpin(pO, pD)
        else:
            xo = self.xoutA[l]
            kall = self.alloc([64, 4, 2304], BF16, "kall")
            vall = self.alloc([128, 18, 256], BF16, "vall")
            msk = [self.alloc([128, 16, 128], BF16, "msk%d" % k) for k in range(2)]
            PT = [self.alloc([128, T], BF16, "sPT%d" % k) for k in range(2)]
            PM = [self.alloc([128, T], BF16, "sPM%d" % k) for k in range(2)]
            xr = xo.ap.rearrange("(r p) c -> p r c", p=128)
            for h_ in range(4):
                r0_, c0_ = X_SWK[h_]
                self.load(kall.ap[:, h_, 0:2048].rearrange("p (r t) -> p r t", r=4), xr[r0_:r0_ + 64, :, c0_:c0_ + T], [kall.b], [xo.b], q="pool")
            for a_ in range(4):
                self.load(vall.ap[:, 0:16, :].rearrange("p (r a) n -> p r a n", r=4)[:, :, a_, :], xr[:, :, X_SWV + a_ * 256:X_SWV + (a_ + 1) * 256], [vall.b], [xo.b], q="pool")
            self.load(kall.ap[:, :, 2048:2304], d["swk_cT"][l], [kall.b], q="pool")
            self.load(vall.ap[:, 16:18, :], d["swv_c"][l].rearrange("(a p) n -> p a n", p=128), [vall.b], q="pool")
            mdr = d["swamask"].rearrange("p (j i t) -> p j i t", j=16, i=4)
            own = self.tg == "own"
            k = 0
            mi = 0
            slab = None
            for hk in range(4):
                if hk % 2 == 0:
                    slab = self.wslab(self.rows(win, 0, D, O_SQ + (hk // 2) * 512, 512), 16, 512)
                q_heads(slab, [(hk % 2) * 4 + g for g in range(4)], 0)
                for i in range(4):
                    if own:
                        mk = msk[mi % 2]
                        mi += 1
                        self.load(mk.ap[:, :, :], mdr[:, :, i, :], [mk.b], q="pool")
                        jl = [(j, (mk.ap[:, j, :], mk.b) if j < 16 else None) for j in range(18)]
                    else:
                        Q = 4 * self.tg + i
                        jl = []
                        if Q - 1 >= 0:
                            jl.append((Q - 1, (self.maskLb.ap[:, :], self.maskLb.b)))
                        jl.append((Q, None))
                        if Q + 1 <= 15:
                            jl.append((Q + 1, (self.maskUb.ap[:, :], self.maskUb.b)))
                        jl += [(16, None), (17, None)]
                    pO = self.pps()
                    pD = self.pps()
                    for ji, (j, mk_) in enumerate(jl):
                        psc = self.nps()
                        self.mm(psc.ap[:, :].rearrange("p (g t) -> p g t", g=4), kall.ap[0:64, hk, j * 128:(j + 1) * 128], qs.ap[0:64, 0:4, i * 128:(i + 1) * 128], True, True, [kall.b, qs.b], [psc.b])
                        pt_ = PT[k % 2]
                        self.act(pt_.ap[:, :], psc.ap[:, :], AF.Exp, [psc.b], [pt_.b], scale=SWA_SCALE)
                        if mk_ is not None:
                            pm_ = PM[k % 2]
                            self.tt(pm_.ap[:, :].rearrange("p (g t) -> p g t", g=4), pt_.ap[:, :].rearrange("p (g t) -> p g t", g=4),
                                    mk_[0].unsqueeze(1).broadcast_to([128, 4, 128]), ALU.mult, [pt_.b, mk_[1]], [pm_.b])
                        else:
                            pm_ = pt_
                        k += 1
                        first, last_ = ji == 0, ji == len(jl) - 1
                        self.mm(pO.ap[0:64, :], vall.ap[:, j, hk * 64:(hk + 1) * 64], pm_.ap[:, :], first, last_, [vall.b, pm_.b], [pO.b])
                        self.mm(pD.ap[0:64, :], self.onesb.ap[:, 0:64], pm_.ap[:, :], first, last_, [self.onesb.b, pm_.b], [pD.b])
                    for g in range(4):
                        hd = hk * 4 + g
                        self.ts(rd.ap[:, g * 128:(g + 1) * 128], pD.ap[0:64, g * 128:(g + 1) * 128], self.esink.ap[0:64, hd:hd + 1], None, ALU.add, None, [pD.b, self.esink.b], [rd.b])
                    self.recip(rd.ap[:, :], rd.ap[:, :], [rd.b], [rd.b])
                    self.tt(oT_swa.ap[:, hk * 4:hk * 4 + 4, i * 128:(i + 1) * 128], pO.ap[0:64, :].rearrange("p (g t) -> p g t", g=4),
                            rd.ap[:, :].rearrange("p (g t) -> p g t", g=4), ALU.mult, [pO.b, rd.b], [oT_swa.b])
                    self.unpin(pO, pD)
        self.release(m1)

    def stage_merge(self, l, oT_mla, oT_gla, oT_swa):
        d = self.d
        win = d["w_in"][l]
        hT = self.hT
        xT = self.xT
        m1 = self.mark()
        merged = self.alloc([128, 8, T], F32, "merged", nb=8)
        mergedb = self.alloc([128, 16, T], BF16, "mergedb", nb=16)
        sig = [self.alloc([128, T], F32, "sig%d" % k) for k in range(2)]
        tm = [self.alloc([128, T], F32, "tm%d" % k) for k in range(2)]
        self.push_xslots(64)
        branches = [("w_br_mla", O_GM, oT_mla, 8, 128), ("w_br_gla", O_GG, oT_gla, 8, 128), ("w_br_swa", O_GS, oT_swa, 16, 64)]
        k = 0
        for half in range(2):
            for bi, (wn, og, oT, nk, kp) in enumerate(branches):
                for b2 in range(2):
                    blk = half * 2 + b2
                    sb_ = self.wslab(d[wn][l][:, blk * 512:(blk + 1) * 512].rearrange("(c p) n -> p c n", p=kp), nk, 512)
                    sg_ = self.wslab(self.rows(win, 0, D, og + blk * 512, 512), 16, 512)
                    for jj in range(4):
                        j = blk * 4 + jj
                        jl = b2 * 4 + jj
                        pb = self.nps()
                        pg = self.nps()
                        for kc in range(nk):
                            self.mm(pb.ap[:, :], sb_.ap[0:kp, kc, jj * 128:(jj + 1) * 128], oT.ap[0:kp, kc, :], kc == 0, kc == nk - 1, [sb_.b, oT.b], [pb.b])
                        for c in range(16):
                            self.mm(pg.ap[:, :], sg_.ap[:, c, jj * 128:(jj + 1) * 128], hT.ap[:, c, :], c == 0, c == 15, [sg_.b, hT.bs[c]], [pg.b])
                        s_ = sig[k % 2]
                        t_ = tm[k % 2]
                        k += 1
                        self.act(s_.ap[:, :], pg.ap[:, :], AF.Sigmoid, [pg.b], [s_.b])
                        if bi == 0:
                            self.tt(merged.ap[:, jl, :], s_.ap[:, :], pb.ap[:, :], ALU.mult, [s_.b, pb.b], [merged.bs[jl]])
                        else:
                            self.tt(t_.ap[:, :], s_.ap[:, :], pb.ap[:, :], ALU.mult, [s_.b, pb.b], [t_.b])
                            if bi == 1:
                                self.tt(merged.ap[:, jl, :], merged.ap[:, jl, :], t_.ap[:, :], ALU.add, [merged.bs[jl], t_.b], [merged.bs[jl]])
                            else:
                                self.tt(mergedb.ap[:, j, :], merged.ap[:, jl, :], t_.ap[:, :], ALU.add, [merged.bs[jl], t_.b], [mergedb.bs[j]])
        wo = d["w_out"][l]
        for blk in range(4):
            sw = self.wslab(self.rows(wo, 0, D, blk * 512, 512), 16, 512)
            for dj in range(4):
                ch = blk * 4 + dj
                po = self.nps()
                for j in range(16):
                    self.mm(po.ap[:, :], sw.ap[:, j, dj * 128:(dj + 1) * 128], mergedb.ap[:, j, :], j == 0, j == 15, [sw.b, mergedb.bs[j]], [po.b])
                self.stt(xT.ap[:, ch, :], po.ap[:, :], self.modG.ap[:, ch:ch + 1], xT.ap[:, ch, :], ALU.mult, ALU.add, [po.b, self.modG.b, xT.bs[ch]], [xT.bs[ch]])
        self.pop_xslots()
        self.release(m1)

    def load_x(self, src, c0=0, reads=()):
        xv = src.rearrange("(c p) t -> p c t", p=128)
        for c8 in range(2):
            self.load(self.xT.ap[:, c8 * 8:(c8 + 1) * 8, :], xv[:, c8 * 8:(c8 + 1) * 8, c0:c0 + T], self.xT.bs[c8 * 8:(c8 + 1) * 8], reads)

    def store_x(self, dst, c0=0, writes=()):
        dv = dst.rearrange("(c p) t -> p c t", p=128)
        for c8 in range(2):
            self.store(dv[:, c8 * 8:(c8 + 1) * 8, c0:c0 + T], self.xT.ap[:, c8 * 8:(c8 + 1) * 8, :], self.xT.bs[c8 * 8:(c8 + 1) * 8], writes)

    def load_xq(self, src, c0, dq, reads=()):
        xv = src.rearrange("(c p) t -> p c t", p=128)
        self.load(self.xT.ap[:, dq * 4:(dq + 1) * 4, :], xv[:, dq * 4:(dq + 1) * 4, c0:c0 + T], self.xT.bs[dq * 4:(dq + 1) * 4], reads)

    def store_xq(self, dst, c0, dq, writes=()):
        dv = dst.rearrange("(c p) t -> p c t", p=128)
        self.store(dv[:, dq * 4:(dq + 1) * 4, c0:c0 + T], self.xT.ap[:, dq * 4:(dq + 1) * 4, :], self.xT.bs[dq * 4:(dq + 1) * 4], writes)

    def run(self, stop=None):
        d = self.d
        self.consts()
        self.adaln()
        if self.do_p:
            self.load_x(d["xpT"])
            for l in range(self.n_layers):
                self.ffn(l, 1, 0)
                if stop == "ffn1":
                    break
                self.mixer(l, 0, False)
                if stop == "mixer":
                    break
                self.ffn(l, 2, 0)
            self.store_x(d["ypT"])
        if self.do_s:
            xs = self.xscr
            oo, _ = SP_OFF["osel"]
            for l in range(self.n_layers):
                last = (l == self.n_layers - 1)
                if last:
                    macc = self.mark()
                    acc = self.alloc([128, 16, T], F32, "xown", nb=16)
                def fetch(tg_):
                    if l == 0:
                        self.load_x(d["xsT"], tg_ * T)
                    else:
                        self.load_x(xs.ap, tg_ * T, [xs.bs[tg_]])
                for tg in range(4):
                    self.load(self.rope.ap[:, :, :], d["rope"][tg], [self.rope.b])
                    if tg == 0 or stop == "ffn1":
                        fetch(tg)
                    self.ffn(l, 1, 1)
                    if last:
                        sel = self.sp.ap[:, oo + tg:oo + tg + 1]
                        for c in range(16):
                            if tg == 0:
                                self.ts(acc.ap[:, c, :], self.xT.ap[:, c, :], sel, None, ALU.mult, None, [self.xT.bs[c], self.sp.b], [acc.bs[c]])
                            else:
                                self.stt(acc.ap[:, c, :], self.xT.ap[:, c, :], sel, acc.ap[:, c, :], ALU.mult, ALU.add, [self.xT.bs[c], self.sp.b, acc.bs[c]], [acc.bs[c]])
                    else:
                        self.store_x(xs.ap, tg * T, [xs.bs[tg]])
                    if stop == "ffn1":
                        continue
                    self.mixer(l, 1, True, 1, tg, after_norm=(lambda t_=tg: fetch(t_ + 1)) if tg < 3 else None)
                if not last:
                    for tg in range(4):
                        self.load(self.rope.ap[:, :, :], d["rope"][tg], [self.rope.b])
                        if tg == 0:
                            self.load_x(xs.ap, tg * T, [xs.bs[tg]])
                        self.mixer(l, 1, True, 2, tg)

                        def handoff(dq, t_=tg):
                            self.store_xq(xs.ap, t_ * T, dq, [xs.bs[t_]])
                            if t_ < 3:
                                self.load_xq(xs.ap, (t_ + 1) * T, dq, [xs.bs[t_ + 1]])
                        self.ffn(l, 2, 1, after_dq=handoff)
                else:
                    for c in range(16):
                        self.vcopy(self.xT.ap[:, c, :], acc.ap[:, c, :], [acc.bs[c]], [self.xT.bs[c]])
                    self.release(macc)
                    self.load(self.rope.ap[:, :, :], d["rope"][4], [self.rope.b])
                    if stop != "ffn1":
                        self.mixer(l, 1, True, 2, "own")
                        if stop != "mixer":
                            self.ffn(l, 2, 1)
                    self.store_x(d["ysT"])
        self.P.wait_all_dma("sp")
        return self.P.emit()


def _rope_tables(pos):
    half = 16
    inv = (10000.0 ** (-np.arange(half, dtype=np.float32) / half)).astype(np.float32)
    tab = np.zeros((64, 2, len(pos)), np.float32)
    for dd in range(64):
        comp = (pos // 64) if dd < 32 else (pos % 64)
        ang = comp.astype(np.float32) * inv[dd % 16]
        tab[dd, 0] = np.cos(ang).astype(np.float32)
        tab[dd, 1] = np.sin(ang).astype(np.float32)
    return tab


def _perm_T():
    pm = np.zeros((64, 64), np.float32)
    for dd in range(64):
        if dd % 32 < 16:
            pm[dd, dd + 16] = -1.0
        else:
            pm[dd, dd - 16] = 1.0
    return np.ascontiguousarray(pm.T)


def _swamask(q):
    m = np.zeros((128, 16, 4, 128), np.float32)
    s = np.arange(128)[:, None]
    t = np.arange(128)[None, :]
    for i in range(4):
        for j in range(16):
            dlt = j - (4 * q + i)
            if dlt == 0:
                m[:, j, i, :] = 1.0
            elif dlt == -1:
                m[:, j, i, :] = (t <= s)
            elif dlt == 1:
                m[:, j, i, :] = (s <= t)
    return m.reshape(128, -1)


def prep_core(inp, core):
    b, q = core // 4, core % 4
    f = lambda a: np.ascontiguousarray(np.asarray(a, dtype=np.float32))
    m = {}
    m["xpT"] = f(np.asarray(inp["x_prompt"])[2 * core:2 * core + 2].reshape(T, D).T)
    m["xsT"] = f(np.asarray(inp["x_sample"])[b].T)
    sp = np.zeros((128, SP_N), np.float32)

    def put(name, arr):
        o, w = SP_OFF[name]
        arr = np.asarray(arr, np.float32)
        assert arr.shape[1] == w, (name, arr.shape, w)
        sp[0:arr.shape[0], o:o + w] = arr
    put("ident", np.eye(128, dtype=np.float32))
    put("permT", _perm_T())
    s_ = np.arange(128)[:, None]
    t_ = np.arange(128)[None, :]
    put("maskU", (s_ <= t_).astype(np.float32))
    put("maskL", (s_ >= t_).astype(np.float32))
    cond = np.stack([np.asarray(inp["c_ctx"]), np.asarray(inp["c"])[b]], axis=1)
    put("cond", cond.reshape(16, 128, 2).transpose(1, 0, 2).reshape(128, 32))
    gs = np.zeros((128, 4, 8), np.float32)
    for tg_ in range(4):
        gs[:, tg_, tg_] = 1.0
        gs[:, tg_, 4 + tg_] = 1.0
    put("gsel", gs.reshape(128, 32))
    put("gsel_own", gs[:, q, :])
    os_ = np.zeros((128, 4), np.float32)
    os_[:, q] = 1.0
    put("osel", os_)
    col = lambda v: np.asarray(v, np.float32).reshape(-1, 1)
    for l in range(DEPTH):
        put("bada%d" % l, np.asarray(inp["b_ada"])[l].reshape(144, 128).T)
        put("gn%d" % l, np.concatenate([np.asarray(inp[k])[l].reshape(16, 128).T for k in ("g_norm1", "g_norm2", "g_norm3")], axis=1))
        put("gq%d" % l, np.asarray(inp["g_mla_q"])[l].reshape(4, 128).T)
        put("gkv%d" % l, np.asarray(inp["g_mla_kv"])[l].reshape(4, 128).T)
        put("gqn_n%d" % l, col(np.asarray(inp["g_mla_qn"])[l][:128]))
        put("gqn_r%d" % l, col(np.asarray(inp["g_mla_qn"])[l][128:]))
        put("gkn_n%d" % l, col(np.asarray(inp["g_mla_kn"])[l][:128]))
        put("gkn_r%d" % l, col(np.asarray(inp["g_mla_kn"])[l][128:]))
        put("ggo%d" % l, np.asarray(inp["g_gla_out"])[l].reshape(2, 128).T)
        put("gsq%d" % l, col(np.asarray(inp["g_swa_qn"])[l]))
        put("gsk%d" % l, col(np.asarray(inp["g_swa_kn"])[l]))
        put("sink%d" % l, np.broadcast_to(np.asarray(inp["swa_sink"])[l][None, :], (128, 16)))
    m["smallp"] = sp
    m["rope"] = np.stack([_rope_tables(q_ * T + np.arange(T)) for q_ in (0, 1, 2, 3, q)])
    m["wg"] = f(np.stack([np.asarray(inp["w_gla_gf"]), np.asarray(inp["w_gla_gb"])], axis=1).transpose(2, 0, 1, 3))
    m["bg"] = f(np.stack([np.asarray(inp["b_gla_gf"]), np.asarray(inp["b_gla_gb"])], axis=1)[None])
    m["swamask"] = _swamask(q)
    for l in range(DEPTH):
        m["w_ada%d" % l] = np.asarray(inp["w_ada"], dtype=np.float32)[l]
    for k in ("w_ff1_gu", "w_ff2_gu", "w_ff1_down", "w_ff2_down", "w_in", "w_mla_uq", "w_mla_ukv",
              "w_br_mla", "w_br_gla", "w_br_swa", "w_out"):
        m[k] = np.asarray(inp[k], dtype=np.float32)
    m["ckv_cT"] = f(np.asarray(inp["cache_mla_ckv"])[b].transpose(0, 2, 1))
    m["kpe_cT"] = f(np.asarray(inp["cache_mla_kpe"])[b].transpose(0, 2, 1))
    m["swk_cT"] = f(np.asarray(inp["cache_swa_k"])[b].transpose(0, 3, 2, 1))
    m["swv_c"] = f(np.asarray(inp["cache_swa_v"])[b].reshape(DEPTH, 256, 256))
    m["state"] = f(np.asarray(inp["state_gla"])[b])
    return m


_CACHE = {}


def _get_builder():
    if "b" not in _CACHE:
        b = Builder(do_s=True, do_p=True, n_layers=DEPTH, n_cores=8)
        b.run()
        _CACHE["b"] = b
    return _CACHE["b"]


def kernel(**inputs):
    b = _get_builder()
    shared = {}
    in_maps = []
    for core in range(8):
        m = prep_core(inputs, core)
        for k in list(m.keys()):
            if k.startswith("w_") and k != "wg":
                if k not in shared:
                    shared[k] = m[k]
                m[k] = shared[k]
        in_maps.append(m)
    res = run_bass_kernel_spmd(b.nc, in_maps, core_ids=list(range(8)))
    r = res.results
    B, S = 16, 256
    y_p = np.zeros((B, S, D), np.float32)
    y_s = np.zeros((2, 2048, D), np.float32)
    n_ckv = np.zeros((B, DEPTH, S, 512), np.float32)
    n_kpe = np.zeros((B, DEPTH, S, 64), np.float32)
    n_sk = np.zeros((B, DEPTH, S, 4, 64), np.float32)
    n_sv = np.zeros((B, DEPTH, S, 4, 64), np.float32)
    n_sg = np.zeros((B, DEPTH, 2, 4, 128, 256), np.float32)
    for core in range(8):
        o = r[core]
        bq, q = core // 4, core % 4
        y_p[2 * core:2 * core + 2] = np.asarray(o["ypT"]).T.reshape(2, S, D)
        y_s[bq, q * T:(q + 1) * T] = np.asarray(o["ysT"]).T
        for l in range(DEPTH):
            n_ckv[2 * core:2 * core + 2, l] = np.asarray(o["n_ckvT"])[l].T.reshape(2, S, 512)
            n_kpe[2 * core:2 * core + 2, l] = np.asarray(o["n_kpeT"])[l].T.reshape(2, S, 64)
            n_sk[2 * core:2 * core + 2, l] = np.asarray(o["n_skT"])[l].transpose(2, 0, 1).reshape(2, S, 4, 64)
            n_sv[2 * core:2 * core + 2, l] = np.asarray(o["n_sv"])[l].reshape(2, S, 4, 64)
            n_sg[2 * core:2 * core + 2, l] = np.asarray(o["n_sg"])[l]
    return (y_p, y_s, n_ckv, n_kpe, n_sk, n_sv, n_sg)
```

```python
import contextlib
import numpy as np
import concourse.bass as bass
import concourse.mybir as mybir
from concourse.bass_utils import run_bass_kernel_spmd

F32 = mybir.dt.float32
BF16 = mybir.dt.bfloat16
AF = mybir.ActivationFunctionType
ALU = mybir.AluOpType

D = 2048
NCH = 16
T = 512
DEPTH = 2
DFF = 5632
NFC = 44
EPS = 1e-6
IN_COLS = 11872
O_CQ, O_CKV, O_KPE, O_GQ, O_GK, O_GV, O_GOUT, O_GGF, O_GGB = 0, 512, 1024, 1088, 1600, 2112, 3136, 4160, 4176
O_SQ, O_SK, O_SV, O_GM, O_GG, O_GS = 4192, 5216, 5472, 5728, 7776, 9824
MLA_SCALE = 192 ** -0.5
SWA_SCALE = 64 ** -0.5
GLA_QS = 128 ** -0.5


class Buf:
    __slots__ = ("name", "w", "r")

    def __init__(self, name="", w=None):
        self.name = name
        self.w = w
        self.r = {}


class Chan:
    __slots__ = ("sem", "count", "name")

    def __init__(self, name):
        self.sem = None
        self.count = 0
        self.name = name


class Op:
    __slots__ = ("eng", "fn", "deps", "idx", "signal", "waits", "chan")

    def __init__(self, eng, fn, deps, idx, chan=None):
        self.eng = eng
        self.fn = fn
        self.deps = deps
        self.idx = idx
        self.signal = False
        self.waits = None
        self.chan = chan


ENGS = ("pe", "act", "dve", "pool", "sp")


class Prog:
    def __init__(self, nc):
        self.nc = nc
        self.ops = {e: [] for e in ENGS}
        self.chans = []

    def _collect(self, reads, writes):
        deps = {}
        for b in reads:
            if b.w is not None:
                k, v = b.w
                if deps.get(k, -1) < v:
                    deps[k] = v
        for b in writes:
            if b.w is not None:
                k, v = b.w
                if deps.get(k, -1) < v:
                    deps[k] = v
            for k, v in b.r.items():
                if deps.get(k, -1) < v:
                    deps[k] = v
        return deps

    def _commit(self, tok, reads, writes):
        k, v = tok
        for b in reads:
            if b.r.get(k, -1) < v:
                b.r[k] = v
        for b in writes:
            b.w = tok
            b.r = {}

    def op(self, eng, fn, reads=(), writes=()):
        deps = self._collect(reads, writes)
        idx = len(self.ops[eng])
        if eng == "pe":
            deps.pop(("e", "pe"), None)
        o = Op(eng, fn, deps, idx)
        self.ops[eng].append(o)
        self._commit((("e", eng), idx), reads, writes)
        return (("e", eng), idx)

    def new_chan(self, name):
        c = Chan(name)
        self.chans.append(c)
        return c

    def dma(self, q, chan, fn, reads=(), writes=()):
        deps = self._collect(reads, writes)
        if chan.count > 0:
            k = ("d", chan)
            if deps.get(k, -1) < chan.count:
                deps[k] = chan.count
        o = Op(q, fn, deps, len(self.ops[q]), chan=chan)
        self.ops[q].append(o)
        chan.count += 16
        self._commit((("d", chan), chan.count), reads, writes)

    def fence(self, fn, engs=("pe", "act", "dve")):
        deps = {("e", f): len(self.ops[f]) - 1 for f in engs if len(self.ops[f]) > 0}
        for c in self.chans:
            if c.count > 0:
                deps[("d", c)] = c.count
        idx = len(self.ops["dve"])
        o = Op("dve", fn, deps, idx)
        self.ops["dve"].append(o)
        return (("e", "dve"), idx)

    def wait_all_dma(self, eng="sp"):
        deps = {("d", c): c.count for c in self.chans if c.count > 0}
        self.ops[eng].append(Op(eng, None, deps, len(self.ops[eng])))

    def emit(self):
        nc = self.nc
        for e in ENGS:
            waited = {}
            for o in self.ops[e]:
                w = []
                for k, v in o.deps.items():
                    if waited.get(k, -1) >= v:
                        continue
                    waited[k] = v
                    w.append((k, v))
                    if k[0] == "e":
                        self.ops[k[1]][v].signal = True
                o.waits = w
        cnt = {}
        for e in ENGS:
            c = 0
            arr = []
            for o in self.ops[e]:
                if o.signal:
                    c += 1
                arr.append(c)
            cnt[e] = arr
        with contextlib.ExitStack() as st:
            esem = {e: st.enter_context(nc.semaphore("s_" + e)) for e in ENGS}
            for i, c in enumerate(self.chans):
                if c.count > 0:
                    c.sem = st.enter_context(nc.semaphore("d%d_%s" % (i, c.name)))
            block = st.enter_context(nc.Block())
            hw = {"pe": block.tensor, "act": block.scalar, "dve": block.vector,
                  "pool": block.gpsimd, "sp": block.sync}

            def run(e):
                def body(eng):
                    for o in self.ops[e]:
                        for k, v in o.waits:
                            if k[0] == "e":
                                eng.wait_ge(esem[k[1]], cnt[k[1]][v])
                            else:
                                eng.wait_ge(k[1].sem, v)
                        if o.fn is None:
                            continue
                        ins = o.fn(eng)
                        if o.chan is not None:
                            ins.then_inc(o.chan.sem, 16)
                        elif o.signal:
                            ins.then_inc(esem[e], 1)
                return body
            for e in ENGS:
                hw[e](run(e))
        return {e: len(self.ops[e]) for e in ENGS}


class Tl:
    __slots__ = ("ap", "b", "bs")

    def __init__(self, ap, b, bs=None):
        self.ap = ap
        self.b = b
        self.bs = bs


def sp_layout():
    off = {}
    c = [0]

    def add(name, w):
        off[name] = (c[0], w)
        c[0] += w
    add("ident", 128)
    add("permT", 64)
    add("maskU", 128)
    add("maskL", 128)
    add("cond", 32)
    add("gsel", 32)
    add("gsel_own", 8)
    add("osel", 4)
    for l in range(DEPTH):
        add("bada%d" % l, 144)
        add("gn%d" % l, 48)
        add("gq%d" % l, 4)
        add("gkv%d" % l, 4)
        add("gqn_n%d" % l, 1)
        add("gqn_r%d" % l, 1)
        add("gkn_n%d" % l, 1)
        add("gkn_r%d" % l, 1)
        add("ggo%d" % l, 2)
        add("gsq%d" % l, 1)
        add("gsk%d" % l, 1)
        add("sink%d" % l, 16)
    return off, c[0]


SP_OFF, SP_N = sp_layout()
ARENA_ELEMS = 56 * 1024
XROWS, XCOLS = 128, 4608
X_CKV, X_SWV = 0, 3584
X_KR = (0, 2048)
X_SWK = [(64, 2048), (0, 2560), (64, 2560), (0, 3072)]
X_SS = (64, 3072)
YCOLS = 8 + 2048


class Builder:
    def __init__(self, do_s=True, n_layers=DEPTH, dbg=None, n_cores=8, do_p=True):
        self.do_p = do_p
        self.rgroups = [[0, 1, 2, 3], [4, 5, 6, 7]] if n_cores == 8 else [[0, 1, 2, 3]]
        self.do_s = do_s
        self.n_layers = n_layers
        self.dbg = dbg
        nc = self.nc = bass.Bass("TRN2", target_bir_lowering=False)
        self.P = Prog(nc)
        P = self.P
        din = lambda name, shape: nc.dram_tensor(name, list(shape), F32, kind="ExternalInput").ap()
        dout = lambda name, shape: nc.dram_tensor(name, list(shape), F32, kind="ExternalOutput").ap()
        self.d = d = {}
        d["xpT"] = din("xpT", [D, T])
        d["xsT"] = din("xsT", [D, 4 * T])
        d["smallp"] = din("smallp", [128, SP_N])
        d["rope"] = din("rope", [5, 64, 2, T])
        d["wg"] = din("wg", [16, DEPTH, 2, 512])
        d["bg"] = din("bg", [1, DEPTH, 2, 512])
        d["swamask"] = din("swamask", [128, 16 * 4 * 128])
        d["w_ada"] = [din("w_ada%d" % l, [D, 9 * D]) for l in range(DEPTH)]
        for nm in ("w_ff1_gu", "w_ff2_gu"):
            d[nm] = din(nm, [DEPTH, D, 2 * DFF])
        for nm in ("w_ff1_down", "w_ff2_down"):
            d[nm] = din(nm, [DEPTH, DFF, D])
        d["w_in"] = din("w_in", [DEPTH, D, IN_COLS])
        d["w_mla_uq"] = din("w_mla_uq", [DEPTH, 512, 1536])
        d["w_mla_ukv"] = din("w_mla_ukv", [DEPTH, 512, 2048])
        for nm in ("w_br_mla", "w_br_gla", "w_br_swa"):
            d[nm] = din(nm, [DEPTH, 1024, D])
        d["w_out"] = din("w_out", [DEPTH, D, D])
        d["ckv_cT"] = din("ckv_cT", [DEPTH, 512, 256])
        d["kpe_cT"] = din("kpe_cT", [DEPTH, 64, 256])
        d["swk_cT"] = din("swk_cT", [DEPTH, 64, 4, 256])
        d["swv_c"] = din("swv_c", [DEPTH, 256, 256])
        d["state"] = din("state", [DEPTH, 2, 4, 128, 256])
        d["ypT"] = dout("ypT", [D, T])
        d["ysT"] = dout("ysT", [D, T])
        d["n_ckvT"] = dout("n_ckvT", [DEPTH, 512, T])
        d["n_kpeT"] = dout("n_kpeT", [DEPTH, 64, T])
        d["n_skT"] = dout("n_skT", [DEPTH, 4, 64, T])
        d["n_sv"] = dout("n_sv", [DEPTH, T, 256])
        d["n_sg"] = dout("n_sg", [DEPTH, 2, 2, 4, 128, 256])
        if dbg is not None:
            d["dbg"] = dout("dbg", list(dbg))
        self.xscr = Tl(nc.dram_tensor("xscr", [D, 4 * T], F32).ap(), None, [Buf("xscr%d" % k) for k in range(4)])
        self.xoutA = [Tl(nc.dram_tensor("xoutA%d" % l, [4 * XROWS, XCOLS], F32).ap(), Buf("xoutA")) for l in range(DEPTH)]
        self.xoutB = [Tl(nc.dram_tensor("xoutB%d" % l, [4 * 128, YCOLS], F32).ap(), Buf("xoutB")) for l in range(DEPTH)]

        def sb(name, shape, dt):
            return Tl(nc.alloc_sbuf_tensor(name, list(shape), dt), Buf(name))
        self.xT = sb("xT", [128, NCH, T], F32)
        self.xT.bs = [Buf("xT%d" % c) for c in range(NCH)]
        self.nslots = 3
        self.slots = [sb("slot%d" % i, [128, 8192], BF16) for i in range(self.nslots)]
        self.slot_ch = [P.new_chan("slot%d" % i) for i in range(self.nslots)]
        self.slot_chs = list(self.slot_ch)
        self.slot_i = 0
        self.xslot_ch = [P.new_chan("xslot%d" % i) for i in range(4)]
        self.sp = sb("smallp_s", [128, SP_N], F32)
        self.rope = sb("rope_s", [64, 2, T], F32)
        self.identb = sb("identb", [128, 128], BF16)
        self.onesb = sb("onesb", [128, 128], BF16)
        self.onesf = sb("onesf", [33, 128], F32)
        self.maskUb = sb("maskUb", [128, 128], BF16)
        self.maskLb = sb("maskLb", [128, 128], BF16)
        self.modT = sb("modT", [128, DEPTH, 144, 2], F32)
        self.modA = sb("modA", [128, 16], F32)
        self.modG = sb("modG", [128, 16], F32)
        self.esink = sb("esink", [128, 16], F32)
        self.fscr = sb("fscr", [128, 2], F32)
        self.arena_elems = (nc.sbuf_bytes_remaining - 512) // 2 // 64 * 64
        self.arena_t = nc.alloc_sbuf_tensor("arena", [128, self.arena_elems], BF16)
        self.a_off = 0
        self.a_tok = None
        self.ps = [Tl(nc.alloc_psum_tensor("ps%d" % i, [128, 512], F32), Buf("ps%d" % i)) for i in range(8)]
        self.ch_in = P.new_chan("in")
        self.ch_out = [P.new_chan("out%d" % i) for i in range(4)]
        self.out_i = 0
        self.ch_x = P.new_chan("xchg")
        self.ch_pl = P.new_chan("pload")
        self.pinned = set()

    def alloc(self, shape, dt, name="", nb=0):
        n = int(np.prod(shape[1:]))
        sz = n * (2 if dt == F32 else 1)
        sz = (sz + 15) // 16 * 16
        assert self.a_off + sz <= self.arena_elems, ("arena overflow", name, self.a_off, sz, self.arena_elems)
        self.a_peak = max(getattr(self, "a_peak", 0), self.a_off + sz)
        v = self.arena_t[0:shape[0], self.a_off:self.a_off + sz]
        self.a_off += sz
        if dt == F32:
            v = v.bitcast(F32)
        v = v[:, 0:n]
        if len(shape) == 3:
            v = v.rearrange("p (a b) -> p a b", a=shape[1])
        elif len(shape) == 4:
            v = v.rearrange("p (a b c) -> p a b c", a=shape[1], b=shape[2])
        return Tl(v, Buf(name, self.a_tok), [Buf(name + str(k), self.a_tok) for k in range(nb)] if nb else None)

    def mark(self):
        return self.a_off

    def release(self, m):
        fs = self.fscr
        self.a_tok = self.P.fence(lambda e: e.memset(fs.ap[:, 0:1], 0.0))
        self.a_off = m

    def sps(self, name):
        o, w = SP_OFF[name]
        return self.sp.ap[:, o:o + w]

    def mm(self, out, lhsT, rhs, start, stop, reads, writes):
        self.P.op("pe", lambda e: e.matmul(out, lhsT, rhs, start=start, stop=stop), reads, writes)

    def tr(self, out, in_, ident, reads, writes):
        self.P.op("pe", lambda e: e.transpose(out, in_, ident), reads, writes)

    def act(self, out, in_, func, reads, writes, bias=None, scale=None):
        kw = {}
        if bias is not None:
            kw["bias"] = bias
        if scale is not None:
            kw["scale"] = scale
        self.P.op("act", lambda e: e.activation(out=out, in_=in_, func=func, **kw), reads, writes)

    def tt(self, out, in0, in1, op, reads, writes):
        self.P.op("dve", lambda e: e.tensor_tensor(out=out, in0=in0, in1=in1, op=op), reads, writes)

    def stt(self, out, in0, scalar, in1, op0, op1, reads, writes):
        self.P.op("dve", lambda e: e.scalar_tensor_tensor(out=out, in0=in0, scalar=scalar, in1=in1, op0=op0, op1=op1), reads, writes)

    def ts(self, out, in0, s1, s2, op0, op1, reads, writes):
        if s2 is None:
            self.P.op("dve", lambda e: e.tensor_scalar(out=out, in0=in0, scalar1=s1, scalar2=None, op0=op0), reads, writes)
        else:
            self.P.op("dve", lambda e: e.tensor_scalar(out=out, in0=in0, scalar1=s1, scalar2=s2, op0=op0, op1=op1), reads, writes)

    def recip(self, out, in_, reads, writes):
        self.P.op("dve", lambda e: e.reciprocal(out=out, in_=in_), reads, writes)

    def vcopy(self, out, in_, reads, writes):
        self.P.op("dve", lambda e: e.tensor_copy(out=out, in_=in_), reads, writes)

    def load(self, out, in_, writes, reads=(), q="sp", ch=None):
        if ch is None:
            ch = self.ch_in if q == "sp" else self.ch_pl
        self.P.dma(q, ch, lambda e: e.dma_start(out=out, in_=in_), reads, writes)

    def store(self, out, in_, reads, writes=()):
        ch = self.ch_out[self.out_i % len(self.ch_out)]
        self.out_i += 1
        self.P.dma("sp", ch, lambda e: e.dma_start(out=out, in_=in_), reads, writes)

    def wslab(self, src, kc, n):
        s = self.slot_i % len(self.slots)
        self.slot_i += 1
        sl = self.slots[s]
        ch_ = self.slot_chs[s]
        if kc is None:
            view = sl.ap[0:src.shape[0], 0:n]
        else:
            view = sl.ap[0:src.shape[0], 0:kc * n].rearrange("p (k n) -> p k n", k=kc)
        self.P.dma("pool", ch_, lambda e: e.dma_start(out=view, in_=src), (), [sl.b])
        return Tl(view, sl.b)

    def push_xslots(self, reserve):
        free = self.arena_elems - self.a_off
        n_extra = max(0, min(len(self.xslot_ch), (free - reserve) // 8192))
        for k_ in range(n_extra):
            self.slots.append(self.alloc([128, 8192], BF16, "xslot%d" % k_))
            self.slot_chs.append(self.xslot_ch[k_])

    def pop_xslots(self):
        del self.slots[self.nslots:]
        del self.slot_chs[self.nslots:]

    def rows(self, w2d, r0, nrow, c0, ncol, p=128):
        return w2d[r0:r0 + nrow, c0:c0 + ncol].rearrange("(c p) n -> p c n", p=p)

    def consts(self):
        d = self.d
        self.load(self.sp.ap[:, :], d["smallp"], [self.sp.b])
        o, _ = SP_OFF["ident"]
        self.identf = self.sp.ap[:, o:o + 128]
        self.vcopy(self.identb.ap[:, :], self.identf, [self.sp.b], [self.identb.b])
        self.vcopy(self.maskUb.ap[:, :], self.sps("maskU"), [self.sp.b], [self.maskUb.b])
        self.vcopy(self.maskLb.ap[:, :], self.sps("maskL"), [self.sp.b], [self.maskLb.b])
        ob, of = self.onesb, self.onesf
        self.P.op("dve", lambda e: e.memset(ob.ap[:, :], 1.0), (), [ob.b])
        self.P.op("dve", lambda e: e.memset(of.ap[:, :], 1.0), (), [of.b])

    def adaln(self):
        d = self.d
        m0 = self.mark()
        self.push_xslots(4 * 1024)
        scT = self.alloc([128, 16, 2], BF16, "scT")
        st = [self.alloc([2, 512], F32, "adast%d" % i) for i in range(2)]
        cond = self.sps("cond").rearrange("p (c j) -> p c j", c=16)
        self.act(scT.ap[:, :, :], cond, AF.Silu, [self.sp.b], [scT.b])
        psM = self.ps[7]
        for l in range(self.n_layers):
            wv = d["w_ada"][l].rearrange("(c p) n -> p c n", p=128)
            for sbk in range(36):
                slab = self.wslab(wv[:, :, sbk * 512:(sbk + 1) * 512], 16, 512)
                pa = self.ps[sbk % 2]
                for c in range(16):
                    self.mm(pa.ap[0:2, :], scT.ap[:, c, :], slab.ap[:, c, :], c == 0, c == 15, [scT.b, slab.b], [pa.b])
                s_ = st[sbk % 2]
                self.act(s_.ap[:, :], pa.ap[0:2, :], AF.Copy, [pa.b], [s_.b])
                for jj in range(4):
                    j = sbk * 4 + jj
                    self.tr(psM.ap[:, 2 * j:2 * j + 2], s_.ap[0:2, jj * 128:(jj + 1) * 128], self.identf[0:2, 0:2], [s_.b, self.sp.b], [psM.b])
            bo, _ = SP_OFF["bada%d" % l]
            bias = self.sp.ap[:, bo:bo + 144].unsqueeze(2).broadcast_to([128, 144, 2])
            self.tt(self.modT.ap[:, l, :, :], psM.ap[:, 0:288].rearrange("p (j c) -> p j c", c=2), bias, ALU.add, [psM.b, self.sp.b], [self.modT.b])
        self.pop_xslots()
        self.release(m0)

    def mod_setup(self, l, i, cond, half_gate):
        go, _ = SP_OFF["gn%d" % l]
        g = self.sp.ap[:, go + 16 * i:go + 16 * i + 16]
        sc = self.modT.ap[:, l, (3 * i + 1) * 16:(3 * i + 2) * 16, cond]
        gt = self.modT.ap[:, l, (3 * i + 2) * 16:(3 * i + 3) * 16, cond]
        self.stt(self.modA.ap[:, :], sc, 1.0, g, ALU.add, ALU.mult, [self.modT.b, self.sp.b], [self.modA.b])
        self.ts(self.modG.ap[:, :], gt, 0.5 if half_gate else 1.0, None, ALU.mult, None, [self.modT.b], [self.modG.b])

    def rstd_bc(self, ss_ps, n, width, tmp, out, reads):
        self.act(tmp.ap, ss_ps, AF.Sqrt, reads, [tmp.b], bias=EPS, scale=1.0 / n)
        self.recip(out.ap, tmp.ap, [tmp.b], [out.b])

    def norm_mod(self, l, i, cond, hT):
        m0 = self.mark()
        sq = [self.alloc([128, T], BF16, "sq%d" % k) for k in range(2)]
        tmp = self.alloc([128, T], F32, "nm_tmp")
        rstd = self.alloc([128, T], F32, "nm_rstd")
        tf = [self.alloc([128, T], F32, "nm_t%d" % k) for k in range(2)]
        xT = self.xT
        ssp = self.ps[6]
        for c in range(16):
            s_ = sq[c % 2]
            self.act(s_.ap[:, :], xT.ap[:, c, :], AF.Square, [xT.bs[c]], [s_.b])
            self.mm(ssp.ap[:, :], self.onesb.ap[:, :], s_.ap[:, :], c == 0, c == 15, [self.onesb.b, s_.b], [ssp.b])
        self.rstd_bc(ssp.ap[:, :], D, T, Tl(tmp.ap[:, :], tmp.b), Tl(rstd.ap[:, :], rstd.b), [ssp.b])
        for c in range(16):
            t_ = tf[c % 2]
            self.stt(t_.ap[:, :], xT.ap[:, c, :], self.modA.ap[:, c:c + 1], rstd.ap[:, :], ALU.mult, ALU.mult, [xT.bs[c], self.modA.b, rstd.b], [t_.b])
            self.act(hT.ap[:, c, :], t_.ap[:, :], AF.Identity, [t_.b, self.modT.b], [hT.bs[c]], bias=self.modT.ap[:, l, 3 * i * 16 + c, cond:cond + 1])
        self.release(m0)

    def ffn(self, l, which, cond, after_dq=None):
        d = self.d
        wgu = d["w_ff%d_gu" % which][l]
        wdn = d["w_ff%d_down" % which][l]
        i = 0 if which == 1 else 2
        m0 = self.mark()
        self.push_xslots(33 * 1024)
        hT = self.alloc([128, 16, T], BF16, "hT", nb=16)
        self.mod_setup(l, i, cond, True)
        self.norm_mod(l, i, cond, hT)
        actT = self.alloc([128, NFC, T], BF16, "actT", nb=NFC)
        sil = [self.alloc([128, T], F32, "sil%d" % k) for k in range(2)]
        k = 0
        for fb in range(11):
            sa = self.wslab(self.rows(wgu, 0, D, fb * 512, 512), 16, 512)
            su = self.wslab(self.rows(wgu, 0, D, DFF + fb * 512, 512), 16, 512)
            for j in range(4):
                fc = fb * 4 + j
                pa = self.ps[(k % 2) * 2]
                pu = self.ps[(k % 2) * 2 + 1]
                for c in range(16):
                    self.mm(pa.ap[:, :], sa.ap[:, c, j * 128:(j + 1) * 128], hT.ap[:, c, :], c == 0, c == 15, [sa.b, hT.bs[c]], [pa.b])
                for c in range(16):
                    self.mm(pu.ap[:, :], su.ap[:, c, j * 128:(j + 1) * 128], hT.ap[:, c, :], c == 0, c == 15, [su.b, hT.bs[c]], [pu.b])
                s_ = sil[k % 2]
                self.act(s_.ap[:, :], pa.ap[:, :], AF.Silu, [pa.b], [s_.b])
                self.tt(actT.ap[:, fc, :], s_.ap[:, :], pu.ap[:, :], ALU.mult, [s_.b, pu.b], [actT.bs[fc]])
                k += 1
        xT = self.xT
        for dq in range(4):
            for sl in range(3):
                nk = 16 if sl < 2 else 12
                sw = self.wslab(self.rows(wdn, sl * 2048, nk * 128, dq * 512, 512), nk, 512)
                for kc in range(nk):
                    fc = sl * 16 + kc
                    for dj in range(4):
                        po = self.ps[4 + dj]
                        self.mm(po.ap[:, :], sw.ap[:, kc, dj * 128:(dj + 1) * 128], actT.ap[:, fc, :], fc == 0, fc == NFC - 1, [sw.b, actT.bs[fc]], [po.b])
            for dj in range(4):
                ch = dq * 4 + dj
                po = self.ps[4 + dj]
                self.stt(xT.ap[:, ch, :], po.ap[:, :], self.modG.ap[:, ch:ch + 1], xT.ap[:, ch, :], ALU.mult, ALU.add, [po.b, self.modG.b, xT.bs[ch]], [xT.bs[ch]])
            if after_dq is not None:
                after_dq(dq)
        self.pop_xslots()
        self.release(m0)

    def nps(self):
        while True:
            self._psi = getattr(self, "_psi", 0) + 1
            if (self._psi % 8) not in self.pinned:
                return self.ps[self._psi % 8]

    def pps(self):
        t = self.nps()
        self.pinned.add(self._psi % 8)
        return t

    def unpin(self, *tls):
        for t in tls:
            self.pinned.discard(self.ps.index(t))

    def rope_apply(self, x, out_ap, out_b, tmp1, tmp2):
        pp = self.nps()
        po, _ = SP_OFF["permT"]
        self.mm(pp.ap[0:64, :], self.sp.ap[0:64, po:po + 64], x.ap, True, True, [self.sp.b, x.b], [pp.b])
        self.tt(tmp1.ap, x.ap, self.rope.ap[:, 0, :], ALU.mult, [x.b, self.rope.b], [tmp1.b])
        self.tt(tmp2.ap, pp.ap[0:64, :], self.rope.ap[:, 1, :], ALU.mult, [pp.b, self.rope.b], [tmp2.b])
        self.tt(out_ap, tmp1.ap, tmp2.ap, ALU.add, [tmp1.b, tmp2.b], [out_b])

    def sumsq_rstd(self, parts_list, n, rstd, tmp, width=T, extra=None, m=128):
        ss = self.nps()
        nmm = len(parts_list) + (len(extra) if extra else 0)
        k = 0
        for (src, sb_, npart) in parts_list:
            sq = self.sqt[self._sqi % 2]
            self._sqi += 1
            self.act(sq.ap[0:npart, 0:width], src, AF.Square, [sb_], [sq.b])
            self.mm(ss.ap[0:m, 0:width], self.onesb.ap[0:npart, 0:m], sq.ap[0:npart, 0:width], k == 0, k == nmm - 1, [self.onesb.b, sq.b], [ss.b])
            k += 1
        if extra:
            for (lh, rh, rd) in extra:
                self.mm(ss.ap[0:m, 0:width], lh, rh, k == 0, k == nmm - 1, rd, [ss.b])
                k += 1
        self.act(tmp.ap[0:m, 0:width], ss.ap[0:m, 0:width], AF.Sqrt, [ss.b], [tmp.b], bias=EPS, scale=1.0 / n)
        self.recip(rstd.ap[0:m, 0:width], tmp.ap[0:m, 0:width], [tmp.b], [rstd.b])

    def mixer(self, l, cond, is_s, phase=0, tg=0, after_norm=None):
        self.tg = tg
        m0 = self.mark()
        hT = self.alloc([128, 16, T], BF16, "hT", nb=16)
        self.hT = hT
        self.mod_setup(l, 1, cond, False)
        self.norm_mod(l, 1, cond, hT)
        if after_norm is not None:
            after_norm()
        oT_gla = None if (is_s and phase == 1) else self.alloc([128, 8, T], BF16, "oT_gla")
        self.sqt = [self.alloc([128, T], BF16, "sqt%d" % k) for k in range(2)]
        self._sqi = 0
        self.rtmp = self.alloc([128, T], F32, "rtmp")
        self.rstd = [self.alloc([128, T], F32, "rstd%d" % k) for k in range(2)]
        self._ri = 0
        if is_s and phase == 1:
            m1 = self.mark()
            self.stage_kv(l, True)
            self.release(m1)
            self.stage_gla(l, True, True, oT_gla)
            self.release(m0)
            return
        if not is_s:
            self.stage_gla(l, False, False, oT_gla)
            self.stage_kv(l, False)
        oT_mla = self.alloc([128, 8, T], BF16, "oT_mla")
        self.stage_mla(l, is_s, oT_mla)
        if is_s:
            self.stage_gla(l, True, False, oT_gla)
        oT_swa = self.alloc([64, 16, T], BF16, "oT_swa")
        self.stage_swa(l, is_s, oT_swa)
        self.stage_merge(l, oT_mla, oT_gla, oT_swa)
        self.release(m0)

    def next_rstd(self):
        self._ri += 1
        return self.rstd[self._ri % 2]

    def stage_kv(self, l, is_s):
        d = self.d
        win = d["w_in"][l]
        hT = self.hT
        if not is_s:
            self.ckvb = self.alloc([128, 4, T], BF16, "ckvb")
            self.kpef = self.alloc([64, T], F32, "kpef")
            self.sqk = self.alloc([64, T], BF16, "sqk")
            self.swkb = self.alloc([64, 4, T], BF16, "swkb")
            self.swvb = self.alloc([128, 4, 256], BF16, "swvb")
            kpef = self.kpef
        else:
            kpef = self.alloc([64, T], F32, "kpef")
            sqk = self.alloc([64, T], BF16, "sqk_s")
        m1 = self.mark()
        ckvf = self.alloc([128, 4, T], F32, "ckvf")
        skn = [self.alloc([64, T], F32, "skn%d" % k) for k in range(2)]
        svf = self.alloc([128, 4, 256], F32, "svf")
        rt1 = self.alloc([64, T], F32, "rt1")
        rt2 = self.alloc([64, T], F32, "rt2")
        rt3 = self.alloc([64, T], F32, "rt3")
        xo_ = self.xoutA[l]
        xin = Tl(xo_.ap[self.tg * 128:(self.tg + 1) * 128, :], xo_.b)
        slab = self.wslab(self.rows(win, 0, D, O_CKV, 512), 16, 512)
        pcs = [self.pps() for k in range(4)]
        for c4 in range(4):
            for c in range(16):
                self.mm(pcs[c4].ap[:, :], slab.ap[:, c, c4 * 128:(c4 + 1) * 128], hT.ap[:, c, :], c == 0, c == 15, [slab.b, hT.bs[c]], [pcs[c4].b])
        rs = self.next_rstd()
        self.sumsq_rstd([(pcs[c4].ap[:, :], pcs[c4].b, 128) for c4 in range(4)], 512, rs, self.rtmp)
        go, _ = SP_OFF["gkv%d" % l]
        for c4 in range(4):
            self.stt(ckvf.ap[:, c4, :], pcs[c4].ap[:, :], self.sp.ap[:, go + c4:go + c4 + 1], rs.ap[:, :], ALU.mult, ALU.mult, [pcs[c4].b, self.sp.b, rs.b], [ckvf.b])
        if not is_s:
            self.store(d["n_ckvT"][l].rearrange("(c p) t -> p c t", p=128), ckvf.ap[:, :, :], [ckvf.b])
            self.act(self.ckvb.ap[:, :, :], ckvf.ap[:, :, :], AF.Copy, [ckvf.b], [self.ckvb.b])
        else:
            self.store(xin.ap[:, X_CKV:X_CKV + 2048].rearrange("p (c t) -> p c t", c=4), ckvf.ap[:, :, :], [ckvf.b], [xin.b])
        self.unpin(*pcs)
        slab = self.wslab(self.rows(win, 0, D, O_KPE, 64), 16, 64)
        pk = self.nps()
        for c in range(16):
            self.mm(pk.ap[0:64, :], slab.ap[:, c, :], hT.ap[:, c, :], c == 0, c == 15, [slab.b, hT.bs[c]], [pk.b])
        self.act(kpef.ap[:, :], pk.ap[0:64, :], AF.Copy, [pk.b], [kpef.b])
        if not is_s:
            self.store(d["n_kpeT"][l], kpef.ap[:, :], [kpef.b])
            self.act(self.sqk.ap[:, :], kpef.ap[:, :], AF.Square, [kpef.b], [self.sqk.b])
        else:
            self.act(sqk.ap[:, :], kpef.ap[:, :], AF.Square, [kpef.b], [sqk.b])
            pr = self.nps()
            self.mm(pr.ap[0:1, :], self.onesb.ap[0:64, 0:1], sqk.ap[:, :], True, True, [self.onesb.b, sqk.b], [pr.b])
            self.act(rt3.ap[0:1, :], pr.ap[0:1, :], AF.Copy, [pr.b], [rt3.b])
            self.store(xin.ap[X_SS[0]:X_SS[0] + 1, X_SS[1]:X_SS[1] + T], rt3.ap[0:1, :], [rt3.b], [xin.b])
            self.P.op("dve", lambda e: e.memset(rt1.ap[:, :], 0.0), (), [rt1.b])
            self.store(xin.ap[X_SS[0] + 1:128, X_SS[1]:X_SS[1] + T], rt1.ap[0:63, :], [rt1.b], [xin.b])
            go, _ = SP_OFF["gkn_r%d" % l]
            kg = skn[0]
            self.ts(kg.ap[:, :], kpef.ap[:, :], self.sp.ap[0:64, go:go + 1], None, ALU.mult, None, [kpef.b, self.sp.b], [kg.b])
            self.rope_apply(Tl(kg.ap[:, :], kg.b), skn[1].ap[:, :], skn[1].b, Tl(rt1.ap[:, :], rt1.b), Tl(rt2.ap[:, :], rt2.b))
            self.store(xin.ap[X_KR[0]:X_KR[0] + 64, X_KR[1]:X_KR[1] + T], skn[1].ap[:, :], [skn[1].b], [xin.b])
        slab = self.wslab(self.rows(win, 0, D, O_SK, 512), 16, 512)
        go, _ = SP_OFF["gsk%d" % l]
        for hk in range(4):
            pk = self.nps()
            for c in range(16):
                self.mm(pk.ap[0:64, :], slab.ap[:, c, hk * 64:(hk + 1) * 64], hT.ap[:, c, :], c == 0, c == 15, [slab.b, hT.bs[c]], [pk.b])
            rs = self.next_rstd()
            self.sumsq_rstd([(pk.ap[0:64, :], pk.b, 64)], 64, rs, self.rtmp, m=64)
            sk_ = skn[hk % 2]
            self.stt(sk_.ap[:, :], pk.ap[0:64, :], self.sp.ap[0:64, go:go + 1], rs.ap[0:64, :], ALU.mult, ALU.mult, [pk.b, self.sp.b, rs.b], [sk_.b])
            if not is_s:
                self.store(d["n_skT"][l, hk], sk_.ap[:, :], [sk_.b])
                self.act(self.swkb.ap[:, hk, :], sk_.ap[:, :], AF.Copy, [sk_.b], [self.swkb.b])
            else:
                self.rope_apply(Tl(sk_.ap[:, :], sk_.b), rt3.ap[:, :], rt3.b, Tl(rt1.ap[:, :], rt1.b), Tl(rt2.ap[:, :], rt2.b))
                self.store(xin.ap[X_SWK[hk][0]:X_SWK[hk][0] + 64, X_SWK[hk][1]:X_SWK[hk][1] + T], rt3.ap[:, :], [rt3.b], [xin.b])
        for st in range(4):
            pv = self.nps()
            for c in range(16):
                self.mm(pv.ap[:, 0:256], hT.ap[:, c, st * 128:(st + 1) * 128], slab.ap[:, c, 256:512], c == 0, c == 15, [slab.b, hT.bs[c]], [pv.b])
            self.act(svf.ap[:, st, :], pv.ap[:, 0:256], AF.Copy, [pv.b], [svf.b])
        if not is_s:
            self.store(d["n_sv"][l].rearrange("(a p) n -> p a n", p=128), svf.ap[:, :, :], [svf.b])
            self.vcopy(self.swvb.ap[:, :, :], svf.ap[:, :, :], [svf.b], [self.swvb.b])
        else:
            self.store(xin.ap[:, X_SWV:X_SWV + 1024].rearrange("p (a n) -> p a n", a=4), svf.ap[:, :, :], [svf.b], [xin.b])
        if not is_s:
            self.release(m1)

    def stage_gla(self, l, is_s, state_only, oT_gla):
        d = self.d
        win = d["w_in"][l]
        hT = self.hT
        full = not state_only
        m1 = self.mark()
        if full:
            oT = self.alloc([128, 8, T], F32, "oTg")
        Sf = self.alloc([128, 2, 4, 256], F32, "Sf")
        Sb = self.alloc([128, 2, 4, 256], BF16, "Sb")
        tS = self.alloc([128, 4, 256], F32, "tS")
        if is_s and full:
            mi = self.mark()
            self.gla_init_states(l, Sf, Sb, tS)
            self.release(mi)
        m2 = self.mark()
        kg = self.alloc([128, 4, T], BF16, "kg")
        vtm = self.alloc([128, 4, 1024], BF16, "vtm")
        la = self.alloc([128, 4, 512], F32, "la")
        ggT = self.alloc([17, 2, T], F32, "ggT")
        wgt = self.alloc([17, 2, 512], F32, "wgt")
        nbuf = 2 if state_only else 1
        ebs = [self.alloc([128, 4, 128], F32, "eb%d" % k_) for k_ in range(nbuf)]
        enbs = [self.alloc([128, 4, 128], F32, "enb%d" % k_) for k_ in range(nbuf)]
        kts = [self.alloc([128, 4, 128], BF16, "kt%d" % k_) for k_ in range(nbuf)]
        ktms = [self.alloc([128, 4, 128], BF16, "ktm%d" % k_) for k_ in range(nbuf)]
        it_ = 0
        cum = self.alloc([128, 2, 128], F32, "cum")
        PA = self.alloc([128, 2, 4], F32, "PA")
        if full:
            qg = self.alloc([128, 4, T], BF16, "qg")
            qt = self.alloc([128, 4, 128], BF16, "qt")
            ATs = self.alloc([128, 4, 128], BF16, "ATs")
        self.ts(cum.ap[:, 0, :], self.sps("maskU"), 1.0 / 16, None, ALU.mult, None, [self.sp.b], [cum.b])
        self.ts(cum.ap[:, 1, :], self.sps("maskL"), 1.0 / 16, None, ALU.mult, None, [self.sp.b], [cum.b])
        self.load(wgt.ap[0:16, :, :], d["wg"][:, l, :, :], [wgt.b])
        self.load(wgt.ap[16:17, :, :], d["bg"][:, l, :, :], [wgt.b])
        self.P.op("dve", lambda e: e.memset(ggT.ap[:, :, :], 1.0), (), [ggT.b])
        if full:
            slab = self.wslab(self.rows(win, 0, D, O_GQ, 512), 16, 512)
            for h in range(4):
                pq = self.nps()
                for c in range(16):
                    self.mm(pq.ap[:, :], slab.ap[:, c, h * 128:(h + 1) * 128], hT.ap[:, c, :], c == 0, c == 15, [slab.b, hT.bs[c]], [pq.b])
                self.ts(qg.ap[:, h, :], pq.ap[:, :], GLA_QS, None, ALU.mult, None, [pq.b], [qg.b])
        slab = self.wslab(self.rows(win, 0, D, O_GK, 512), 16, 512)
        for h in range(4):
            pq = self.nps()
            for c in range(16):
                self.mm(pq.ap[:, :], slab.ap[:, c, h * 128:(h + 1) * 128], hT.ap[:, c, :], c == 0, c == 15, [slab.b, hT.bs[c]], [pq.b])
            self.vcopy(kg.ap[:, h, :], pq.ap[:, :], [pq.b], [kg.b])
        for half in range(2):
            slab = self.wslab(self.rows(win, 0, D, O_GV + half * 512, 512), 16, 512)
            for st in range(4):
                pv = self.nps()
                for c in range(16):
                    self.mm(pv.ap[:, :], hT.ap[:, c, st * 128:(st + 1) * 128], slab.ap[:, c, :], c == 0, c == 15, [slab.b, hT.bs[c]], [pv.b])
                if st % 2 == 0:
                    self.act(vtm.ap[:, st, half * 512:(half + 1) * 512], pv.ap[:, :], AF.Copy, [pv.b], [vtm.b])
                else:
                    self.vcopy(vtm.ap[:, st, half * 512:(half + 1) * 512], pv.ap[:, :], [pv.b], [vtm.b])
        slab = self.wslab(self.rows(win, 0, D, O_GGF, 32), 16, 32)
        for dr in range(2):
            pg = self.nps()
            for c in range(16):
                self.mm(pg.ap[0:16, :], slab.ap[:, c, dr * 16:(dr + 1) * 16], hT.ap[:, c, :], c == 0, c == 15, [slab.b, hT.bs[c]], [pg.b])
            self.vcopy(ggT.ap[0:16, dr, :], pg.ap[0:16, :], [pg.b], [ggT.b])
        seqs = [[0, 1, 2, 3]] if is_s else [[0, 1], [2, 3]]
        first_o = [True] * 4
        st2 = ""
        if st2 == "proj":
            self.release(m1)
            return
        for dr in range(2):
            for st in range(4):
                pl = self.nps()
                self.mm(pl.ap[:, :], ggT.ap[0:17, dr, st * 128:(st + 1) * 128], wgt.ap[0:17, dr, :], True, True, [ggT.b, wgt.b], [pl.b])
                self.act(la.ap[:, st, :], pl.ap[:, :], AF.Sigmoid, [pl.b], [la.b])
            self.act(la.ap[:, :, :], la.ap[:, :, :], AF.Ln, [la.b], [la.b])
            if st2 == "la":
                self.release(m1)
                return
            for si, seq in enumerate(seqs):
                order = seq if dr == 0 else seq[::-1]
                zero_state = not (is_s and full)
                for n in order:
                    eb, enb, kt, ktm = ebs[it_ % nbuf], enbs[it_ % nbuf], kts[it_ % nbuf], ktms[it_ % nbuf]
                    it_ += 1
                    csl = slice(n * 128, (n + 1) * 128)
                    pb = self.nps()
                    for h in range(4):
                        self.mm(pb.ap[:, h * 128:(h + 1) * 128], la.ap[:, n, h * 128:(h + 1) * 128], cum.ap[:, dr, :], True, True, [la.b, cum.b], [pb.b])
                    pbv = pb.ap[:, :].rearrange("p (h t) -> p h t", h=4)
                    self.act(enb.ap[:, :, :], pbv, AF.Exp, [pb.b], [enb.b], scale=-1.0)
                    self.act(eb.ap[:, :, :], pbv, AF.Exp, [pb.b], [eb.b])
                    ebl = eb.ap[:, :, 127] if dr == 0 else eb.ap[:, :, 0]
                    self.tt(kt.ap[:, :, :], kg.ap[:, :, csl], enb.ap[:, :, :], ALU.mult, [kg.b, enb.b], [kt.b])
                    if full:
                        self.tt(qt.ap[:, :, :], qg.ap[:, :, csl], eb.ap[:, :, :], ALU.mult, [qg.b, eb.b], [qt.b])
                        pa = self.nps()
                        for h in range(4):
                            self.mm(pa.ap[:, h * 128:(h + 1) * 128], kt.ap[:, h, :], qt.ap[:, h, :], True, True, [kt.b, qt.b], [pa.b])
                        mk = self.maskUb if dr == 0 else self.maskLb
                        self.tt(ATs.ap[:, :, :], pa.ap[:, :].rearrange("p (h t) -> p h t", h=4), mk.ap[:, :].unsqueeze(1).broadcast_to([128, 4, 128]), ALU.mult, [pa.b, mk.b], [ATs.b])
                        po = [self.pps(), self.pps()]
                        for h in range(4):
                            for vc in range(2):
                                j = h * 2 + vc
                                dst = po[j // 4].ap[:, (j % 4) * 128:(j % 4 + 1) * 128]
                                self.mm(dst, vtm.ap[:, n, h * 256 + vc * 128:h * 256 + (vc + 1) * 128], ATs.ap[:, h, :], True, zero_state, [vtm.b, ATs.b], [po[j // 4].b])
                                if not zero_state:
                                    self.mm(dst, Sb.ap[:, dr, h, vc * 128:(vc + 1) * 128], qt.ap[:, h, :], False, True, [Sb.b, qt.b], [po[j // 4].b])
                        for hf in range(2):
                            ov = oT.ap[:, hf * 4:(hf + 1) * 4, csl]
                            pv_ = po[hf].ap[:, :].rearrange("p (j t) -> p j t", j=4)
                            if first_o[n]:
                                self.act(ov, pv_, AF.Copy, [po[hf].b], [oT.b])
                            else:
                                self.tt(ov, ov, pv_, ALU.add, [oT.b, po[hf].b], [oT.b])
                        first_o[n] = False
                        self.unpin(*po)
                    pt = self.nps()
                    ptb = pt.ap[:, 0:256].bitcast(BF16)
                    for h in range(4):
                        self.tr(ptb[:, h * 128:(h + 1) * 128], kt.ap[:, h, :], self.identb.ap[:, :], [kt.b, self.identb.b], [pt.b])
                    self.act(ktm.ap[:, :, :], ptb.rearrange("p (h k) -> p h k", h=4), AF.Copy, [pt.b], [ktm.b])
                    pd = [self.pps(), self.pps()]
                    for h in range(4):
                        self.mm(pd[h // 2].ap[:, (h % 2) * 256:(h % 2 + 1) * 256], ktm.ap[:, h, :], vtm.ap[:, n, h * 256:(h + 1) * 256], True, True, [ktm.b, vtm.b], [pd[h // 2].b])
                    eblb = ebl.unsqueeze(2).broadcast_to([128, 4, 256])
                    for hf in range(2):
                        sv_ = Sf.ap[:, dr, hf * 2:(hf + 1) * 2, :]
                        dv_ = pd[hf].ap[:, :].rearrange("p (h v) -> p h v", h=2)
                        ev_ = ebl[:, hf * 2:(hf + 1) * 2].unsqueeze(2).broadcast_to([128, 2, 256])
                        tv_ = tS.ap[:, hf * 2:(hf + 1) * 2, :]
                        if zero_state:
                            self.tt(sv_, dv_, ev_, ALU.mult, [pd[hf].b, eb.b], [Sf.b])
                        else:
                            self.tt(tv_, dv_, sv_, ALU.add, [pd[hf].b, Sf.b], [tS.b])
                            self.tt(sv_, tv_, ev_, ALU.mult, [tS.b, eb.b], [Sf.b])
                    self.unpin(*pd)
                    if full:
                        self.act(Sb.ap[:, dr, :, :], Sf.ap[:, dr, :, :], AF.Copy, [Sf.b], [Sb.b])
                    if state_only:
                        if n == order[0]:
                            self.vcopy(PA.ap[:, dr, :], ebl, [eb.b], [PA.b])
                        else:
                            self.tt(PA.ap[:, dr, :], PA.ap[:, dr, :], ebl, ALU.mult, [PA.b, eb.b], [PA.b])
                    zero_state = False
                if not is_s:
                    self.store(d["n_sg"][l, si, dr].rearrange("h k v -> k h v"), Sf.ap[:, dr, :, :], [Sf.b])
        if state_only:
            xo_ = self.xoutB[l]
            xin = Tl(xo_.ap[self.tg * 128:(self.tg + 1) * 128, :], xo_.b)
            self.store(xin.ap[:, 0:8], PA.ap[:, :, :].rearrange("p a b -> p (a b)"), [PA.b], [xin.b])
            self.store(xin.ap[:, 8:8 + 2048], Sf.ap[:, :, :, :].rearrange("p a b c -> p (a b c)"), [Sf.b], [xin.b])
        self.release(m2)
        if full:
            go, _ = SP_OFF["ggo%d" % l]
            on = [self.alloc([128, T], F32, "on%d" % k) for k in range(2)]
            sg = [self.alloc([128, T], F32, "sg%d" % k) for k in range(2)]
            slabs = [self.wslab(self.rows(win, 0, D, O_GOUT + half * 512, 512), 16, 512) for half in range(2)]
            for h in range(4):
                rs = self.next_rstd()
                self.sumsq_rstd([(oT.ap[:, h * 2 + vc, :], oT.b, 128) for vc in range(2)], 256, rs, self.rtmp)
                for vc in range(2):
                    j = h * 2 + vc
                    pg = self.nps()
                    sl = slabs[j // 4]
                    for c in range(16):
                        self.mm(pg.ap[:, :], sl.ap[:, c, (j % 4) * 128:(j % 4 + 1) * 128], hT.ap[:, c, :], c == 0, c == 15, [sl.b, hT.bs[c]], [pg.b])
                    self.act(sg[j % 2].ap[:, :], pg.ap[:, :], AF.Silu, [pg.b], [sg[j % 2].b])
                    self.stt(on[j % 2].ap[:, :], oT.ap[:, j, :], self.sp.ap[:, go + vc:go + vc + 1], rs.ap[:, :], ALU.mult, ALU.mult, [oT.b, self.sp.b, rs.b], [on[j % 2].b])
                    self.tt(oT_gla.ap[:, j, :], on[j % 2].ap[:, :], sg[j % 2].ap[:, :], ALU.mult, [on[j % 2].b, sg[j % 2].b], [oT_gla.b])
        self.release(m1)

    def gla_init_states(self, l, Sf, Sb, tS):
        d = self.d
        xo = self.xoutB[l]
        F = self.alloc([128, 4, 256], F32, "glaF")
        Sl = self.alloc([128, 4, 256], F32, "glaSl")
        Aj = self.alloc([128, 4, 8], F32, "glaA")
        go, _ = SP_OFF["gsel"]
        self.load(Aj.ap[:, :, :], xo.ap[:, 0:8].rearrange("(r p) c -> p r c", p=128), [Aj.b], [xo.b])
        for dr in range(2):
            self.load(F.ap[:, :, :], d["state"][l, dr].rearrange("h k v -> k h v"), [F.b])
            order = [0, 1, 2, 3] if dr == 0 else [3, 2, 1, 0]
            for idx, j in enumerate(order):
                if self.tg == "own":
                    go_ = SP_OFF["gsel_own"][0]
                else:
                    go_ = go + self.tg * 8
                sel = self.sp.ap[:, go_ + dr * 4 + j:go_ + dr * 4 + j + 1]
                sv_ = Sf.ap[:, dr, :, :]
                if idx == 0:
                    self.ts(sv_, F.ap[:, :, :], sel, None, ALU.mult, None, [F.b, self.sp.b], [Sf.b])
                else:
                    self.stt(sv_, F.ap[:, :, :], sel, sv_, ALU.mult, ALU.add, [F.b, self.sp.b, Sf.b], [Sf.b])
                if idx < 3:
                    self.load(Sl.ap[:, :, :], xo.ap[j * 128:(j + 1) * 128, 8 + dr * 1024:8 + (dr + 1) * 1024].rearrange("p (h v) -> p h v", h=4), [Sl.b], [xo.b])
                    av = Aj.ap[:, j, dr * 4:(dr + 1) * 4].unsqueeze(2).broadcast_to([128, 4, 256])
                    self.tt(tS.ap[:, :, :], F.ap[:, :, :], av, ALU.mult, [F.b, Aj.b], [tS.b])
                    self.tt(F.ap[:, :, :], tS.ap[:, :, :], Sl.ap[:, :, :], ALU.add, [tS.b, Sl.b], [F.b])
            self.act(Sb.ap[:, dr, :, :], Sf.ap[:, dr, :, :], AF.Copy, [Sf.b], [Sb.b])

    def stage_mla(self, l, is_s, oT_mla):
        d = self.d
        win = d["w_in"][l]
        hT = self.hT
        m1 = self.mark()
        NK = 2304 if is_s else T
        cqn = self.alloc([128, 4, T], BF16, "cqn")
        qn = self.alloc([128, 8, T], BF16, "qn")
        qr = self.alloc([64, 8, T], BF16, "qr")
        knT = self.alloc([128, NK], BF16, "knT")
        krT = self.alloc([64, NK], BF16, "krT")
        vh = self.alloc([128, NK // 128, 128], BF16, "vh")
        PT = [self.alloc([128, T], BF16, "PT%d" % k) for k in range(3)]
        rd = self.alloc([128, T], F32, "rd")
        if is_s:
            ckva = self.alloc([128, 4, NK], BF16, "ckva")
            krx = self.alloc([64, NK], BF16, "krx")
            ssr = self.alloc([1, NK], BF16, "ssr")
            kpc = self.alloc([64, 256], F32, "kpc")
            sqc = self.alloc([64, 256], BF16, "sqc")
            qrf = self.alloc([64, T], F32, "qrf")
            rt1 = self.alloc([64, T], F32, "mrt1")
            rt2 = self.alloc([64, T], F32, "mrt2")
            ckv_src = ckva
        else:
            ckv_src = self.ckvb
        slab = self.wslab(self.rows(win, 0, D, O_CQ, 512), 16, 512)
        pcs = [self.pps() for k in range(4)]
        for c4 in range(4):
            for c in range(16):
                self.mm(pcs[c4].ap[:, :], slab.ap[:, c, c4 * 128:(c4 + 1) * 128], hT.ap[:, c, :], c == 0, c == 15, [slab.b, hT.bs[c]], [pcs[c4].b])
        rs = self.next_rstd()
        self.sumsq_rstd([(pcs[c4].ap[:, :], pcs[c4].b, 128) for c4 in range(4)], 512, rs, self.rtmp)
        go, _ = SP_OFF["gq%d" % l]
        for c4 in range(4):
            self.stt(cqn.ap[:, c4, :], pcs[c4].ap[:, :], self.sp.ap[:, go + c4:go + c4 + 1], rs.ap[:, :], ALU.mult, ALU.mult, [pcs[c4].b, self.sp.b, rs.b], [cqn.b])
        self.unpin(*pcs)
        wq = self.wslab(d["w_mla_uq"][l].rearrange("(c p) n -> p c n", p=128), 4, 1536)
        gn, _ = SP_OFF["gqn_n%d" % l]
        gr, _ = SP_OFF["gqn_r%d" % l]
        for h in range(8):
            pn = self.pps()
            pr = self.pps()
            for kc in range(4):
                self.mm(pn.ap[:, :], wq.ap[:, kc, h * 192:h * 192 + 128], cqn.ap[:, kc, :], kc == 0, kc == 3, [wq.b, cqn.b], [pn.b])
            for kc in range(4):
                self.mm(pr.ap[0:64, :], wq.ap[:, kc, h * 192 + 128:h * 192 + 192], cqn.ap[:, kc, :], kc == 0, kc == 3, [wq.b, cqn.b], [pr.b])
            rs = self.next_rstd()
            self.sumsq_rstd([(pn.ap[:, :], pn.b, 128), (pr.ap[0:64, :], pr.b, 64)], 192, rs, self.rtmp)
            self.stt(qn.ap[:, h, :], pn.ap[:, :], self.sp.ap[:, gn:gn + 1], rs.ap[:, :], ALU.mult, ALU.mult, [pn.b, self.sp.b, rs.b], [qn.b])
            if is_s:
                self.stt(qrf.ap[:, :], pr.ap[0:64, :], self.sp.ap[0:64, gr:gr + 1], rs.ap[0:64, :], ALU.mult, ALU.mult, [pr.b, self.sp.b, rs.b], [qrf.b])
                self.rope_apply(Tl(qrf.ap[:, :], qrf.b), qr.ap[:, h, :], qr.b, Tl(rt1.ap[:, :], rt1.b), Tl(rt2.ap[:, :], rt2.b))
            else:
                self.stt(qr.ap[:, h, :], pr.ap[0:64, :], self.sp.ap[0:64, gr:gr + 1], rs.ap[0:64, :], ALU.mult, ALU.mult, [pr.b, self.sp.b, rs.b], [qr.b])
            self.unpin(pn, pr)
        if is_s:
            xo = self.xoutA[l]
            xr = xo.ap.rearrange("(r p) c -> p r c", p=128)
            for c_ in range(4):
                self.load(ckva.ap[:, c_, 0:2048].rearrange("p (r t) -> p r t", r=4), xr[:, :, X_CKV + c_ * T:X_CKV + (c_ + 1) * T], [ckva.b], [xo.b], q="pool")
            self.load(krx.ap[0:64, 0:2048].rearrange("p (r t) -> p r t", r=4), xr[X_KR[0]:X_KR[0] + 64, :, X_KR[1]:X_KR[1] + T], [krx.b], [xo.b], q="pool")
            self.load(ssr.ap[0:1, 0:2048].rearrange("p (r t) -> p r t", r=4), xr[X_SS[0]:X_SS[0] + 1, :, X_SS[1]:X_SS[1] + T], [ssr.b], [xo.b], q="pool")
            self.load(ckva.ap[:, :, 2048:2304], d["ckv_cT"][l].rearrange("(c p) t -> p c t", p=128), [ckva.b], q="pool")
            self.load(kpc.ap[:, :], d["kpe_cT"][l], [kpc.b])
            self.act(sqc.ap[:, :], kpc.ap[:, :], AF.Square, [kpc.b], [sqc.b])
            go, _ = SP_OFF["gkn_r%d" % l]
            self.ts(kpc.ap[:, :], kpc.ap[:, :], self.sp.ap[0:64, go:go + 1], None, ALU.mult, None, [kpc.b, self.sp.b], [kpc.b])
        wkv = self.wslab(d["w_mla_ukv"][l].rearrange("(c p) n -> p c n", p=128), 4, 2048)
        gn, _ = SP_OFF["gkn_n%d" % l]
        gr, _ = SP_OFF["gkn_r%d" % l]
        blocks = [(n0, min(512, NK - n0)) for n0 in range(0, NK, 512)]
        for h in range(8):
            for (n0, nb) in blocks:
                pk = self.pps()
                for kc in range(4):
                    self.mm(pk.ap[:, 0:nb], wkv.ap[:, kc, h * 256:h * 256 + 128], ckv_src.ap[:, kc, n0:n0 + nb], kc == 0, kc == 3, [wkv.b, ckv_src.b], [pk.b])
                rs = self.next_rstd()
                if not is_s:
                    extra = [(self.onesb.ap[0:64, :], self.sqk.ap[:, n0:n0 + nb], [self.onesb.b, self.sqk.b])]
                elif n0 < 2048:
                    extra = [(self.onesb.ap[0:1, :], ssr.ap[0:1, n0:n0 + nb], [self.onesb.b, ssr.b])]
                else:
                    extra = [(self.onesb.ap[0:64, :], sqc.ap[:, :], [self.onesb.b, sqc.b])]
                self.sumsq_rstd([(pk.ap[:, 0:nb], pk.b, 128)], 192, rs, self.rtmp, width=nb, extra=extra)
                self.stt(knT.ap[:, n0:n0 + nb], pk.ap[:, 0:nb], self.sp.ap[:, gn:gn + 1], rs.ap[:, 0:nb], ALU.mult, ALU.mult, [pk.b, self.sp.b, rs.b], [knT.b])
                self.unpin(pk)
                if not is_s:
                    self.stt(krT.ap[:, n0:n0 + nb], self.kpef.ap[:, n0:n0 + nb], self.sp.ap[0:64, gr:gr + 1], rs.ap[0:64, 0:nb], ALU.mult, ALU.mult, [self.kpef.b, self.sp.b, rs.b], [krT.b])
                elif n0 < 2048:
                    self.tt(krT.ap[:, n0:n0 + nb], krx.ap[0:64, n0:n0 + nb], rs.ap[0:64, 0:nb], ALU.mult, [krx.b, rs.b], [krT.b])
                else:
                    self.tt(krT.ap[:, n0:n0 + nb], kpc.ap[:, :], rs.ap[0:64, 0:nb], ALU.mult, [kpc.b, rs.b], [krT.b])
                pv = self.nps()
                for i in range(nb // 128):
                    st = n0 // 128 + i
                    for kc in range(4):
                        self.mm(pv.ap[:, i * 128:(i + 1) * 128], ckv_src.ap[:, kc, st * 128:(st + 1) * 128], wkv.ap[:, kc, h * 256 + 128:h * 256 + 256], kc == 0, kc == 3, [ckv_src.b, wkv.b], [pv.b])
                self.act(vh.ap[:, n0 // 128:n0 // 128 + nb // 128, :], pv.ap[:, 0:nb].rearrange("p (a v) -> p a v", v=128), AF.Copy, [pv.b], [vh.b])
            qsets = [(0, T, list(range(NK // 128)))] if is_s else [(0, 256, [0, 1]), (256, 256, [2, 3])]
            pO = self.pps()
            pD = self.pps()
            for (q0, qn_, tiles) in qsets:
                for ti, st in enumerate(tiles):
                    psc = self.nps()
                    self.mm(psc.ap[:, 0:qn_], knT.ap[:, st * 128:(st + 1) * 128], qn.ap[:, h, q0:q0 + qn_], True, False, [knT.b, qn.b], [psc.b])
                    self.mm(psc.ap[:, 0:qn_], krT.ap[0:64, st * 128:(st + 1) * 128], qr.ap[0:64, h, q0:q0 + qn_], False, True, [krT.b, qr.b], [psc.b])
                    pt_ = PT[self._sqi % 3]
                    self._sqi += 1
                    self.act(pt_.ap[:, 0:qn_], psc.ap[:, 0:qn_], AF.Exp, [psc.b], [pt_.b], scale=MLA_SCALE)
                    first, last = ti == 0, ti == len(tiles) - 1
                    self.mm(pO.ap[:, q0:q0 + qn_], vh.ap[:, st, :], pt_.ap[:, 0:qn_], first, last, [vh.b, pt_.b], [pO.b])
                    self.mm(pD.ap[:, q0:q0 + qn_], self.onesb.ap[:, :], pt_.ap[:, 0:qn_], first, last, [self.onesb.b, pt_.b], [pD.b])
            self.recip(rd.ap[:, :], pD.ap[:, :], [pD.b], [rd.b])
            self.tt(oT_mla.ap[:, h, :], pO.ap[:, :], rd.ap[:, :], ALU.mult, [pO.b, rd.b], [oT_mla.b])
            self.unpin(pO, pD)
        self.release(m1)

    def stage_swa(self, l, is_s, oT_swa):
        d = self.d
        win = d["w_in"][l]
        hT = self.hT
        m1 = self.mark()
        nqh = 4 if is_s else 16
        qs = self.alloc([64, nqh, T], BF16, "qs")
        rd = self.alloc([64, T], F32, "srd")
        so, _ = SP_OFF["sink%d" % l]
        self.act(self.esink.ap[:, :], self.sp.ap[:, so:so + 16], AF.Exp, [self.sp.b], [self.esink.b])
        if is_s:
            qf = self.alloc([64, T], F32, "sqf")
            rt1 = self.alloc([64, T], F32, "srt1")
            rt2 = self.alloc([64, T], F32, "srt2")
        go, _ = SP_OFF["gsq%d" % l]

        def q_heads(slab, hh_list, dst0):
            for n_, hh in enumerate(hh_list):
                pq = self.nps()
                for c in range(16):
                    self.mm(pq.ap[0:64, :], slab.ap[:, c, hh * 64:(hh + 1) * 64], hT.ap[:, c, :], c == 0, c == 15, [slab.b, hT.bs[c]], [pq.b])
                rs = self.next_rstd()
                self.sumsq_rstd([(pq.ap[0:64, :], pq.b, 64)], 64, rs, self.rtmp, m=64)
                if is_s:
                    self.stt(qf.ap[:, :], pq.ap[0:64, :], self.sp.ap[0:64, go:go + 1], rs.ap[0:64, :], ALU.mult, ALU.mult, [pq.b, self.sp.b, rs.b], [qf.b])
                    self.rope_apply(Tl(qf.ap[:, :], qf.b), qs.ap[:, dst0 + n_, :], qs.b, Tl(rt1.ap[:, :], rt1.b), Tl(rt2.ap[:, :], rt2.b))
                else:
                    self.stt(qs.ap[:, dst0 + n_, :], pq.ap[0:64, :], self.sp.ap[0:64, go:go + 1], rs.ap[0:64, :], ALU.mult, ALU.mult, [pq.b, self.sp.b, rs.b], [qs.b])
        if not is_s:
            for half in range(2):
                slab = self.wslab(self.rows(win, 0, D, O_SQ + half * 512, 512), 16, 512)
                q_heads(slab, list(range(8)), half * 8)
        if not is_s:
            PT = [self.alloc([128, 2, T], BF16, "sPT%d" % k) for k in range(2)]
            k = 0
            for hk in range(4):
                for bb in range(2):
                    for gp in range(2):
                        ps_ = [self.pps(), self.pps()]
                        for st in range(2):
                            kt_ = 2 * bb + st
                            self.mm(ps_[st].ap[:, :].rearrange("p (g t) -> p g t", g=2), self.swkb.ap[0:64, hk, kt_ * 128:(kt_ + 1) * 128],
                                    qs.ap[0:64, hk * 4 + gp * 2:hk * 4 + gp * 2 + 2, bb * 256:(bb + 1) * 256], True, True, [self.swkb.b, qs.b], [ps_[st].b])
                        pt_ = PT[k % 2]
                        k += 1
                        for st in range(2):
                            self.act(pt_.ap[:, st, :], ps_[st].ap[:, :], AF.Exp, [ps_[st].b], [pt_.b], scale=SWA_SCALE)
                        self.unpin(*ps_)
                        pO = self.pps()
                        pD = self.pps()
                        for st in range(2):
                            kt_ = 2 * bb + st
                            self.mm(pO.ap[0:64, :], self.swvb.ap[:, kt_, hk * 64:(hk + 1) * 64], pt_.ap[:, st, :], st == 0, st == 1, [self.swvb.b, pt_.b], [pO.b])
                            self.mm(pD.ap[0:64, :], self.onesb.ap[:, 0:64], pt_.ap[:, st, :], st == 0, st == 1, [self.onesb.b, pt_.b], [pD.b])
                        for g2 in range(2):
                            hd = hk * 4 + gp * 2 + g2
                            self.ts(rd.ap[:, g2 * 256:(g2 + 1) * 256], pD.ap[0:64, g2 * 256:(g2 + 1) * 256], self.esink.ap[0:64, hd:hd + 1], None, ALU.add, None, [pD.b, self.esink.b], [rd.b])
                        self.recip(rd.ap[:, :], rd.ap[:, :], [rd.b], [rd.b])
                        self.tt(oT_swa.ap[:, hk * 4 + gp * 2:hk * 4 + gp * 2 + 2, bb * 256:(bb + 1) * 256], pO.ap[0:64, :].rearrange("p (g t) -> p g t", g=2),
                                rd.ap[:, :].rearrange("p (g t) -> p g t", g=2), ALU.mult, [pO.b, rd.b], [oT_swa.b])
                        self.unpin(pO, pD)
        else:
            xo = self.xoutA[l]
            kall = self.alloc([64, 4, 2304], BF16, "kall")
            vall = self.alloc([128, 18, 256], BF16, "vall")
            msk = [self.alloc([128, 16, 128], BF16, "msk%d" % k) for k in range(2)]
            PT = [self.alloc([128, T], BF16, "sPT%d" % k) for k in range(2)]
            PM = [self.alloc([128, T], BF16, "sPM%d" % k) for k in range(2)]
            slab = self.wslab(self.rows(win, 0, D, O_SQ, 512), 16, 512)
            xr = xo.ap.rearrange("(r p) c -> p r c", p=128)
            for h_ in range(4):
                r0_, c0_ = X_SWK[h_]
                self.load(kall.ap[:, h_, 0:2048].rearrange("p (r t) -> p r t", r=4), xr[r0_:r0_ + 64, :, c0_:c0_ + T], [kall.b], [xo.b], q="pool")
            for a_ in range(4):
                self.load(vall.ap[:, 0:16, :].rearrange("p (r a) n -> p r a n", r=4)[:, :, a_, :], xr[:, :, X_SWV + a_ * 256:X_SWV + (a_ + 1) * 256], [vall.b], [xo.b], q="pool")
            self.load(kall.ap[:, :, 2048:2304], d["swk_cT"][l], [kall.b], q="pool")
            self.load(vall.ap[:, 16:18, :], d["swv_c"][l].rearrange("(a p) n -> p a n", p=128), [vall.b], q="pool")
            mdr = d["swamask"].rearrange("p (j i t) -> p j i t", j=16, i=4)
            own = self.tg == "own"
            k = 0
            mi = 0
            for hk in range(4):
                if hk == 2:
                    slab = self.wslab(self.rows(win, 0, D, O_SQ + 512, 512), 16, 512)
                q_heads(slab, [(hk % 2) * 4 + g for g in range(4)], 0)
                for i in range(4):
                    if own:
                        mk = msk[mi % 2]
                        mi += 1
                        self.load(mk.ap[:, :, :], mdr[:, :, i, :], [mk.b], q="pool")
                        jl = [(j, (mk.ap[:, j, :], mk.b) if j < 16 else None) for j in range(18)]
                    else:
                        Q = 4 * self.tg + i
                        jl = []
                        if Q - 1 >= 0:
                            jl.append((Q - 1, (self.maskLb.ap[:, :], self.maskLb.b)))
                        jl.append((Q, None))
                        if Q + 1 <= 15:
                            jl.append((Q + 1, (self.maskUb.ap[:, :], self.maskUb.b)))
                        jl += [(16, None), (17, None)]
                    pO = self.pps()
                    pD = self.pps()
                    for ji, (j, mk_) in enumerate(jl):
                        psc = self.nps()
                        self.mm(psc.ap[:, :].rearrange("p (g t) -> p g t", g=4), kall.ap[0:64, hk, j * 128:(j + 1) * 128], qs.ap[0:64, 0:4, i * 128:(i + 1) * 128], True, True, [kall.b, qs.b], [psc.b])
                        pt_ = PT[k % 2]
                        self.act(pt_.ap[:, :], psc.ap[:, :], AF.Exp, [psc.b], [pt_.b], scale=SWA_SCALE)
                        if mk_ is not None:
                            pm_ = PM[k % 2]
                            self.tt(pm_.ap[:, :].rearrange("p (g t) -> p g t", g=4), pt_.ap[:, :].rearrange("p (g t) -> p g t", g=4),
                                    mk_[0].unsqueeze(1).broadcast_to([128, 4, 128]), ALU.mult, [pt_.b, mk_[1]], [pm_.b])
                        else:
                            pm_ = pt_
                        k += 1
                        first, last_ = ji == 0, ji == len(jl) - 1
                        self.mm(pO.ap[0:64, :], vall.ap[:, j, hk * 64:(hk + 1) * 64], pm_.ap[:, :], first, last_, [vall.b, pm_.b], [pO.b])
                        self.mm(pD.ap[0:64, :], self.onesb.ap[:, 0:64], pm_.ap[:, :], first, last_, [self.onesb.b, pm_.b], [pD.b])
                    for g in range(4):
                        hd = hk * 4 + g
                        self.ts(rd.ap[:, g * 128:(g + 1) * 128], pD.ap[0:64, g * 128:(g + 1) * 128], self.esink.ap[0:64, hd:hd + 1], None, ALU.add, None, [pD.b, self.esink.b], [rd.b])
                    self.recip(rd.ap[:, :], rd.ap[:, :], [rd.b], [rd.b])
                    self.tt(oT_swa.ap[:, hk * 4:hk * 4 + 4, i * 128:(i + 1) * 128], pO.ap[0:64, :].rearrange("p (g t) -> p g t", g=4),
                            rd.ap[:, :].rearrange("p (g t) -> p g t", g=4), ALU.mult, [pO.b, rd.b], [oT_swa.b])
                    self.unpin(pO, pD)
        self.release(m1)

    def stage_merge(self, l, oT_mla, oT_gla, oT_swa):
        d = self.d
        win = d["w_in"][l]
        hT = self.hT
        xT = self.xT
        m1 = self.mark()
        merged = self.alloc([128, 8, T], F32, "merged", nb=8)
        mergedb = self.alloc([128, 16, T], BF16, "mergedb", nb=16)
        sig = [self.alloc([128, T], F32, "sig%d" % k) for k in range(2)]
        tm = [self.alloc([128, T], F32, "tm%d" % k) for k in range(2)]
        self.push_xslots(64)
        branches = [("w_br_mla", O_GM, oT_mla, 8, 128), ("w_br_gla", O_GG, oT_gla, 8, 128), ("w_br_swa", O_GS, oT_swa, 16, 64)]
        k = 0
        for half in range(2):
            for bi, (wn, og, oT, nk, kp) in enumerate(branches):
                for b2 in range(2):
                    blk = half * 2 + b2
                    sb_ = self.wslab(d[wn][l][:, blk * 512:(blk + 1) * 512].rearrange("(c p) n -> p c n", p=kp), nk, 512)
                    sg_ = self.wslab(self.rows(win, 0, D, og + blk * 512, 512), 16, 512)
                    for jj in range(4):
                        j = blk * 4 + jj
                        jl = b2 * 4 + jj
                        pb = self.nps()
                        pg = self.nps()
                        for kc in range(nk):
                            self.mm(pb.ap[:, :], sb_.ap[0:kp, kc, jj * 128:(jj + 1) * 128], oT.ap[0:kp, kc, :], kc == 0, kc == nk - 1, [sb_.b, oT.b], [pb.b])
                        for c in range(16):
                            self.mm(pg.ap[:, :], sg_.ap[:, c, jj * 128:(jj + 1) * 128], hT.ap[:, c, :], c == 0, c == 15, [sg_.b, hT.bs[c]], [pg.b])
                        s_ = sig[k % 2]
                        t_ = tm[k % 2]
                        k += 1
                        self.act(s_.ap[:, :], pg.ap[:, :], AF.Sigmoid, [pg.b], [s_.b])
                        if bi == 0:
                            self.tt(merged.ap[:, jl, :], s_.ap[:, :], pb.ap[:, :], ALU.mult, [s_.b, pb.b], [merged.bs[jl]])
                        else:
                            self.tt(t_.ap[:, :], s_.ap[:, :], pb.ap[:, :], ALU.mult, [s_.b, pb.b], [t_.b])
                            if bi == 1:
                                self.tt(merged.ap[:, jl, :], merged.ap[:, jl, :], t_.ap[:, :], ALU.add, [merged.bs[jl], t_.b], [merged.bs[jl]])
                            else:
                                self.tt(mergedb.ap[:, j, :], merged.ap[:, jl, :], t_.ap[:, :], ALU.add, [merged.bs[jl], t_.b], [mergedb.bs[j]])
        wo = d["w_out"][l]
        for blk in range(4):
            sw = self.wslab(self.rows(wo, 0, D, blk * 512, 512), 16, 512)
            for dj in range(4):
                ch = blk * 4 + dj
                po = self.nps()
                for j in range(16):
                    self.mm(po.ap[:, :], sw.ap[:, j, dj * 128:(dj + 1) * 128], mergedb.ap[:, j, :], j == 0, j == 15, [sw.b, mergedb.bs[j]], [po.b])
                self.stt(xT.ap[:, ch, :], po.ap[:, :], self.modG.ap[:, ch:ch + 1], xT.ap[:, ch, :], ALU.mult, ALU.add, [po.b, self.modG.b, xT.bs[ch]], [xT.bs[ch]])
        self.pop_xslots()
        self.release(m1)

    def load_x(self, src, c0=0, reads=()):
        xv = src.rearrange("(c p) t -> p c t", p=128)
        for c8 in range(2):
            self.load(self.xT.ap[:, c8 * 8:(c8 + 1) * 8, :], xv[:, c8 * 8:(c8 + 1) * 8, c0:c0 + T], self.xT.bs[c8 * 8:(c8 + 1) * 8], reads)

    def store_x(self, dst, c0=0, writes=()):
        dv = dst.rearrange("(c p) t -> p c t", p=128)
        for c8 in range(2):
            self.store(dv[:, c8 * 8:(c8 + 1) * 8, c0:c0 + T], self.xT.ap[:, c8 * 8:(c8 + 1) * 8, :], self.xT.bs[c8 * 8:(c8 + 1) * 8], writes)

    def load_xq(self, src, c0, dq, reads=()):
        xv = src.rearrange("(c p) t -> p c t", p=128)
        self.load(self.xT.ap[:, dq * 4:(dq + 1) * 4, :], xv[:, dq * 4:(dq + 1) * 4, c0:c0 + T], self.xT.bs[dq * 4:(dq + 1) * 4], reads)

    def store_xq(self, dst, c0, dq, writes=()):
        dv = dst.rearrange("(c p) t -> p c t", p=128)
        self.store(dv[:, dq * 4:(dq + 1) * 4, c0:c0 + T], self.xT.ap[:, dq * 4:(dq + 1) * 4, :], self.xT.bs[dq * 4:(dq + 1) * 4], writes)

    def run(self, stop=None):
        d = self.d
        self.consts()
        self.adaln()
        if self.do_p:
            self.load_x(d["xpT"])
            for l in range(self.n_layers):
                self.ffn(l, 1, 0)
                if stop == "ffn1":
                    break
                self.mixer(l, 0, False)
                if stop == "mixer":
                    break
                self.ffn(l, 2, 0)
            self.store_x(d["ypT"])
        if self.do_s:
            xs = self.xscr
            oo, _ = SP_OFF["osel"]
            for l in range(self.n_layers):
                last = (l == self.n_layers - 1)
                if last:
                    macc = self.mark()
                    acc = self.alloc([128, 16, T], F32, "xown", nb=16)
                def fetch(tg_):
                    if l == 0:
                        self.load_x(d["xsT"], tg_ * T)
                    else:
                        self.load_x(xs.ap, tg_ * T, [xs.bs[tg_]])
                for tg in range(4):
                    self.load(self.rope.ap[:, :, :], d["rope"][tg], [self.rope.b])
                    if tg == 0 or stop == "ffn1":
                        fetch(tg)
                    self.ffn(l, 1, 1)
                    if last:
                        sel = self.sp.ap[:, oo + tg:oo + tg + 1]
                        for c in range(16):
                            if tg == 0:
                                self.ts(acc.ap[:, c, :], self.xT.ap[:, c, :], sel, None, ALU.mult, None, [self.xT.bs[c], self.sp.b], [acc.bs[c]])
                            else:
                                self.stt(acc.ap[:, c, :], self.xT.ap[:, c, :], sel, acc.ap[:, c, :], ALU.mult, ALU.add, [self.xT.bs[c], self.sp.b, acc.bs[c]], [acc.bs[c]])
                    else:
                        self.store_x(xs.ap, tg * T, [xs.bs[tg]])
                    if stop == "ffn1":
                        continue
                    self.mixer(l, 1, True, 1, tg, after_norm=(lambda t_=tg: fetch(t_ + 1)) if tg < 3 else None)
                if not last:
                    for tg in range(4):
                        self.load(self.rope.ap[:, :, :], d["rope"][tg], [self.rope.b])
                        if tg == 0:
                            self.load_x(xs.ap, tg * T, [xs.bs[tg]])
                        self.mixer(l, 1, True, 2, tg)

                        def handoff(dq, t_=tg):
                            self.store_xq(xs.ap, t_ * T, dq, [xs.bs[t_]])
                            if t_ < 3:
                                self.load_xq(xs.ap, (t_ + 1) * T, dq, [xs.bs[t_ + 1]])
                        self.ffn(l, 2, 1, after_dq=handoff)
                else:
                    for c in range(16):
                        self.vcopy(self.xT.ap[:, c, :], acc.ap[:, c, :], [acc.bs[c]], [self.xT.bs[c]])
                    self.release(macc)
                    self.load(self.rope.ap[:, :, :], d["rope"][4], [self.rope.b])
                    if stop != "ffn1":
                        self.mixer(l, 1, True, 2, "own")
                        if stop != "mixer":
                            self.ffn(l, 2, 1)
                    self.store_x(d["ysT"])
        self.P.wait_all_dma("sp")
        return self.P.emit()


def _rope_tables(pos):
    half = 16
    inv = (10000.0 ** (-np.arange(half, dtype=np.float32) / half)).astype(np.float32)
    tab = np.zeros((64, 2, len(pos)), np.float32)
    for dd in range(64):
        comp = (pos // 64) if dd < 32 else (pos % 64)
        ang = comp.astype(np.float32) * inv[dd % 16]
        tab[dd, 0] = np.cos(ang).astype(np.float32)
        tab[dd, 1] = np.sin(ang).astype(np.float32)
    return tab


def _perm_T():
    pm = np.zeros((64, 64), np.float32)
    for dd in range(64):
        if dd % 32 < 16:
            pm[dd, dd + 16] = -1.0
        else:
            pm[dd, dd - 16] = 1.0
    return np.ascontiguousarray(pm.T)


def _swamask(q):
    m = np.zeros((128, 16, 4, 128), np.float32)
    s = np.arange(128)[:, None]
    t = np.arange(128)[None, :]
    for i in range(4):
        for j in range(16):
            dlt = j - (4 * q + i)
            if dlt == 0:
                m[:, j, i, :] = 1.0
            elif dlt == -1:
                m[:, j, i, :] = (t <= s)
            elif dlt == 1:
                m[:, j, i, :] = (s <= t)
    return m.reshape(128, -1)


def prep_core(inp, core):
    b, q = core // 4, core % 4
    f = lambda a: np.ascontiguousarray(np.asarray(a, dtype=np.float32))
    m = {}
    m["xpT"] = f(np.asarray(inp["x_prompt"])[2 * core:2 * core + 2].reshape(T, D).T)
    m["xsT"] = f(np.asarray(inp["x_sample"])[b].T)
    sp = np.zeros((128, SP_N), np.float32)

    def put(name, arr):
        o, w = SP_OFF[name]
        arr = np.asarray(arr, np.float32)
        assert arr.shape[1] == w, (name, arr.shape, w)
        sp[0:arr.shape[0], o:o + w] = arr
    put("ident", np.eye(128, dtype=np.float32))
    put("permT", _perm_T())
    s_ = np.arange(128)[:, None]
    t_ = np.arange(128)[None, :]
    put("maskU", (s_ <= t_).astype(np.float32))
    put("maskL", (s_ >= t_).astype(np.float32))
    cond = np.stack([np.asarray(inp["c_ctx"]), np.asarray(inp["c"])[b]], axis=1)
    put("cond", cond.reshape(16, 128, 2).transpose(1, 0, 2).reshape(128, 32))
    gs = np.zeros((128, 4, 8), np.float32)
    for tg_ in range(4):
        gs[:, tg_, tg_] = 1.0
        gs[:, tg_, 4 + tg_] = 1.0
    put("gsel", gs.reshape(128, 32))
    put("gsel_own", gs[:, q, :])
    os_ = np.zeros((128, 4), np.float32)
    os_[:, q] = 1.0
    put("osel", os_)
    col = lambda v: np.asarray(v, np.float32).reshape(-1, 1)
    for l in range(DEPTH):
        put("bada%d" % l, np.asarray(inp["b_ada"])[l].reshape(144, 128).T)
        put("gn%d" % l, np.concatenate([np.asarray(inp[k])[l].reshape(16, 128).T for k in ("g_norm1", "g_norm2", "g_norm3")], axis=1))
        put("gq%d" % l, np.asarray(inp["g_mla_q"])[l].reshape(4, 128).T)
        put("gkv%d" % l, np.asarray(inp["g_mla_kv"])[l].reshape(4, 128).T)
        put("gqn_n%d" % l, col(np.asarray(inp["g_mla_qn"])[l][:128]))
        put("gqn_r%d" % l, col(np.asarray(inp["g_mla_qn"])[l][128:]))
        put("gkn_n%d" % l, col(np.asarray(inp["g_mla_kn"])[l][:128]))
        put("gkn_r%d" % l, col(np.asarray(inp["g_mla_kn"])[l][128:]))
        put("ggo%d" % l, np.asarray(inp["g_gla_out"])[l].reshape(2, 128).T)
        put("gsq%d" % l, col(np.asarray(inp["g_swa_qn"])[l]))
        put("gsk%d" % l, col(np.asarray(inp["g_swa_kn"])[l]))
        put("sink%d" % l, np.broadcast_to(np.asarray(inp["swa_sink"])[l][None, :], (128, 16)))
    m["smallp"] = sp
    m["rope"] = np.stack([_rope_tables(q_ * T + np.arange(T)) for q_ in (0, 1, 2, 3, q)])
    m["wg"] = f(np.stack([np.asarray(inp["w_gla_gf"]), np.asarray(inp["w_gla_gb"])], axis=1).transpose(2, 0, 1, 3))
    m["bg"] = f(np.stack([np.asarray(inp["b_gla_gf"]), np.asarray(inp["b_gla_gb"])], axis=1)[None])
    m["swamask"] = _swamask(q)
    for l in range(DEPTH):
        m["w_ada%d" % l] = np.asarray(inp["w_ada"], dtype=np.float32)[l]
    for k in ("w_ff1_gu", "w_ff2_gu", "w_ff1_down", "w_ff2_down", "w_in", "w_mla_uq", "w_mla_ukv",
              "w_br_mla", "w_br_gla", "w_br_swa", "w_out"):
        m[k] = np.asarray(inp[k], dtype=np.float32)
    m["ckv_cT"] = f(np.asarray(inp["cache_mla_ckv"])[b].transpose(0, 2, 1))
    m["kpe_cT"] = f(np.asarray(inp["cache_mla_kpe"])[b].transpose(0, 2, 1))
    m["swk_cT"] = f(np.asarray(inp["cache_swa_k"])[b].transpose(0, 3, 2, 1))
    m["swv_c"] = f(np.asarray(inp["cache_swa_v"])[b].reshape(DEPTH, 256, 256))
    m["state"] = f(np.asarray(inp["state_gla"])[b])
    return m


_CACHE = {}


def _get_builder():
    if "b" not in _CACHE:
        b = Builder(do_s=True, do_p=True, n_layers=DEPTH, n_cores=8)
        b.run()
        _CACHE["b"] = b
    return _CACHE["b"]


def kernel(**inputs):
    b = _get_builder()
    shared = {}
    in_maps = []
    for core in range(8):
        m = prep_core(inputs, core)
        for k in list(m.keys()):
            if k.startswith("w_") and k != "wg":
                if k not in shared:
                    shared[k] = m[k]
                m[k] = shared[k]
        in_maps.append(m)
    res = run_bass_kernel_spmd(b.nc, in_maps, core_ids=list(range(8)))
    r = res.results
    B, S = 16, 256
    y_p = np.zeros((B, S, D), np.float32)
    y_s = np.zeros((2, 2048, D), np.float32)
    n_ckv = np.zeros((B, DEPTH, S, 512), np.float32)
    n_kpe = np.zeros((B, DEPTH, S, 64), np.float32)
    n_sk = np.zeros((B, DEPTH, S, 4, 64), np.float32)
    n_sv = np.zeros((B, DEPTH, S, 4, 64), np.float32)
    n_sg = np.zeros((B, DEPTH, 2, 4, 128, 256), np.float32)
    for core in range(8):
        o = r[core]
        bq, q = core // 4, core % 4
        y_p[2 * core:2 * core + 2] = np.asarray(o["ypT"]).T.reshape(2, S, D)
        y_s[bq, q * T:(q + 1) * T] = np.asarray(o["ysT"]).T
        for l in range(DEPTH):
            n_ckv[2 * core:2 * core + 2, l] = np.asarray(o["n_ckvT"])[l].T.reshape(2, S, 512)
            n_kpe[2 * core:2 * core + 2, l] = np.asarray(o["n_kpeT"])[l].T.reshape(2, S, 64)
            n_sk[2 * core:2 * core + 2, l] = np.asarray(o["n_skT"])[l].transpose(2, 0, 1).reshape(2, S, 4, 64)
            n_sv[2 * core:2 * core + 2, l] = np.asarray(o["n_sv"])[l].reshape(2, S, 4, 64)
            n_sg[2 * core:2 * core + 2, l] = np.asarray(o["n_sg"])[l]
    return (y_p, y_s, n_ckv, n_kpe, n_sk, n_sv, n_sg)
```

```python
import contextlib
import numpy as np
import concourse.bass as bass
import concourse.mybir as mybir
from concourse.bass_utils import run_bass_kernel_spmd

F32 = mybir.dt.float32
BF16 = mybir.dt.bfloat16
AF = mybir.ActivationFunctionType
ALU = mybir.AluOpType

D = 2048
NCH = 16
T = 512
DEPTH = 2
DFF = 5632
NFC = 44
EPS = 1e-6
IN_COLS = 11872
O_CQ, O_CKV, O_KPE, O_GQ, O_GK, O_GV, O_GOUT, O_GGF, O_GGB = 0, 512, 1024, 1088, 1600, 2112, 3136, 4160, 4176
O_SQ, O_SK, O_SV, O_GM, O_GG, O_GS = 4192, 5216, 5472, 5728, 7776, 9824
MLA_SCALE = 192 ** -0.5
SWA_SCALE = 64 ** -0.5
GLA_QS = 128 ** -0.5


class Buf:
    __slots__ = ("name", "w", "r")

    def __init__(self, name="", w=None):
        self.name = name
        self.w = w
        self.r = {}


class Chan:
    __slots__ = ("sem", "count", "name")

    def __init__(self, name):
        self.sem = None
        self.count = 0
        self.name = name


class Op:
    __slots__ = ("eng", "fn", "deps", "idx", "signal", "waits", "chan")

    def __init__(self, eng, fn, deps, idx, chan=None):
        self.eng = eng
        self.fn = fn
        self.deps = deps
        self.idx = idx
        self.signal = False
        self.waits = None
        self.chan = chan


ENGS = ("pe", "act", "dve", "pool", "sp")


class Prog:
    def __init__(self, nc):
        self.nc = nc
        self.ops = {e: [] for e in ENGS}
        self.chans = []

    def _collect(self, reads, writes):
        deps = {}
        for b in reads:
            if b.w is not None:
                k, v = b.w
                if deps.get(k, -1) < v:
                    deps[k] = v
        for b in writes:
            if b.w is not None:
                k, v = b.w
                if deps.get(k, -1) < v:
                    deps[k] = v
            for k, v in b.r.items():
                if deps.get(k, -1) < v:
                    deps[k] = v
        return deps

    def _commit(self, tok, reads, writes):
        k, v = tok
        for b in reads:
            if b.r.get(k, -1) < v:
                b.r[k] = v
        for b in writes:
            b.w = tok
            b.r = {}

    def op(self, eng, fn, reads=(), writes=()):
        deps = self._collect(reads, writes)
        idx = len(self.ops[eng])
        if eng == "pe":
            deps.pop(("e", "pe"), None)
        o = Op(eng, fn, deps, idx)
        self.ops[eng].append(o)
        self._commit((("e", eng), idx), reads, writes)
        return (("e", eng), idx)

    def new_chan(self, name):
        c = Chan(name)
        self.chans.append(c)
        return c

    def dma(self, q, chan, fn, reads=(), writes=()):
        deps = self._collect(reads, writes)
        if chan.count > 0:
            k = ("d", chan)
            if deps.get(k, -1) < chan.count:
                deps[k] = chan.count
        o = Op(q, fn, deps, len(self.ops[q]), chan=chan)
        self.ops[q].append(o)
        chan.count += 16
        self._commit((("d", chan), chan.count), reads, writes)

    def fence(self, fn, engs=("pe", "act", "dve")):
        deps = {("e", f): len(self.ops[f]) - 1 for f in engs if len(self.ops[f]) > 0}
        for c in self.chans:
            if c.count > 0:
                deps[("d", c)] = c.count
        idx = len(self.ops["dve"])
        o = Op("dve", fn, deps, idx)
        self.ops["dve"].append(o)
        return (("e", "dve"), idx)

    def wait_all_dma(self, eng="sp"):
        deps = {("d", c): c.count for c in self.chans if c.count > 0}
        self.ops[eng].append(Op(eng, None, deps, len(self.ops[eng])))

    def emit(self):
        nc = self.nc
        for e in ENGS:
            waited = {}
            for o in self.ops[e]:
                w = []
                for k, v in o.deps.items():
                    if waited.get(k, -1) >= v:
                        continue
                    waited[k] = v
                    w.append((k, v))
                    if k[0] == "e":
                        self.ops[k[1]][v].signal = True
                o.waits = w
        cnt = {}
        for e in ENGS:
            c = 0
            arr = []
            for o in self.ops[e]:
                if o.signal:
                    c += 1
                arr.append(c)
            cnt[e] = arr
        with contextlib.ExitStack() as st:
            esem = {e: st.enter_context(nc.semaphore("s_" + e)) for e in ENGS}
            for i, c in enumerate(self.chans):
                if c.count > 0:
                    c.sem = st.enter_context(nc.semaphore("d%d_%s" % (i, c.name)))
            block = st.enter_context(nc.Block())
            hw = {"pe": block.tensor, "act": block.scalar, "dve": block.vector,
                  "pool": block.gpsimd, "sp": block.sync}

            def run(e):
                def body(eng):
                    for o in self.ops[e]:
                        for k, v in o.waits:
                            if k[0] == "e":
                                eng.wait_ge(esem[k[1]], cnt[k[1]][v])
                            else:
                                eng.wait_ge(k[1].sem, v)
                        if o.fn is None:
                            continue
                        ins = o.fn(eng)
                        if o.chan is not None:
                            ins.then_inc(o.chan.sem, 16)
                        elif o.signal:
                            ins.then_inc(esem[e], 1)
                return body
            for e in ENGS:
                hw[e](run(e))
        return {e: len(self.ops[e]) for e in ENGS}


class Tl:
    __slots__ = ("ap", "b", "bs")

    def __init__(self, ap, b, bs=None):
        self.ap = ap
        self.b = b
        self.bs = bs


def sp_layout():
    off = {}
    c = [0]

    def add(name, w):
        off[name] = (c[0], w)
        c[0] += w
    add("ident", 128)
    add("permT", 64)
    add("maskU", 128)
    add("maskL", 128)
    add("cond", 32)
    add("gsel", 32)
    add("gsel_own", 8)
    add("osel", 4)
    for l in range(DEPTH):
        add("bada%d" % l, 144)
        add("gn%d" % l, 48)
        add("gq%d" % l, 4)
        add("gkv%d" % l, 4)
        add("gqn_n%d" % l, 1)
        add("gqn_r%d" % l, 1)
        add("gkn_n%d" % l, 1)
        add("gkn_r%d" % l, 1)
        add("ggo%d" % l, 2)
        add("gsq%d" % l, 1)
        add("gsk%d" % l, 1)
        add("sink%d" % l, 16)
    return off, c[0]


SP_OFF, SP_N = sp_layout()
ARENA_ELEMS = 56 * 1024
XROWS, XCOLS = 128, 4608
X_CKV, X_SWV = 0, 3584
X_KR = (0, 2048)
X_SWK = [(64, 2048), (0, 2560), (64, 2560), (0, 3072)]
X_SS = (64, 3072)
YCOLS = 8 + 2048


class Builder:
    def __init__(self, do_s=True, n_layers=DEPTH, dbg=None, n_cores=8, do_p=True):
        self.do_p = do_p
        self.rgroups = [[0, 1, 2, 3], [4, 5, 6, 7]] if n_cores == 8 else [[0, 1, 2, 3]]
        self.do_s = do_s
        self.n_layers = n_layers
        self.dbg = dbg
        nc = self.nc = bass.Bass("TRN2", target_bir_lowering=False)
        self.P = Prog(nc)
        P = self.P
        din = lambda name, shape: nc.dram_tensor(name, list(shape), F32, kind="ExternalInput").ap()
        dout = lambda name, shape: nc.dram_tensor(name, list(shape), F32, kind="ExternalOutput").ap()
        self.d = d = {}
        d["xpT"] = din("xpT", [D, T])
        d["xsT"] = din("xsT", [D, 4 * T])
        d["smallp"] = din("smallp", [128, SP_N])
        d["rope"] = din("rope", [5, 64, 2, T])
        d["wg"] = din("wg", [16, DEPTH, 2, 512])
        d["bg"] = din("bg", [1, DEPTH, 2, 512])
        d["swamask"] = din("swamask", [128, 16 * 4 * 128])
        d["w_ada"] = [din("w_ada%d" % l, [D, 9 * D]) for l in range(DEPTH)]
        for nm in ("w_ff1_gu", "w_ff2_gu"):
            d[nm] = din(nm, [DEPTH, D, 2 * DFF])
        for nm in ("w_ff1_down", "w_ff2_down"):
            d[nm] = din(nm, [DEPTH, DFF, D])
        d["w_in"] = din("w_in", [DEPTH, D, IN_COLS])
        d["w_mla_uq"] = din("w_mla_uq", [DEPTH, 512, 1536])
        d["w_mla_ukv"] = din("w_mla_ukv", [DEPTH, 512, 2048])
        for nm in ("w_br_mla", "w_br_gla", "w_br_swa"):
            d[nm] = din(nm, [DEPTH, 1024, D])
        d["w_out"] = din("w_out", [DEPTH, D, D])
        d["ckv_cT"] = din("ckv_cT", [DEPTH, 512, 256])
        d["kpe_cT"] = din("kpe_cT", [DEPTH, 64, 256])
        d["swk_cT"] = din("swk_cT", [DEPTH, 64, 4, 256])
        d["swv_c"] = din("swv_c", [DEPTH, 256, 256])
        d["state"] = din("state", [DEPTH, 2, 4, 128, 256])
        d["ypT"] = dout("ypT", [D, T])
        d["ysT"] = dout("ysT", [D, T])
        d["n_ckvT"] = dout("n_ckvT", [DEPTH, 512, T])
        d["n_kpeT"] = dout("n_kpeT", [DEPTH, 64, T])
        d["n_skT"] = dout("n_skT", [DEPTH, 4, 64, T])
        d["n_sv"] = dout("n_sv", [DEPTH, T, 256])
        d["n_sg"] = dout("n_sg", [DEPTH, 2, 2, 4, 128, 256])
        if dbg is not None:
            d["dbg"] = dout("dbg", list(dbg))
        self.xscr = Tl(nc.dram_tensor("xscr", [D, 4 * T], F32).ap(), None, [Buf("xscr%d" % k) for k in range(4)])
        self.xoutA = [Tl(nc.dram_tensor("xoutA%d" % l, [4 * XROWS, XCOLS], F32).ap(), Buf("xoutA")) for l in range(DEPTH)]
        self.xoutB = [Tl(nc.dram_tensor("xoutB%d" % l, [4 * 128, YCOLS], F32).ap(), Buf("xoutB")) for l in range(DEPTH)]

        def sb(name, shape, dt):
            return Tl(nc.alloc_sbuf_tensor(name, list(shape), dt), Buf(name))
        self.xT = sb("xT", [128, NCH, T], F32)
        self.xT.bs = [Buf("xT%d" % c) for c in range(NCH)]
        self.nslots = 3
        self.slots = [sb("slot%d" % i, [128, 8192], BF16) for i in range(self.nslots)]
        self.slot_ch = [P.new_chan("slot%d" % i) for i in range(self.nslots)]
        self.slot_chs = list(self.slot_ch)
        self.slot_i = 0
        self.xslot_ch = [P.new_chan("xslot%d" % i) for i in range(4)]
        self.sp = sb("smallp_s", [128, SP_N], F32)
        self.rope = sb("rope_s", [64, 2, T], F32)
        self.identb = sb("identb", [128, 128], BF16)
        self.onesb = sb("onesb", [128, 128], BF16)
        self.onesf = sb("onesf", [33, 128], F32)
        self.maskUb = sb("maskUb", [128, 128], BF16)
        self.maskLb = sb("maskLb", [128, 128], BF16)
        self.modT = sb("modT", [128, DEPTH, 144, 2], F32)
        self.modA = sb("modA", [128, 16], F32)
        self.modG = sb("modG", [128, 16], F32)
        self.esink = sb("esink", [128, 16], F32)
        self.fscr = sb("fscr", [128, 2], F32)
        self.arena_elems = (nc.sbuf_bytes_remaining - 512) // 2 // 64 * 64
        self.arena_t = nc.alloc_sbuf_tensor("arena", [128, self.arena_elems], BF16)
        self.a_off = 0
        self.a_tok = None
        self.ps = [Tl(nc.alloc_psum_tensor("ps%d" % i, [128, 512], F32), Buf("ps%d" % i)) for i in range(8)]
        self.ch_in = P.new_chan("in")
        self.ch_out = [P.new_chan("out%d" % i) for i in range(4)]
        self.out_i = 0
        self.ch_x = P.new_chan("xchg")
        self.ch_pl = P.new_chan("pload")
        self.pinned = set()

    def alloc(self, shape, dt, name="", nb=0):
        n = int(np.prod(shape[1:]))
        sz = n * (2 if dt == F32 else 1)
        sz = (sz + 15) // 16 * 16
        assert self.a_off + sz <= self.arena_elems, ("arena overflow", name, self.a_off, sz, self.arena_elems)
        self.a_peak = max(getattr(self, "a_peak", 0), self.a_off + sz)
        v = self.arena_t[0:shape[0], self.a_off:self.a_off + sz]
        self.a_off += sz
        if dt == F32:
            v = v.bitcast(F32)
        v = v[:, 0:n]
        if len(shape) == 3:
            v = v.rearrange("p (a b) -> p a b", a=shape[1])
        elif len(shape) == 4:
            v = v.rearrange("p (a b c) -> p a b c", a=shape[1], b=shape[2])
        return Tl(v, Buf(name, self.a_tok), [Buf(name + str(k), self.a_tok) for k in range(nb)] if nb else None)

    def mark(self):
        return self.a_off

    def release(self, m):
        fs = self.fscr
        self.a_tok = self.P.fence(lambda e: e.memset(fs.ap[:, 0:1], 0.0))
        self.a_off = m

    def sps(self, name):
        o, w = SP_OFF[name]
        return self.sp.ap[:, o:o + w]

    def mm(self, out, lhsT, rhs, start, stop, reads, writes):
        self.P.op("pe", lambda e: e.matmul(out, lhsT, rhs, start=start, stop=stop), reads, writes)

    def tr(self, out, in_, ident, reads, writes):
        self.P.op("pe", lambda e: e.transpose(out, in_, ident), reads, writes)

    def act(self, out, in_, func, reads, writes, bias=None, scale=None):
        kw = {}
        if bias is not None:
            kw["bias"] = bias
        if scale is not None:
            kw["scale"] = scale
        self.P.op("act", lambda e: e.activation(out=out, in_=in_, func=func, **kw), reads, writes)

    def tt(self, out, in0, in1, op, reads, writes):
        self.P.op("dve", lambda e: e.tensor_tensor(out=out, in0=in0, in1=in1, op=op), reads, writes)

    def stt(self, out, in0, scalar, in1, op0, op1, reads, writes):
        self.P.op("dve", lambda e: e.scalar_tensor_tensor(out=out, in0=in0, scalar=scalar, in1=in1, op0=op0, op1=op1), reads, writes)

    def ts(self, out, in0, s1, s2, op0, op1, reads, writes):
        if s2 is None:
            self.P.op("dve", lambda e: e.tensor_scalar(out=out, in0=in0, scalar1=s1, scalar2=None, op0=op0), reads, writes)
        else:
            self.P.op("dve", lambda e: e.tensor_scalar(out=out, in0=in0, scalar1=s1, scalar2=s2, op0=op0, op1=op1), reads, writes)

    def recip(self, out, in_, reads, writes):
        self.P.op("dve", lambda e: e.reciprocal(out=out, in_=in_), reads, writes)

    def vcopy(self, out, in_, reads, writes):
        self.P.op("dve", lambda e: e.tensor_copy(out=out, in_=in_), reads, writes)

    def load(self, out, in_, writes, reads=(), q="sp", ch=None):
        if ch is None:
            ch = self.ch_in if q == "sp" else self.ch_pl
        self.P.dma(q, ch, lambda e: e.dma_start(out=out, in_=in_), reads, writes)

    def store(self, out, in_, reads, writes=()):
        ch = self.ch_out[self.out_i % len(self.ch_out)]
        self.out_i += 1
        self.P.dma("sp", ch, lambda e: e.dma_start(out=out, in_=in_), reads, writes)

    def wslab(self, src, kc, n):
        s = self.slot_i % len(self.slots)
        self.slot_i += 1
        sl = self.slots[s]
        ch_ = self.slot_chs[s]
        if kc is None:
            view = sl.ap[0:src.shape[0], 0:n]
        else:
            view = sl.ap[0:src.shape[0], 0:kc * n].rearrange("p (k n) -> p k n", k=kc)
        self.P.dma("pool", ch_, lambda e: e.dma_start(out=view, in_=src), (), [sl.b])
        return Tl(view, sl.b)

    def push_xslots(self, reserve):
        free = self.arena_elems - self.a_off
        n_extra = max(0, min(len(self.xslot_ch), (free - reserve) // 8192))
        for k_ in range(n_extra):
            self.slots.append(self.alloc([128, 8192], BF16, "xslot%d" % k_))
            self.slot_chs.append(self.xslot_ch[k_])

    def pop_xslots(self):
        del self.slots[self.nslots:]
        del self.slot_chs[self.nslots:]

    def rows(self, w2d, r0, nrow, c0, ncol, p=128):
        return w2d[r0:r0 + nrow, c0:c0 + ncol].rearrange("(c p) n -> p c n", p=p)

    def consts(self):
        d = self.d
        self.load(self.sp.ap[:, :], d["smallp"], [self.sp.b])
        o, _ = SP_OFF["ident"]
        self.identf = self.sp.ap[:, o:o + 128]
        self.vcopy(self.identb.ap[:, :], self.identf, [self.sp.b], [self.identb.b])
        self.vcopy(self.maskUb.ap[:, :], self.sps("maskU"), [self.sp.b], [self.maskUb.b])
        self.vcopy(self.maskLb.ap[:, :], self.sps("maskL"), [self.sp.b], [self.maskLb.b])
        ob, of = self.onesb, self.onesf
        self.P.op("dve", lambda e: e.memset(ob.ap[:, :], 1.0), (), [ob.b])
        self.P.op("dve", lambda e: e.memset(of.ap[:, :], 1.0), (), [of.b])

    def adaln(self):
        d = self.d
        m0 = self.mark()
        self.push_xslots(4 * 1024)
        scT = self.alloc([128, 16, 2], BF16, "scT")
        st = [self.alloc([2, 512], F32, "adast%d" % i) for i in range(2)]
        cond = self.sps("cond").rearrange("p (c j) -> p c j", c=16)
        self.act(scT.ap[:, :, :], cond, AF.Silu, [self.sp.b], [scT.b])
        psM = self.ps[7]
        for l in range(self.n_layers):
            wv = d["w_ada"][l].rearrange("(c p) n -> p c n", p=128)
            for sbk in range(36):
                slab = self.wslab(wv[:, :, sbk * 512:(sbk + 1) * 512], 16, 512)
                pa = self.ps[sbk % 2]
                for c in range(16):
                    self.mm(pa.ap[0:2, :], scT.ap[:, c, :], slab.ap[:, c, :], c == 0, c == 15, [scT.b, slab.b], [pa.b])
                s_ = st[sbk % 2]
                self.act(s_.ap[:, :], pa.ap[0:2, :], AF.Copy, [pa.b], [s_.b])
                for jj in range(4):
                    j = sbk * 4 + jj
                    self.tr(psM.ap[:, 2 * j:2 * j + 2], s_.ap[0:2, jj * 128:(jj + 1) * 128], self.identf[0:2, 0:2], [s_.b, self.sp.b], [psM.b])
            bo, _ = SP_OFF["bada%d" % l]
            bias = self.sp.ap[:, bo:bo + 144].unsqueeze(2).broadcast_to([128, 144, 2])
            self.tt(self.modT.ap[:, l, :, :], psM.ap[:, 0:288].rearrange("p (j c) -> p j c", c=2), bias, ALU.add, [psM.b, self.sp.b], [self.modT.b])
        self.pop_xslots()
        self.release(m0)

    def mod_setup(self, l, i, cond, half_gate):
        go, _ = SP_OFF["gn%d" % l]
        g = self.sp.ap[:, go + 16 * i:go + 16 * i + 16]
        sc = self.modT.ap[:, l, (3 * i + 1) * 16:(3 * i + 2) * 16, cond]
        gt = self.modT.ap[:, l, (3 * i + 2) * 16:(3 * i + 3) * 16, cond]
        self.stt(self.modA.ap[:, :], sc, 1.0, g, ALU.add, ALU.mult, [self.modT.b, self.sp.b], [self.modA.b])
        self.ts(self.modG.ap[:, :], gt, 0.5 if half_gate else 1.0, None, ALU.mult, None, [self.modT.b], [self.modG.b])

    def rstd_bc(self, ss_ps, n, width, tmp, out, reads):
        self.act(tmp.ap, ss_ps, AF.Sqrt, reads, [tmp.b], bias=EPS, scale=1.0 / n)
        self.recip(out.ap, tmp.ap, [tmp.b], [out.b])

    def norm_mod(self, l, i, cond, hT):
        m0 = self.mark()
        sq = [self.alloc([128, T], BF16, "sq%d" % k) for k in range(2)]
        tmp = self.alloc([128, T], F32, "nm_tmp")
        rstd = self.alloc([128, T], F32, "nm_rstd")
        tf = [self.alloc([128, T], F32, "nm_t%d" % k) for k in range(2)]
        xT = self.xT
        ssp = self.ps[6]
        for c in range(16):
            s_ = sq[c % 2]
            self.act(s_.ap[:, :], xT.ap[:, c, :], AF.Square, [xT.bs[c]], [s_.b])
            self.mm(ssp.ap[:, :], self.onesb.ap[:, :], s_.ap[:, :], c == 0, c == 15, [self.onesb.b, s_.b], [ssp.b])
        self.rstd_bc(ssp.ap[:, :], D, T, Tl(tmp.ap[:, :], tmp.b), Tl(rstd.ap[:, :], rstd.b), [ssp.b])
        for c in range(16):
            t_ = tf[c % 2]
            self.stt(t_.ap[:, :], xT.ap[:, c, :], self.modA.ap[:, c:c + 1], rstd.ap[:, :], ALU.mult, ALU.mult, [xT.bs[c], self.modA.b, rstd.b], [t_.b])
            self.act(hT.ap[:, c, :], t_.ap[:, :], AF.Identity, [t_.b, self.modT.b], [hT.bs[c]], bias=self.modT.ap[:, l, 3 * i * 16 + c, cond:cond + 1])
        self.release(m0)

    def ffn(self, l, which, cond, after_dq=None):
        d = self.d
        wgu = d["w_ff%d_gu" % which][l]
        wdn = d["w_ff%d_down" % which][l]
        i = 0 if which == 1 else 2
        m0 = self.mark()
        self.push_xslots(33 * 1024)
        hT = self.alloc([128, 16, T], BF16, "hT", nb=16)
        self.mod_setup(l, i, cond, True)
        self.norm_mod(l, i, cond, hT)
        actT = self.alloc([128, NFC, T], BF16, "actT", nb=NFC)
        sil = [self.alloc([128, T], F32, "sil%d" % k) for k in range(2)]
        k = 0
        for fb in range(11):
            sa = self.wslab(self.rows(wgu, 0, D, fb * 512, 512), 16, 512)
            su = self.wslab(self.rows(wgu, 0, D, DFF + fb * 512, 512), 16, 512)
            for j in range(4):
                fc = fb * 4 + j
                pa = self.ps[(k % 2) * 2]
                pu = self.ps[(k % 2) * 2 + 1]
                for c in range(16):
                    self.mm(pa.ap[:, :], sa.ap[:, c, j * 128:(j + 1) * 128], hT.ap[:, c, :], c == 0, c == 15, [sa.b, hT.bs[c]], [pa.b])
                for c in range(16):
                    self.mm(pu.ap[:, :], su.ap[:, c, j * 128:(j + 1) * 128], hT.ap[:, c, :], c == 0, c == 15, [su.b, hT.bs[c]], [pu.b])
                s_ = sil[k % 2]
                self.act(s_.ap[:, :], pa.ap[:, :], AF.Silu, [pa.b], [s_.b])
                self.tt(actT.ap[:, fc, :], s_.ap[:, :], pu.ap[:, :], ALU.mult, [s_.b, pu.b], [actT.bs[fc]])
                k += 1
        xT = self.xT
        for dq in range(4):
            for sl in range(3):
                nk = 16 if sl < 2 else 12
                sw = self.wslab(self.rows(wdn, sl * 2048, nk * 128, dq * 512, 512), nk, 512)
                for kc in range(nk):
                    fc = sl * 16 + kc
                    for dj in range(4):
                        po = self.ps[4 + dj]
                        self.mm(po.ap[:, :], sw.ap[:, kc, dj * 128:(dj + 1) * 128], actT.ap[:, fc, :], fc == 0, fc == NFC - 1, [sw.b, actT.bs[fc]], [po.b])
            for dj in range(4):
                ch = dq * 4 + dj
                po = self.ps[4 + dj]
                self.stt(xT.ap[:, ch, :], po.ap[:, :], self.modG.ap[:, ch:ch + 1], xT.ap[:, ch, :], ALU.mult, ALU.add, [po.b, self.modG.b, xT.bs[ch]], [xT.bs[ch]])
            if after_dq is not None:
                after_dq(dq)
        self.pop_xslots()
        self.release(m0)

    def nps(self):
        while True:
            self._psi = getattr(self, "_psi", 0) + 1
            if (self._psi % 8) not in self.pinned:
                return self.ps[self._psi % 8]

    def pps(self):
        t = self.nps()
        self.pinned.add(self._psi % 8)
        return t

    def unpin(self, *tls):
        for t in tls:
            self.pinned.discard(self.ps.index(t))

    def rope_apply(self, x, out_ap, out_b, tmp1, tmp2):
        pp = self.nps()
        po, _ = SP_OFF["permT"]
        self.mm(pp.ap[0:64, :], self.sp.ap[0:64, po:po + 64], x.ap, True, True, [self.sp.b, x.b], [pp.b])
        self.tt(tmp1.ap, x.ap, self.rope.ap[:, 0, :], ALU.mult, [x.b, self.rope.b], [tmp1.b])
        self.tt(tmp2.ap, pp.ap[0:64, :], self.rope.ap[:, 1, :], ALU.mult, [pp.b, self.rope.b], [tmp2.b])
        self.tt(out_ap, tmp1.ap, tmp2.ap, ALU.add, [tmp1.b, tmp2.b], [out_b])

    def sumsq_rstd(self, parts_list, n, rstd, tmp, width=T, extra=None, m=128):
        ss = self.nps()
        nmm = len(parts_list) + (len(extra) if extra else 0)
        k = 0
        for (src, sb_, npart) in parts_list:
            sq = self.sqt[self._sqi % 2]
            self._sqi += 1
            self.act(sq.ap[0:npart, 0:width], src, AF.Square, [sb_], [sq.b])
            self.mm(ss.ap[0:m, 0:width], self.onesb.ap[0:npart, 0:m], sq.ap[0:npart, 0:width], k == 0, k == nmm - 1, [self.onesb.b, sq.b], [ss.b])
            k += 1
        if extra:
            for (lh, rh, rd) in extra:
                self.mm(ss.ap[0:m, 0:width], lh, rh, k == 0, k == nmm - 1, rd, [ss.b])
                k += 1
        self.act(tmp.ap[0:m, 0:width], ss.ap[0:m, 0:width], AF.Sqrt, [ss.b], [tmp.b], bias=EPS, scale=1.0 / n)
        self.recip(rstd.ap[0:m, 0:width], tmp.ap[0:m, 0:width], [tmp.b], [rstd.b])

    def mixer(self, l, cond, is_s, phase=0, tg=0, after_norm=None):
        self.tg = tg
        m0 = self.mark()
        hT = self.alloc([128, 16, T], BF16, "hT", nb=16)
        self.hT = hT
        self.mod_setup(l, 1, cond, False)
        self.norm_mod(l, 1, cond, hT)
        if after_norm is not None:
            after_norm()
        oT_gla = None if (is_s and phase == 1) else self.alloc([128, 8, T], BF16, "oT_gla")
        self.sqt = [self.alloc([128, T], BF16, "sqt%d" % k) for k in range(2)]
        self._sqi = 0
        self.rtmp = self.alloc([128, T], F32, "rtmp")
        self.rstd = [self.alloc([128, T], F32, "rstd%d" % k) for k in range(2)]
        self._ri = 0
        if is_s and phase == 1:
            m1 = self.mark()
            self.stage_kv(l, True)
            self.release(m1)
            self.stage_gla(l, True, True, oT_gla)
            self.release(m0)
            return
        if not is_s:
            self.stage_gla(l, False, False, oT_gla)
            self.stage_kv(l, False)
        oT_mla = self.alloc([128, 8, T], BF16, "oT_mla")
        self.stage_mla(l, is_s, oT_mla)
        if is_s:
            self.stage_gla(l, True, False, oT_gla)
        oT_swa = self.alloc([64, 16, T], BF16, "oT_swa")
        self.stage_swa(l, is_s, oT_swa)
        self.stage_merge(l, oT_mla, oT_gla, oT_swa)
        self.release(m0)

    def next_rstd(self):
        self._ri += 1
        return self.rstd[self._ri % 2]

    def stage_kv(self, l, is_s):
        d = self.d
        win = d["w_in"][l]
        hT = self.hT
        if not is_s:
            self.ckvb = self.alloc([128, 4, T], BF16, "ckvb")
            self.kpef = self.alloc([64, T], F32, "kpef")
            self.sqk = self.alloc([64, T], BF16, "sqk")
            self.swkb = self.alloc([64, 4, T], BF16, "swkb")
            self.swvb = self.alloc([128, 4, 256], BF16, "swvb")
            kpef = self.kpef
        else:
            kpef = self.alloc([64, T], F32, "kpef")
            sqk = self.alloc([64, T], BF16, "sqk_s")
        m1 = self.mark()
        ckvf = self.alloc([128, 4, T], F32, "ckvf")
        skn = [self.alloc([64, T], F32, "skn%d" % k) for k in range(2)]
        svf = self.alloc([128, 4, 256], F32, "svf")
        rt1 = self.alloc([64, T], F32, "rt1")
        rt2 = self.alloc([64, T], F32, "rt2")
        rt3 = self.alloc([64, T], F32, "rt3")
        xo_ = self.xoutA[l]
        xin = Tl(xo_.ap[self.tg * 128:(self.tg + 1) * 128, :], xo_.b)
        slab = self.wslab(self.rows(win, 0, D, O_CKV, 512), 16, 512)
        pcs = [self.pps() for k in range(4)]
        for c4 in range(4):
            for c in range(16):
                self.mm(pcs[c4].ap[:, :], slab.ap[:, c, c4 * 128:(c4 + 1) * 128], hT.ap[:, c, :], c == 0, c == 15, [slab.b, hT.bs[c]], [pcs[c4].b])
        rs = self.next_rstd()
        self.sumsq_rstd([(pcs[c4].ap[:, :], pcs[c4].b, 128) for c4 in range(4)], 512, rs, self.rtmp)
        go, _ = SP_OFF["gkv%d" % l]
        for c4 in range(4):
            self.stt(ckvf.ap[:, c4, :], pcs[c4].ap[:, :], self.sp.ap[:, go + c4:go + c4 + 1], rs.ap[:, :], ALU.mult, ALU.mult, [pcs[c4].b, self.sp.b, rs.b], [ckvf.b])
        if not is_s:
            self.store(d["n_ckvT"][l].rearrange("(c p) t -> p c t", p=128), ckvf.ap[:, :, :], [ckvf.b])
            self.act(self.ckvb.ap[:, :, :], ckvf.ap[:, :, :], AF.Copy, [ckvf.b], [self.ckvb.b])
        else:
            self.store(xin.ap[:, X_CKV:X_CKV + 2048].rearrange("p (c t) -> p c t", c=4), ckvf.ap[:, :, :], [ckvf.b], [xin.b])
        self.unpin(*pcs)
        slab = self.wslab(self.rows(win, 0, D, O_KPE, 64), 16, 64)
        pk = self.nps()
        for c in range(16):
            self.mm(pk.ap[0:64, :], slab.ap[:, c, :], hT.ap[:, c, :], c == 0, c == 15, [slab.b, hT.bs[c]], [pk.b])
        self.act(kpef.ap[:, :], pk.ap[0:64, :], AF.Copy, [pk.b], [kpef.b])
        if not is_s:
            self.store(d["n_kpeT"][l], kpef.ap[:, :], [kpef.b])
            self.act(self.sqk.ap[:, :], kpef.ap[:, :], AF.Square, [kpef.b], [self.sqk.b])
        else:
            self.act(sqk.ap[:, :], kpef.ap[:, :], AF.Square, [kpef.b], [sqk.b])
            pr = self.nps()
            self.mm(pr.ap[0:1, :], self.onesb.ap[0:64, 0:1], sqk.ap[:, :], True, True, [self.onesb.b, sqk.b], [pr.b])
            self.act(rt3.ap[0:1, :], pr.ap[0:1, :], AF.Copy, [pr.b], [rt3.b])
            self.store(xin.ap[X_SS[0]:X_SS[0] + 1, X_SS[1]:X_SS[1] + T], rt3.ap[0:1, :], [rt3.b], [xin.b])
            self.P.op("dve", lambda e: e.memset(rt1.ap[:, :], 0.0), (), [rt1.b])
            self.store(xin.ap[X_SS[0] + 1:128, X_SS[1]:X_SS[1] + T], rt1.ap[0:63, :], [rt1.b], [xin.b])
            go, _ = SP_OFF["gkn_r%d" % l]
            kg = skn[0]
            self.ts(kg.ap[:, :], kpef.ap[:, :], self.sp.ap[0:64, go:go + 1], None, ALU.mult, None, [kpef.b, self.sp.b], [kg.b])
            self.rope_apply(Tl(kg.ap[:, :], kg.b), skn[1].ap[:, :], skn[1].b, Tl(rt1.ap[:, :], rt1.b), Tl(rt2.ap[:, :], rt2.b))
            self.store(xin.ap[X_KR[0]:X_KR[0] + 64, X_KR[1]:X_KR[1] + T], skn[1].ap[:, :], [skn[1].b], [xin.b])
        slab = self.wslab(self.rows(win, 0, D, O_SK, 512), 16, 512)
        go, _ = SP_OFF["gsk%d" % l]
        for hk in range(4):
            pk = self.nps()
            for c in range(16):
                self.mm(pk.ap[0:64, :], slab.ap[:, c, hk * 64:(hk + 1) * 64], hT.ap[:, c, :], c == 0, c == 15, [slab.b, hT.bs[c]], [pk.b])
            rs = self.next_rstd()
            self.sumsq_rstd([(pk.ap[0:64, :], pk.b, 64)], 64, rs, self.rtmp, m=64)
            sk_ = skn[hk % 2]
            self.stt(sk_.ap[:, :], pk.ap[0:64, :], self.sp.ap[0:64, go:go + 1], rs.ap[0:64, :], ALU.mult, ALU.mult, [pk.b, self.sp.b, rs.b], [sk_.b])
            if not is_s:
                self.store(d["n_skT"][l, hk], sk_.ap[:, :], [sk_.b])
                self.act(self.swkb.ap[:, hk, :], sk_.ap[:, :], AF.Copy, [sk_.b], [self.swkb.b])
            else:
                self.rope_apply(Tl(sk_.ap[:, :], sk_.b), rt3.ap[:, :], rt3.b, Tl(rt1.ap[:, :], rt1.b), Tl(rt2.ap[:, :], rt2.b))
                self.store(xin.ap[X_SWK[hk][0]:X_SWK[hk][0] + 64, X_SWK[hk][1]:X_SWK[hk][1] + T], rt3.ap[:, :], [rt3.b], [xin.b])
        for st in range(4):
            pv = self.nps()
            for c in range(16):
                self.mm(pv.ap[:, 0:256], hT.ap[:, c, st * 128:(st + 1) * 128], slab.ap[:, c, 256:512], c == 0, c == 15, [slab.b, hT.bs[c]], [pv.b])
            self.act(svf.ap[:, st, :], pv.ap[:, 0:256], AF.Copy, [pv.b], [svf.b])
        if not is_s:
            self.store(d["n_sv"][l].rearrange("(a p) n -> p a n", p=128), svf.ap[:, :, :], [svf.b])
            self.vcopy(self.swvb.ap[:, :, :], svf.ap[:, :, :], [svf.b], [self.swvb.b])
        else:
            self.store(xin.ap[:, X_SWV:X_SWV + 1024].rearrange("p (a n) -> p a n", a=4), svf.ap[:, :, :], [svf.b], [xin.b])
        if not is_s:
            self.release(m1)

    def stage_gla(self, l, is_s, state_only, oT_gla):
        d = self.d
        win = d["w_in"][l]
        hT = self.hT
        full = not state_only
        m1 = self.mark()
        if full:
            oT = self.alloc([128, 8, T], F32, "oTg")
        Sf = self.alloc([128, 2, 4, 256], F32, "Sf")
        Sb = self.alloc([128, 2, 4, 256], BF16, "Sb")
        tS = self.alloc([128, 4, 256], F32, "tS")
        if is_s and full:
            mi = self.mark()
            self.gla_init_states(l, Sf, Sb, tS)
            self.release(mi)
        m2 = self.mark()
        kg = self.alloc([128, 4, T], BF16, "kg")
        vtm = self.alloc([128, 4, 1024], BF16, "vtm")
        la = self.alloc([128, 4, 512], F32, "la")
        ggT = self.alloc([17, 2, T], F32, "ggT")
        wgt = self.alloc([17, 2, 512], F32, "wgt")
        nbuf = 2 if state_only else 1
        ebs = [self.alloc([128, 4, 128], F32, "eb%d" % k_) for k_ in range(nbuf)]
        enbs = [self.alloc([128, 4, 128], F32, "enb%d" % k_) for k_ in range(nbuf)]
        kts = [self.alloc([128, 4, 128], BF16, "kt%d" % k_) for k_ in range(nbuf)]
        ktms = [self.alloc([128, 4, 128], BF16, "ktm%d" % k_) for k_ in range(nbuf)]
        it_ = 0
        cum = self.alloc([128, 2, 128], F32, "cum")
        PA = self.alloc([128, 2, 4], F32, "PA")
        if full:
            qg = self.alloc([128, 4, T], BF16, "qg")
            qt = self.alloc([128, 4, 128], BF16, "qt")
            ATs = self.alloc([128, 4, 128], BF16, "ATs")
        self.ts(cum.ap[:, 0, :], self.sps("maskU"), 1.0 / 16, None, ALU.mult, None, [self.sp.b], [cum.b])
        self.ts(cum.ap[:, 1, :], self.sps("maskL"), 1.0 / 16, None, ALU.mult, None, [self.sp.b], [cum.b])
        self.load(wgt.ap[0:16, :, :], d["wg"][:, l, :, :], [wgt.b])
        self.load(wgt.ap[16:17, :, :], d["bg"][:, l, :, :], [wgt.b])
        self.P.op("dve", lambda e: e.memset(ggT.ap[:, :, :], 1.0), (), [ggT.b])
        if full:
            slab = self.wslab(self.rows(win, 0, D, O_GQ, 512), 16, 512)
            for h in range(4):
                pq = self.nps()
                for c in range(16):
                    self.mm(pq.ap[:, :], slab.ap[:, c, h * 128:(h + 1) * 128], hT.ap[:, c, :], c == 0, c == 15, [slab.b, hT.bs[c]], [pq.b])
                self.ts(qg.ap[:, h, :], pq.ap[:, :], GLA_QS, None, ALU.mult, None, [pq.b], [qg.b])
        slab = self.wslab(self.rows(win, 0, D, O_GK, 512), 16, 512)
        for h in range(4):
            pq = self.nps()
            for c in range(16):
                self.mm(pq.ap[:, :], slab.ap[:, c, h * 128:(h + 1) * 128], hT.ap[:, c, :], c == 0, c == 15, [slab.b, hT.bs[c]], [pq.b])
            self.vcopy(kg.ap[:, h, :], pq.ap[:, :], [pq.b], [kg.b])
        for half in range(2):
            slab = self.wslab(self.rows(win, 0, D, O_GV + half * 512, 512), 16, 512)
            for st in range(4):
                pv = self.nps()
                for c in range(16):
                    self.mm(pv.ap[:, :], hT.ap[:, c, st * 128:(st + 1) * 128], slab.ap[:, c, :], c == 0, c == 15, [slab.b, hT.bs[c]], [pv.b])
                if st % 2 == 0:
                    self.act(vtm.ap[:, st, half * 512:(half + 1) * 512], pv.ap[:, :], AF.Copy, [pv.b], [vtm.b])
                else:
                    self.vcopy(vtm.ap[:, st, half * 512:(half + 1) * 512], pv.ap[:, :], [pv.b], [vtm.b])
        slab = self.wslab(self.rows(win, 0, D, O_GGF, 32), 16, 32)
        for dr in range(2):
            pg = self.nps()
            for c in range(16):
                self.mm(pg.ap[0:16, :], slab.ap[:, c, dr * 16:(dr + 1) * 16], hT.ap[:, c, :], c == 0, c == 15, [slab.b, hT.bs[c]], [pg.b])
            self.vcopy(ggT.ap[0:16, dr, :], pg.ap[0:16, :], [pg.b], [ggT.b])
        seqs = [[0, 1, 2, 3]] if is_s else [[0, 1], [2, 3]]
        first_o = [True] * 4
        st2 = ""
        if st2 == "proj":
            self.release(m1)
            return
        for dr in range(2):
            for st in range(4):
                pl = self.nps()
                self.mm(pl.ap[:, :], ggT.ap[0:17, dr, st * 128:(st + 1) * 128], wgt.ap[0:17, dr, :], True, True, [ggT.b, wgt.b], [pl.b])
                self.act(la.ap[:, st, :], pl.ap[:, :], AF.Sigmoid, [pl.b], [la.b])
            self.act(la.ap[:, :, :], la.ap[:, :, :], AF.Ln, [la.b], [la.b])
            if st2 == "la":
                self.release(m1)
                return
            for si, seq in enumerate(seqs):
                order = seq if dr == 0 else seq[::-1]
                zero_state = not (is_s and full)
                for n in order:
                    eb, enb, kt, ktm = ebs[it_ % nbuf], enbs[it_ % nbuf], kts[it_ % nbuf], ktms[it_ % nbuf]
                    it_ += 1
                    csl = slice(n * 128, (n + 1) * 128)
                    pb = self.nps()
                    for h in range(4):
                        self.mm(pb.ap[:, h * 128:(h + 1) * 128], la.ap[:, n, h * 128:(h + 1) * 128], cum.ap[:, dr, :], True, True, [la.b, cum.b], [pb.b])
                    pbv = pb.ap[:, :].rearrange("p (h t) -> p h t", h=4)
                    self.act(enb.ap[:, :, :], pbv, AF.Exp, [pb.b], [enb.b], scale=-1.0)
                    self.act(eb.ap[:, :, :], pbv, AF.Exp, [pb.b], [eb.b])
                    ebl = eb.ap[:, :, 127] if dr == 0 else eb.ap[:, :, 0]
                    self.tt(kt.ap[:, :, :], kg.ap[:, :, csl], enb.ap[:, :, :], ALU.mult, [kg.b, enb.b], [kt.b])
                    if full:
                        self.tt(qt.ap[:, :, :], qg.ap[:, :, csl], eb.ap[:, :, :], ALU.mult, [qg.b, eb.b], [qt.b])
                        pa = self.nps()
                        for h in range(4):
                            self.mm(pa.ap[:, h * 128:(h + 1) * 128], kt.ap[:, h, :], qt.ap[:, h, :], True, True, [kt.b, qt.b], [pa.b])
                        mk = self.maskUb if dr == 0 else self.maskLb
                        self.tt(ATs.ap[:, :, :], pa.ap[:, :].rearrange("p (h t) -> p h t", h=4), mk.ap[:, :].unsqueeze(1).broadcast_to([128, 4, 128]), ALU.mult, [pa.b, mk.b], [ATs.b])
                        po = [self.pps(), self.pps()]
                        for h in range(4):
                            for vc in range(2):
                                j = h * 2 + vc
                                dst = po[j // 4].ap[:, (j % 4) * 128:(j % 4 + 1) * 128]
                                self.mm(dst, vtm.ap[:, n, h * 256 + vc * 128:h * 256 + (vc + 1) * 128], ATs.ap[:, h, :], True, zero_state, [vtm.b, ATs.b], [po[j // 4].b])
                                if not zero_state:
                                    self.mm(dst, Sb.ap[:, dr, h, vc * 128:(vc + 1) * 128], qt.ap[:, h, :], False, True, [Sb.b, qt.b], [po[j // 4].b])
                        for hf in range(2):
                            ov = oT.ap[:, hf * 4:(hf + 1) * 4, csl]
                            pv_ = po[hf].ap[:, :].rearrange("p (j t) -> p j t", j=4)
                            if first_o[n]:
                                self.act(ov, pv_, AF.Copy, [po[hf].b], [oT.b])
                            else:
                                self.tt(ov, ov, pv_, ALU.add, [oT.b, po[hf].b], [oT.b])
                        first_o[n] = False
                        self.unpin(*po)
                    pt = self.nps()
                    ptb = pt.ap[:, 0:256].bitcast(BF16)
                    for h in range(4):
                        self.tr(ptb[:, h * 128:(h + 1) * 128], kt.ap[:, h, :], self.identb.ap[:, :], [kt.b, self.identb.b], [pt.b])
                    self.act(ktm.ap[:, :, :], ptb.rearrange("p (h k) -> p h k", h=4), AF.Copy, [pt.b], [ktm.b])
                    pd = [self.pps(), self.pps()]
                    for h in range(4):
                        self.mm(pd[h // 2].ap[:, (h % 2) * 256:(h % 2 + 1) * 256], ktm.ap[:, h, :], vtm.ap[:, n, h * 256:(h + 1) * 256], True, True, [ktm.b, vtm.b], [pd[h // 2].b])
                    eblb = ebl.unsqueeze(2).broadcast_to([128, 4, 256])
                    for hf in range(2):
                        sv_ = Sf.ap[:, dr, hf * 2:(hf + 1) * 2, :]
                        dv_ = pd[hf].ap[:, :].rearrange("p (h v) -> p h v", h=2)
                        ev_ = ebl[:, hf * 2:(hf + 1) * 2].unsqueeze(2).broadcast_to([128, 2, 256])
                        tv_ = tS.ap[:, hf * 2:(hf + 1) * 2, :]
                        if zero_state:
                            self.tt(sv_, dv_, ev_, ALU.mult, [pd[hf].b, eb.b], [Sf.b])
                        else:
                            self.tt(tv_, dv_, sv_, ALU.add, [pd[hf].b, Sf.b], [tS.b])
                            self.tt(sv_, tv_, ev_, ALU.mult, [tS.b, eb.b], [Sf.b])
                    self.unpin(*pd)
                    if full:
                        self.act(Sb.ap[:, dr, :, :], Sf.ap[:, dr, :, :], AF.Copy, [Sf.b], [Sb.b])
                    if state_only:
                        if n == order[0]:
                            self.vcopy(PA.ap[:, dr, :], ebl, [eb.b], [PA.b])
                        else:
                            self.tt(PA.ap[:, dr, :], PA.ap[:, dr, :], ebl, ALU.mult, [PA.b, eb.b], [PA.b])
                    zero_state = False
                if not is_s:
                    self.store(d["n_sg"][l, si, dr].rearrange("h k v -> k h v"), Sf.ap[:, dr, :, :], [Sf.b])
        if state_only:
            xo_ = self.xoutB[l]
            xin = Tl(xo_.ap[self.tg * 128:(self.tg + 1) * 128, :], xo_.b)
            self.store(xin.ap[:, 0:8], PA.ap[:, :, :].rearrange("p a b -> p (a b)"), [PA.b], [xin.b])
            self.store(xin.ap[:, 8:8 + 2048], Sf.ap[:, :, :, :].rearrange("p a b c -> p (a b c)"), [Sf.b], [xin.b])
        self.release(m2)
        if full:
            go, _ = SP_OFF["ggo%d" % l]
            on = [self.alloc([128, T], F32, "on%d" % k) for k in range(2)]
            sg = [self.alloc([128, T], F32, "sg%d" % k) for k in range(2)]
            slabs = [self.wslab(self.rows(win, 0, D, O_GOUT + half * 512, 512), 16, 512) for half in range(2)]
            for h in range(4):
                rs = self.next_rstd()
                self.sumsq_rstd([(oT.ap[:, h * 2 + vc, :], oT.b, 128) for vc in range(2)], 256, rs, self.rtmp)
                for vc in range(2):
                    j = h * 2 + vc
                    pg = self.nps()
                    sl = slabs[j // 4]
                    for c in range(16):
                        self.mm(pg.ap[:, :], sl.ap[:, c, (j % 4) * 128:(j % 4 + 1) * 128], hT.ap[:, c, :], c == 0, c == 15, [sl.b, hT.bs[c]], [pg.b])
                    self.act(sg[j % 2].ap[:, :], pg.ap[:, :], AF.Silu, [pg.b], [sg[j % 2].b])
                    self.stt(on[j % 2].ap[:, :], oT.ap[:, j, :], self.sp.ap[:, go + vc:go + vc + 1], rs.ap[:, :], ALU.mult, ALU.mult, [oT.b, self.sp.b, rs.b], [on[j % 2].b])
                    self.tt(oT_gla.ap[:, j, :], on[j % 2].ap[:, :], sg[j % 2].ap[:, :], ALU.mult, [on[j % 2].b, sg[j % 2].b], [oT_gla.b])
        self.release(m1)

    def gla_init_states(self, l, Sf, Sb, tS):
        d = self.d
        xo = self.xoutB[l]
        F = self.alloc([128, 4, 256], F32, "glaF")
        Sl = self.alloc([128, 4, 256], F32, "glaSl")
        Aj = self.alloc([128, 4, 8], F32, "glaA")
        go, _ = SP_OFF["gsel"]
        self.load(Aj.ap[:, :, :], xo.ap[:, 0:8].rearrange("(r p) c -> p r c", p=128), [Aj.b], [xo.b])
        for dr in range(2):
            self.load(F.ap[:, :, :], d["state"][l, dr].rearrange("h k v -> k h v"), [F.b])
            order = [0, 1, 2, 3] if dr == 0 else [3, 2, 1, 0]
            for idx, j in enumerate(order):
                if self.tg == "own":
                    go_ = SP_OFF["gsel_own"][0]
                else:
                    go_ = go + self.tg * 8
                sel = self.sp.ap[:, go_ + dr * 4 + j:go_ + dr * 4 + j + 1]
                sv_ = Sf.ap[:, dr, :, :]
                if idx == 0:
                    self.ts(sv_, F.ap[:, :, :], sel, None, ALU.mult, None, [F.b, self.sp.b], [Sf.b])
                else:
                    self.stt(sv_, F.ap[:, :, :], sel, sv_, ALU.mult, ALU.add, [F.b, self.sp.b, Sf.b], [Sf.b])
                if idx < 3:
                    self.load(Sl.ap[:, :, :], xo.ap[j * 128:(j + 1) * 128, 8 + dr * 1024:8 + (dr + 1) * 1024].rearrange("p (h v) -> p h v", h=4), [Sl.b], [xo.b])
                    av = Aj.ap[:, j, dr * 4:(dr + 1) * 4].unsqueeze(2).broadcast_to([128, 4, 256])
                    self.tt(tS.ap[:, :, :], F.ap[:, :, :], av, ALU.mult, [F.b, Aj.b], [tS.b])
                    self.tt(F.ap[:, :, :], tS.ap[:, :, :], Sl.ap[:, :, :], ALU.add, [tS.b, Sl.b], [F.b])
            self.act(Sb.ap[:, dr, :, :], Sf.ap[:, dr, :, :], AF.Copy, [Sf.b], [Sb.b])

    def stage_mla(self, l, is_s, oT_mla):
        d = self.d
        win = d["w_in"][l]
        hT = self.hT
        m1 = self.mark()
        NK = 2304 if is_s else T
        cqn = self.alloc([128, 4, T], BF16, "cqn")
        qn = self.alloc([128, 8, T], BF16, "qn")
        qr = self.alloc([64, 8, T], BF16, "qr")
        knT = self.alloc([128, NK], BF16, "knT")
        krT = self.alloc([64, NK], BF16, "krT")
        vh = self.alloc([128, NK // 128, 128], BF16, "vh")
        PT = [self.alloc([128, T], BF16, "PT%d" % k) for k in range(3)]
        rd = self.alloc([128, T], F32, "rd")
        if is_s:
            ckva = self.alloc([128, 4, NK], BF16, "ckva")
            krx = self.alloc([64, NK], BF16, "krx")
            ssr = self.alloc([1, NK], BF16, "ssr")
            kpc = self.alloc([64, 256], F32, "kpc")
            sqc = self.alloc([64, 256], BF16, "sqc")
            qrf = self.alloc([64, T], F32, "qrf")
            rt1 = self.alloc([64, T], F32, "mrt1")
            rt2 = self.alloc([64, T], F32, "mrt2")
            ckv_src = ckva
        else:
            ckv_src = self.ckvb
        slab = self.wslab(self.rows(win, 0, D, O_CQ, 512), 16, 512)
        pcs = [self.pps() for k in range(4)]
        for c4 in range(4):
            for c in range(16):
                self.mm(pcs[c4].ap[:, :], slab.ap[:, c, c4 * 128:(c4 + 1) * 128], hT.ap[:, c, :], c == 0, c == 15, [slab.b, hT.bs[c]], [pcs[c4].b])
        rs = self.next_rstd()
        self.sumsq_rstd([(pcs[c4].ap[:, :], pcs[c4].b, 128) for c4 in range(4)], 512, rs, self.rtmp)
        go, _ = SP_OFF["gq%d" % l]
        for c4 in range(4):
            self.stt(cqn.ap[:, c4, :], pcs[c4].ap[:, :], self.sp.ap[:, go + c4:go + c4 + 1], rs.ap[:, :], ALU.mult, ALU.mult, [pcs[c4].b, self.sp.b, rs.b], [cqn.b])
        self.unpin(*pcs)
        wq = self.wslab(d["w_mla_uq"][l].rearrange("(c p) n -> p c n", p=128), 4, 1536)
        gn, _ = SP_OFF["gqn_n%d" % l]
        gr, _ = SP_OFF["gqn_r%d" % l]
        for h in range(8):
            pn = self.pps()
            pr = self.pps()
            for kc in range(4):
                self.mm(pn.ap[:, :], wq.ap[:, kc, h * 192:h * 192 + 128], cqn.ap[:, kc, :], kc == 0, kc == 3, [wq.b, cqn.b], [pn.b])
            for kc in range(4):
                self.mm(pr.ap[0:64, :], wq.ap[:, kc, h * 192 + 128:h * 192 + 192], cqn.ap[:, kc, :], kc == 0, kc == 3, [wq.b, cqn.b], [pr.b])
            rs = self.next_rstd()
            self.sumsq_rstd([(pn.ap[:, :], pn.b, 128), (pr.ap[0:64, :], pr.b, 64)], 192, rs, self.rtmp)
            self.stt(qn.ap[:, h, :], pn.ap[:, :], self.sp.ap[:, gn:gn + 1], rs.ap[:, :], ALU.mult, ALU.mult, [pn.b, self.sp.b, rs.b], [qn.b])
            if is_s:
                self.stt(qrf.ap[:, :], pr.ap[0:64, :], self.sp.ap[0:64, gr:gr + 1], rs.ap[0:64, :], ALU.mult, ALU.mult, [pr.b, self.sp.b, rs.b], [qrf.b])
                self.rope_apply(Tl(qrf.ap[:, :], qrf.b), qr.ap[:, h, :], qr.b, Tl(rt1.ap[:, :], rt1.b), Tl(rt2.ap[:, :], rt2.b))
            else:
                self.stt(qr.ap[:, h, :], pr.ap[0:64, :], self.sp.ap[0:64, gr:gr + 1], rs.ap[0:64, :], ALU.mult, ALU.mult, [pr.b, self.sp.b, rs.b], [qr.b])
            self.unpin(pn, pr)
        wkv = self.wslab(d["w_mla_ukv"][l].rearrange("(c p) n -> p c n", p=128), 4, 2048)
        if is_s:
            xo = self.xoutA[l]
            xr = xo.ap.rearrange("(r p) c -> p r c", p=128)
            for c_ in range(4):
                self.load(ckva.ap[:, c_, 0:2048].rearrange("p (r t) -> p r t", r=4), xr[:, :, X_CKV + c_ * T:X_CKV + (c_ + 1) * T], [ckva.b], [xo.b], q="pool")
            self.load(krx.ap[0:64, 0:2048].rearrange("p (r t) -> p r t", r=4), xr[X_KR[0]:X_KR[0] + 64, :, X_KR[1]:X_KR[1] + T], [krx.b], [xo.b], q="pool")
            self.load(ssr.ap[0:1, 0:2048].rearrange("p (r t) -> p r t", r=4), xr[X_SS[0]:X_SS[0] + 1, :, X_SS[1]:X_SS[1] + T], [ssr.b], [xo.b], q="pool")
            self.load(ckva.ap[:, :, 2048:2304], d["ckv_cT"][l].rearrange("(c p) t -> p c t", p=128), [ckva.b], q="pool")
            self.load(kpc.ap[:, :], d["kpe_cT"][l], [kpc.b])
            self.act(sqc.ap[:, :], kpc.ap[:, :], AF.Square, [kpc.b], [sqc.b])
            go, _ = SP_OFF["gkn_r%d" % l]
            self.ts(kpc.ap[:, :], kpc.ap[:, :], self.sp.ap[0:64, go:go + 1], None, ALU.mult, None, [kpc.b, self.sp.b], [kpc.b])
        gn, _ = SP_OFF["gkn_n%d" % l]
        gr, _ = SP_OFF["gkn_r%d" % l]
        blocks = [(n0, min(512, NK - n0)) for n0 in range(0, NK, 512)]
        for h in range(8):
            for (n0, nb) in blocks:
                pk = self.pps()
                for kc in range(4):
                    self.mm(pk.ap[:, 0:nb], wkv.ap[:, kc, h * 256:h * 256 + 128], ckv_src.ap[:, kc, n0:n0 + nb], kc == 0, kc == 3, [wkv.b, ckv_src.b], [pk.b])
                rs = self.next_rstd()
                if not is_s:
                    extra = [(self.onesb.ap[0:64, :], self.sqk.ap[:, n0:n0 + nb], [self.onesb.b, self.sqk.b])]
                elif n0 < 2048:
                    extra = [(self.onesb.ap[0:1, :], ssr.ap[0:1, n0:n0 + nb], [self.onesb.b, ssr.b])]
                else:
                    extra = [(self.onesb.ap[0:64, :], sqc.ap[:, :], [self.onesb.b, sqc.b])]
                self.sumsq_rstd([(pk.ap[:, 0:nb], pk.b, 128)], 192, rs, self.rtmp, width=nb, extra=extra)
                self.stt(knT.ap[:, n0:n0 + nb], pk.ap[:, 0:nb], self.sp.ap[:, gn:gn + 1], rs.ap[:, 0:nb], ALU.mult, ALU.mult, [pk.b, self.sp.b, rs.b], [knT.b])
                self.unpin(pk)
                if not is_s:
                    self.stt(krT.ap[:, n0:n0 + nb], self.kpef.ap[:, n0:n0 + nb], self.sp.ap[0:64, gr:gr + 1], rs.ap[0:64, 0:nb], ALU.mult, ALU.mult, [self.kpef.b, self.sp.b, rs.b], [krT.b])
                elif n0 < 2048:
                    self.tt(krT.ap[:, n0:n0 + nb], krx.ap[0:64, n0:n0 + nb], rs.ap[0:64, 0:nb], ALU.mult, [krx.b, rs.b], [krT.b])
                else:
                    self.tt(krT.ap[:, n0:n0 + nb], kpc.ap[:, :], rs.ap[0:64, 0:nb], ALU.mult, [kpc.b, rs.b], [krT.b])
                pv = self.nps()
                for i in range(nb // 128):
                    st = n0 // 128 + i
                    for kc in range(4):
                        self.mm(pv.ap[:, i * 128:(i + 1) * 128], ckv_src.ap[:, kc, st * 128:(st + 1) * 128], wkv.ap[:, kc, h * 256 + 128:h * 256 + 256], kc == 0, kc == 3, [ckv_src.b, wkv.b], [pv.b])
                self.act(vh.ap[:, n0 // 128:n0 // 128 + nb // 128, :], pv.ap[:, 0:nb].rearrange("p (a v) -> p a v", v=128), AF.Copy, [pv.b], [vh.b])
            qsets = [(0, T, list(range(NK // 128)))] if is_s else [(0, 256, [0, 1]), (256, 256, [2, 3])]
            pO = self.pps()
            pD = self.pps()
            for (q0, qn_, tiles) in qsets:
                for ti, st in enumerate(tiles):
                    psc = self.nps()
                    self.mm(psc.ap[:, 0:qn_], knT.ap[:, st * 128:(st + 1) * 128], qn.ap[:, h, q0:q0 + qn_], True, False, [knT.b, qn.b], [psc.b])
                    self.mm(psc.ap[:, 0:qn_], krT.ap[0:64, st * 128:(st + 1) * 128], qr.ap[0:64, h, q0:q0 + qn_], False, True, [krT.b, qr.b], [psc.b])
                    pt_ = PT[self._sqi % 3]
                    self._sqi += 1
                    self.act(pt_.ap[:, 0:qn_], psc.ap[:, 0:qn_], AF.Exp, [psc.b], [pt_.b], scale=MLA_SCALE)
                    first, last = ti == 0, ti == len(tiles) - 1
                    self.mm(pO.ap[:, q0:q0 + qn_], vh.ap[:, st, :], pt_.ap[:, 0:qn_], first, last, [vh.b, pt_.b], [pO.b])
                    self.mm(pD.ap[:, q0:q0 + qn_], self.onesb.ap[:, :], pt_.ap[:, 0:qn_], first, last, [self.onesb.b, pt_.b], [pD.b])
            self.recip(rd.ap[:, :], pD.ap[:, :], [pD.b], [rd.b])
            self.tt(oT_mla.ap[:, h, :], pO.ap[:, :], rd.ap[:, :], ALU.mult, [pO.b, rd.b], [oT_mla.b])
            self.unpin(pO, pD)
        self.release(m1)

    def stage_swa(self, l, is_s, oT_swa):
        d = self.d
        win = d["w_in"][l]
        hT = self.hT
        m1 = self.mark()
        nqh = 4 if is_s else 16
        qs = self.alloc([64, nqh, T], BF16, "qs")
        rd = self.alloc([64, T], F32, "srd")
        so, _ = SP_OFF["sink%d" % l]
        self.act(self.esink.ap[:, :], self.sp.ap[:, so:so + 16], AF.Exp, [self.sp.b], [self.esink.b])
        if is_s:
            qf = self.alloc([64, T], F32, "sqf")
            rt1 = self.alloc([64, T], F32, "srt1")
            rt2 = self.alloc([64, T], F32, "srt2")
        go, _ = SP_OFF["gsq%d" % l]

        def q_heads(slab, hh_list, dst0):
            for n_, hh in enumerate(hh_list):
                pq = self.nps()
                for c in range(16):
                    self.mm(pq.ap[0:64, :], slab.ap[:, c, hh * 64:(hh + 1) * 64], hT.ap[:, c, :], c == 0, c == 15, [slab.b, hT.bs[c]], [pq.b])
                rs = self.next_rstd()
                self.sumsq_rstd([(pq.ap[0:64, :], pq.b, 64)], 64, rs, self.rtmp, m=64)
                if is_s:
                    self.stt(qf.ap[:, :], pq.ap[0:64, :], self.sp.ap[0:64, go:go + 1], rs.ap[0:64, :], ALU.mult, ALU.mult, [pq.b, self.sp.b, rs.b], [qf.b])
                    self.rope_apply(Tl(qf.ap[:, :], qf.b), qs.ap[:, dst0 + n_, :], qs.b, Tl(rt1.ap[:, :], rt1.b), Tl(rt2.ap[:, :], rt2.b))
                else:
                    self.stt(qs.ap[:, dst0 + n_, :], pq.ap[0:64, :], self.sp.ap[0:64, go:go + 1], rs.ap[0:64, :], ALU.mult, ALU.mult, [pq.b, self.sp.b, rs.b], [qs.b])
        if not is_s:
            for half in range(2):
                slab = self.wslab(self.rows(win, 0, D, O_SQ + half * 512, 512), 16, 512)
                q_heads(slab, list(range(8)), half * 8)
        if not is_s:
            PT = [self.alloc([128, 2, T], BF16, "sPT%d" % k) for k in range(2)]
            k = 0
            for hk in range(4):
                for bb in range(2):
                    for gp in range(2):
                        ps_ = [self.pps(), self.pps()]
                        for st in range(2):
                            kt_ = 2 * bb + st
                            self.mm(ps_[st].ap[:, :].rearrange("p (g t) -> p g t", g=2), self.swkb.ap[0:64, hk, kt_ * 128:(kt_ + 1) * 128],
                                    qs.ap[0:64, hk * 4 + gp * 2:hk * 4 + gp * 2 + 2, bb * 256:(bb + 1) * 256], True, True, [self.swkb.b, qs.b], [ps_[st].b])
                        pt_ = PT[k % 2]
                        k += 1
                        for st in range(2):
                            self.act(pt_.ap[:, st, :], ps_[st].ap[:, :], AF.Exp, [ps_[st].b], [pt_.b], scale=SWA_SCALE)
                        self.unpin(*ps_)
                        pO = self.pps()
                        pD = self.pps()
                        for st in range(2):
                            kt_ = 2 * bb + st
                            self.mm(pO.ap[0:64, :], self.swvb.ap[:, kt_, hk * 64:(hk + 1) * 64], pt_.ap[:, st, :], st == 0, st == 1, [self.swvb.b, pt_.b], [pO.b])
                            self.mm(pD.ap[0:64, :], self.onesb.ap[:, 0:64], pt_.ap[:, st, :], st == 0, st == 1, [self.onesb.b, pt_.b], [pD.b])
                        for g2 in range(2):
                            hd = hk * 4 + gp * 2 + g2
                            self.ts(rd.ap[:, g2 * 256:(g2 + 1) * 256], pD.ap[0:64, g2 * 256:(g2 + 1) * 256], self.esink.ap[0:64, hd:hd + 1], None, ALU.add, None, [pD.b, self.esink.b], [rd.b])
                        self.recip(rd.ap[:, :], rd.ap[:, :], [rd.b], [rd.b])
                        self.tt(oT_swa.ap[:, hk * 4 + gp * 2:hk * 4 + gp * 2 + 2, bb * 256:(bb + 1) * 256], pO.ap[0:64, :].rearrange("p (g t) -> p g t", g=2),
                                rd.ap[:, :].rearrange("p (g t) -> p g t", g=2), ALU.mult, [pO.b, rd.b], [oT_swa.b])
                        self.unpin(pO, pD)
        else:
            xo = self.xoutA[l]
            kall = self.alloc([64, 4, 2304], BF16, "kall")
            vall = self.alloc([128, 18, 256], BF16, "vall")
            msk = [self.alloc([128, 16, 128], BF16, "msk%d" % k) for k in range(2)]
            PT = [self.alloc([128, T], BF16, "sPT%d" % k) for k in range(2)]
            PM = [self.alloc([128, T], BF16, "sPM%d" % k) for k in range(2)]
            slab = self.wslab(self.rows(win, 0, D, O_SQ, 512), 16, 512)
            xr = xo.ap.rearrange("(r p) c -> p r c", p=128)
            for h_ in range(4):
                r0_, c0_ = X_SWK[h_]
                self.load(kall.ap[:, h_, 0:2048].rearrange("p (r t) -> p r t", r=4), xr[r0_:r0_ + 64, :, c0_:c0_ + T], [kall.b], [xo.b], q="pool")
            for a_ in range(4):
                self.load(vall.ap[:, 0:16, :].rearrange("p (r a) n -> p r a n", r=4)[:, :, a_, :], xr[:, :, X_SWV + a_ * 256:X_SWV + (a_ + 1) * 256], [vall.b], [xo.b], q="pool")
            self.load(kall.ap[:, :, 2048:2304], d["swk_cT"][l], [kall.b], q="pool")
            self.load(vall.ap[:, 16:18, :], d["swv_c"][l].rearrange("(a p) n -> p a n", p=128), [vall.b], q="pool")
            mdr = d["swamask"].rearrange("p (j i t) -> p j i t", j=16, i=4)
            own = self.tg == "own"
            k = 0
            mi = 0
            for hk in range(4):
                if hk == 2:
                    slab = self.wslab(self.rows(win, 0, D, O_SQ + 512, 512), 16, 512)
                q_heads(slab, [(hk % 2) * 4 + g for g in range(4)], 0)
                for i in range(4):
                    if own:
                        mk = msk[mi % 2]
                        mi += 1
                        self.load(mk.ap[:, :, :], mdr[:, :, i, :], [mk.b], q="pool")
                        jl = [(j, (mk.ap[:, j, :], mk.b) if j < 16 else None) for j in range(18)]
                    else:
                        Q = 4 * self.tg + i
                        jl = []
                        if Q - 1 >= 0:
                            jl.append((Q - 1, (self.maskLb.ap[:, :], self.maskLb.b)))
                        jl.append((Q, None))
                        if Q + 1 <= 15:
                            jl.append((Q + 1, (self.maskUb.ap[:, :], self.maskUb.b)))
                        jl += [(16, None), (17, None)]
                    pO = self.pps()
                    pD = self.pps()
                    for ji, (j, mk_) in enumerate(jl):
                        psc = self.nps()
                        self.mm(psc.ap[:, :].rearrange("p (g t) -> p g t", g=4), kall.ap[0:64, hk, j * 128:(j + 1) * 128], qs.ap[0:64, 0:4, i * 128:(i + 1) * 128], True, True, [kall.b, qs.b], [psc.b])
                        pt_ = PT[k % 2]
                        self.act(pt_.ap[:, :], psc.ap[:, :], AF.Exp, [psc.b], [pt_.b], scale=SWA_SCALE)
                        if mk_ is not None:
                            pm_ = PM[k % 2]
                            self.tt(pm_.ap[:, :].rearrange("p (g t) -> p g t", g=4), pt_.ap[:, :].rearrange("p (g t) -> p g t", g=4),
                                    mk_[0].unsqueeze(1).broadcast_to([128, 4, 128]), ALU.mult, [pt_.b, mk_[1]], [pm_.b])
                        else:
                            pm_ = pt_
                        k += 1
                        first, last_ = ji == 0, ji == len(jl) - 1
                        self.mm(pO.ap[0:64, :], vall.ap[:, j, hk * 64:(hk + 1) * 64], pm_.ap[:, :], first, last_, [vall.b, pm_.b], [pO.b])
                        self.mm(pD.ap[0:64, :], self.onesb.ap[:, 0:64], pm_.ap[:, :], first, last_, [self.onesb.b, pm_.b], [pD.b])
                    for g in range(4):
                        hd = hk * 4 + g
                        self.ts(rd.ap[:, g * 128:(g + 1) * 128], pD.ap[0:64, g * 128:(g + 1) * 128], self.esink.ap[0:64, hd:hd + 1], None, ALU.add, None, [pD.b, self.esink.b], [rd.b])
                    self.recip(rd.ap[:, :], rd.ap[:, :], [rd.b], [rd.b])
                    self.tt(oT_swa.ap[:, hk * 4:hk * 4 + 4, i * 128:(i + 1) * 128], pO.ap[0:64, :].rearrange("p (g t) -> p g t", g=4),
                            rd.ap[:, :].rearrange("p (g t) -> p g t", g=4), ALU.mult, [pO.b, rd.b], [oT_swa.b])
                    self.unpin(pO, pD)
        self.release(m1)

    def stage_merge(self, l, oT_mla, oT_gla, oT_swa):
        d = self.d
        win = d["w_in"][l]
        hT = self.hT
        xT = self.xT
        m1 = self.mark()
        merged = self.alloc([128, 8, T], F32, "merged", nb=8)
        mergedb = self.alloc([128, 16, T], BF16, "mergedb", nb=16)
        sig = [self.alloc([128, T], F32, "sig%d" % k) for k in range(2)]
        tm = [self.alloc([128, T], F32, "tm%d" % k) for k in range(2)]
        self.push_xslots(64)
        branches = [("w_br_mla", O_GM, oT_mla, 8, 128), ("w_br_gla", O_GG, oT_gla, 8, 128), ("w_br_swa", O_GS, oT_swa, 16, 64)]
        k = 0
        for half in range(2):
            for bi, (wn, og, oT, nk, kp) in enumerate(branches):
                for b2 in range(2):
                    blk = half * 2 + b2
                    sb_ = self.wslab(d[wn][l][:, blk * 512:(blk + 1) * 512].rearrange("(c p) n -> p c n", p=kp), nk, 512)
                    sg_ = self.wslab(self.rows(win, 0, D, og + blk * 512, 512), 16, 512)
                    for jj in range(4):
                        j = blk * 4 + jj
                        jl = b2 * 4 + jj
                        pb = self.nps()
                        pg = self.nps()
                        for kc in range(nk):
                            self.mm(pb.ap[:, :], sb_.ap[0:kp, kc, jj * 128:(jj + 1) * 128], oT.ap[0:kp, kc, :], kc == 0, kc == nk - 1, [sb_.b, oT.b], [pb.b])
                        for c in range(16):
                            self.mm(pg.ap[:, :], sg_.ap[:, c, jj * 128:(jj + 1) * 128], hT.ap[:, c, :], c == 0, c == 15, [sg_.b, hT.bs[c]], [pg.b])
                        s_ = sig[k % 2]
                        t_ = tm[k % 2]
                        k += 1
                        self.act(s_.ap[:, :], pg.ap[:, :], AF.Sigmoid, [pg.b], [s_.b])
                        if bi == 0:
                            self.tt(merged.ap[:, jl, :], s_.ap[:, :], pb.ap[:, :], ALU.mult, [s_.b, pb.b], [merged.bs[jl]])
                        else:
                            self.tt(t_.ap[:, :], s_.ap[:, :], pb.ap[:, :], ALU.mult, [s_.b, pb.b], [t_.b])
                            if bi == 1:
                                self.tt(merged.ap[:, jl, :], merged.ap[:, jl, :], t_.ap[:, :], ALU.add, [merged.bs[jl], t_.b], [merged.bs[jl]])
                            else:
                                self.tt(mergedb.ap[:, j, :], merged.ap[:, jl, :], t_.ap[:, :], ALU.add, [merged.bs[jl], t_.b], [mergedb.bs[j]])
        wo = d["w_out"][l]
        for blk in range(4):
            sw = self.wslab(self.rows(wo, 0, D, blk * 512, 512), 16, 512)
            for dj in range(4):
                ch = blk * 4 + dj
                po = self.nps()
                for j in range(16):
                    self.mm(po.ap[:, :], sw.ap[:, j, dj * 128:(dj + 1) * 128], mergedb.ap[:, j, :], j == 0, j == 15, [sw.b, mergedb.bs[j]], [po.b])
                self.stt(xT.ap[:, ch, :], po.ap[:, :], self.modG.ap[:, ch:ch + 1], xT.ap[:, ch, :], ALU.mult, ALU.add, [po.b, self.modG.b, xT.bs[ch]], [xT.bs[ch]])
        self.pop_xslots()
        self.release(m1)

    def load_x(self, src, c0=0, reads=()):
        xv = src.rearrange("(c p) t -> p c t", p=128)
        for c8 in range(2):
            self.load(self.xT.ap[:, c8 * 8:(c8 + 1) * 8, :], xv[:, c8 * 8:(c8 + 1) * 8, c0:c0 + T], self.xT.bs[c8 * 8:(c8 + 1) * 8], reads)

    def store_x(self, dst, c0=0, writes=()):
        dv = dst.rearrange("(c p) t -> p c t", p=128)
        for c8 in range(2):
            self.store(dv[:, c8 * 8:(c8 + 1) * 8, c0:c0 + T], self.xT.ap[:, c8 * 8:(c8 + 1) * 8, :], self.xT.bs[c8 * 8:(c8 + 1) * 8], writes)

    def load_xq(self, src, c0, dq, reads=()):
        xv = src.rearrange("(c p) t -> p c t", p=128)
        self.load(self.xT.ap[:, dq * 4:(dq + 1) * 4, :], xv[:, dq * 4:(dq + 1) * 4, c0:c0 + T], self.xT.bs[dq * 4:(dq + 1) * 4], reads)

    def store_xq(self, dst, c0, dq, writes=()):
        dv = dst.rearrange("(c p) t -> p c t", p=128)
        self.store(dv[:, dq * 4:(dq + 1) * 4, c0:c0 + T], self.xT.ap[:, dq * 4:(dq + 1) * 4, :], self.xT.bs[dq * 4:(dq + 1) * 4], writes)

    def run(self, stop=None):
        d = self.d
        self.consts()
        self.adaln()
        if self.do_p:
            self.load_x(d["xpT"])
            for l in range(self.n_layers):
                self.ffn(l, 1, 0)
                if stop == "ffn1":
                    break
                self.mixer(l, 0, False)
                if stop == "mixer":
                    break
                self.ffn(l, 2, 0)
            self.store_x(d["ypT"])
        if self.do_s:
            xs = self.xscr
            oo, _ = SP_OFF["osel"]
            for l in range(self.n_layers):
                last = (l == self.n_layers - 1)
                if last:
                    macc = self.mark()
                    acc = self.alloc([128, 16, T], F32, "xown", nb=16)
                def fetch(tg_):
                    if l == 0:
                        self.load_x(d["xsT"], tg_ * T)
                    else:
                        self.load_x(xs.ap, tg_ * T, [xs.bs[tg_]])
                for tg in range(4):
                    self.load(self.rope.ap[:, :, :], d["rope"][tg], [self.rope.b])
                    if tg == 0 or stop == "ffn1":
                        fetch(tg)
                    self.ffn(l, 1, 1)
                    if last:
                        sel = self.sp.ap[:, oo + tg:oo + tg + 1]
                        for c in range(16):
                            if tg == 0:
                                self.ts(acc.ap[:, c, :], self.xT.ap[:, c, :], sel, None, ALU.mult, None, [self.xT.bs[c], self.sp.b], [acc.bs[c]])
                            else:
                                self.stt(acc.ap[:, c, :], self.xT.ap[:, c, :], sel, acc.ap[:, c, :], ALU.mult, ALU.add, [self.xT.bs[c], self.sp.b, acc.bs[c]], [acc.bs[c]])
                    else:
                        self.store_x(xs.ap, tg * T, [xs.bs[tg]])
                    if stop == "ffn1":
                        continue
                    self.mixer(l, 1, True, 1, tg, after_norm=(lambda t_=tg: fetch(t_ + 1)) if tg < 3 else None)
                if not last:
                    for tg in range(4):
                        self.load(self.rope.ap[:, :, :], d["rope"][tg], [self.rope.b])
                        if tg == 0:
                            self.load_x(xs.ap, tg * T, [xs.bs[tg]])
                        self.mixer(l, 1, True, 2, tg)

                        def handoff(dq, t_=tg):
                            self.store_xq(xs.ap, t_ * T, dq, [xs.bs[t_]])
                            if t_ < 3:
                                self.load_xq(xs.ap, (t_ + 1) * T, dq, [xs.bs[t_ + 1]])
                        self.ffn(l, 2, 1, after_dq=handoff)
                else:
                    for c in range(16):
                        self.vcopy(self.xT.ap[:, c, :], acc.ap[:, c, :], [acc.bs[c]], [self.xT.bs[c]])
                    self.release(macc)
                    self.load(self.rope.ap[:, :, :], d["rope"][4], [self.rope.b])
                    if stop != "ffn1":
                        self.mixer(l, 1, True, 2, "own")
                        if stop != "mixer":
                            self.ffn(l, 2, 1)
                    self.store_x(d["ysT"])
        self.P.wait_all_dma("sp")
        return self.P.emit()


def _rope_tables(pos):
    half = 16
    inv = (10000.0 ** (-np.arange(half, dtype=np.float32) / half)).astype(np.float32)
    tab = np.zeros((64, 2, len(pos)), np.float32)
    for dd in range(64):
        comp = (pos // 64) if dd < 32 else (pos % 64)
        ang = comp.astype(np.float32) * inv[dd % 16]
        tab[dd, 0] = np.cos(ang).astype(np.float32)
        tab[dd, 1] = np.sin(ang).astype(np.float32)
    return tab


def _perm_T():
    pm = np.zeros((64, 64), np.float32)
    for dd in range(64):
        if dd % 32 < 16:
            pm[dd, dd + 16] = -1.0
        else:
            pm[dd, dd - 16] = 1.0
    return np.ascontiguousarray(pm.T)


def _swamask(q):
    m = np.zeros((128, 16, 4, 128), np.float32)
    s = np.arange(128)[:, None]
    t = np.arange(128)[None, :]
    for i in range(4):
        for j in range(16):
            dlt = j - (4 * q + i)
            if dlt == 0:
                m[:, j, i, :] = 1.0
            elif dlt == -1:
                m[:, j, i, :] = (t <= s)
            elif dlt == 1:
                m[:, j, i, :] = (s <= t)
    return m.reshape(128, -1)


def prep_core(inp, core):
    b, q = core // 4, core % 4
    f = lambda a: np.ascontiguousarray(np.asarray(a, dtype=np.float32))
    m = {}
    m["xpT"] = f(np.asarray(inp["x_prompt"])[2 * core:2 * core + 2].reshape(T, D).T)
    m["xsT"] = f(np.asarray(inp["x_sample"])[b].T)
    sp = np.zeros((128, SP_N), np.float32)

    def put(name, arr):
        o, w = SP_OFF[name]
        arr = np.asarray(arr, np.float32)
        assert arr.shape[1] == w, (name, arr.shape, w)
        sp[0:arr.shape[0], o:o + w] = arr
    put("ident", np.eye(128, dtype=np.float32))
    put("permT", _perm_T())
    s_ = np.arange(128)[:, None]
    t_ = np.arange(128)[None, :]
    put("maskU", (s_ <= t_).astype(np.float32))
    put("maskL", (s_ >= t_).astype(np.float32))
    cond = np.stack([np.asarray(inp["c_ctx"]), np.asarray(inp["c"])[b]], axis=1)
    put("cond", cond.reshape(16, 128, 2).transpose(1, 0, 2).reshape(128, 32))
    gs = np.zeros((128, 4, 8), np.float32)
    for tg_ in range(4):
        gs[:, tg_, tg_] = 1.0
        gs[:, tg_, 4 + tg_] = 1.0
    put("gsel", gs.reshape(128, 32))
    put("gsel_own", gs[:, q, :])
    os_ = np.zeros((128, 4), np.float32)
    os_[:, q] = 1.0
    put("osel", os_)
    col = lambda v: np.asarray(v, np.float32).reshape(-1, 1)
    for l in range(DEPTH):
        put("bada%d" % l, np.asarray(inp["b_ada"])[l].reshape(144, 128).T)
        put("gn%d" % l, np.concatenate([np.asarray(inp[k])[l].reshape(16, 128).T for k in ("g_norm1", "g_norm2", "g_norm3")], axis=1))
        put("gq%d" % l, np.asarray(inp["g_mla_q"])[l].reshape(4, 128).T)
        put("gkv%d" % l, np.asarray(inp["g_mla_kv"])[l].reshape(4, 128).T)
        put("gqn_n%d" % l, col(np.asarray(inp["g_mla_qn"])[l][:128]))
        put("gqn_r%d" % l, col(np.asarray(inp["g_mla_qn"])[l][128:]))
        put("gkn_n%d" % l, col(np.asarray(inp["g_mla_kn"])[l][:128]))
        put("gkn_r%d" % l, col(np.asarray(inp["g_mla_kn"])[l][128:]))
        put("ggo%d" % l, np.asarray(inp["g_gla_out"])[l].reshape(2, 128).T)
        put("gsq%d" % l, col(np.asarray(inp["g_swa_qn"])[l]))
        put("gsk%d" % l, col(np.asarray(inp["g_swa_kn"])[l]))
        put("sink%d" % l, np.broadcast_to(np.asarray(inp["swa_sink"])[l][None, :], (128, 16)))
    m["smallp"] = sp
    m["rope"] = np.stack([_rope_tables(q_ * T + np.arange(T)) for q_ in (0, 1, 2, 3, q)])
    m["wg"] = f(np.stack([np.asarray(inp["w_gla_gf"]), np.asarray(inp["w_gla_gb"])], axis=1).transpose(2, 0, 1, 3))
    m["bg"] = f(np.stack([np.asarray(inp["b_gla_gf"]), np.asarray(inp["b_gla_gb"])], axis=1)[None])
    m["swamask"] = _swamask(q)
    for l in range(DEPTH):
        m["w_ada%d" % l] = np.asarray(inp["w_ada"], dtype=np.float32)[l]
    for k in ("w_ff1_gu", "w_ff2_gu", "w_ff1_down", "w_ff2_down", "w_in", "w_mla_uq", "w_mla_ukv",
              "w_br_mla", "w_br_gla", "w_br_swa", "w_out"):
        m[k] = np.asarray(inp[k], dtype=np.float32)
    m["ckv_cT"] = f(np.asarray(inp["cache_mla_ckv"])[b].transpose(0, 2, 1))
    m["kpe_cT"] = f(np.asarray(inp["cache_mla_kpe"])[b].transpose(0, 2, 1))
    m["swk_cT"] = f(np.asarray(inp["cache_swa_k"])[b].transpose(0, 3, 2, 1))
    m["swv_c"] = f(np.asarray(inp["cache_swa_v"])[b].reshape(DEPTH, 256, 256))
    m["state"] = f(np.asarray(inp["state_gla"])[b])
    return m


_CACHE = {}


def _get_builder():
    if "b" not in _CACHE:
        b = Builder(do_s=True, do_p=True, n_layers=DEPTH, n_cores=8)
        b.run()
        _CACHE["b"] = b
    return _CACHE["b"]


def kernel(**inputs):
    b = _get_builder()
    shared = {}
    in_maps = []
    for core in range(8):
        m = prep_core(inputs, core)
        for k in list(m.keys()):
            if k.startswith("w_") and k != "wg":
                if k not in shared:
                    shared[k] = m[k]
                m[k] = shared[k]
        in_maps.append(m)
    res = run_bass_kernel_spmd(b.nc, in_maps, core_ids=list(range(8)))
    r = res.results
    B, S = 16, 256
    y_p = np.zeros((B, S, D), np.float32)
    y_s = np.zeros((2, 2048, D), np.float32)
    n_ckv = np.zeros((B, DEPTH, S, 512), np.float32)
    n_kpe = np.zeros((B, DEPTH, S, 64), np.float32)
    n_sk = np.zeros((B, DEPTH, S, 4, 64), np.float32)
    n_sv = np.zeros((B, DEPTH, S, 4, 64), np.float32)
    n_sg = np.zeros((B, DEPTH, 2, 4, 128, 256), np.float32)
    for core in range(8):
        o = r[core]
        bq, q = core // 4, core % 4
        y_p[2 * core:2 * core + 2] = np.asarray(o["ypT"]).T.reshape(2, S, D)
        y_s[bq, q * T:(q + 1) * T] = np.asarray(o["ysT"]).T
        for l in range(DEPTH):
            n_ckv[2 * core:2 * core + 2, l] = np.asarray(o["n_ckvT"])[l].T.reshape(2, S, 512)
            n_kpe[2 * core:2 * core + 2, l] = np.asarray(o["n_kpeT"])[l].T.reshape(2, S, 64)
            n_sk[2 * core:2 * core + 2, l] = np.asarray(o["n_skT"])[l].transpose(2, 0, 1).reshape(2, S, 4, 64)
            n_sv[2 * core:2 * core + 2, l] = np.asarray(o["n_sv"])[l].reshape(2, S, 4, 64)
            n_sg[2 * core:2 * core + 2, l] = np.asarray(o["n_sg"])[l]
    return (y_p, y_s, n_ckv, n_kpe, n_sk, n_sv, n_sg)
```
